# Optimizing a Trainium2 kernel written in Bass

```python
import math
import jax
import jax.numpy as jnp
from jax import lax
import numpy as np

D_MODEL = 1024
BATCH = 4
SEQ = 8192
DEPTH = 2
DEC_BATCH = 32
DEC_SEQ = 4
PAST_LEN = 16384
PAGE_SIZE = 128

N_AB_LAYERS = (DEPTH + 1) // 2
N_C_LAYERS = DEPTH // 2
MIX_WIDTH = D_MODEL
POOL_WIDTH = MIX_WIDTH // 2
POOL_WINDOWS = (2, 4, 8, 16)
POOL_GROUP = POOL_WIDTH // len(POOL_WINDOWS)
POOL_STATE = max(POOL_WINDOWS) - 1
SSM_WIDTH = MIX_WIDTH - POOL_WIDTH
SSM_GROUP = 16
SSM_GROUPS = SSM_WIDTH // SSM_GROUP
SSM_STATE = 64
SSM_CHUNK = 128
ATT_HEADS = 16
HEAD_DIM = 64
ATT_WIDTH = ATT_HEADS * HEAD_DIM
DILATED_PATTERNS = ((128, 1), (512, 4), (2048, 16))
MAX_WINDOW = max(w for w, _ in DILATED_PATTERNS)
ATT_BLOCK = 128
FFN_HIDDEN = -(-(8 * D_MODEL) // (3 * 256)) * 256
RMS_EPS = 1e-6

kernel_name = 'hybrid_pool_s5_dilated_attn_step'


def rms_norm(x, g):
    xf = x.astype(jnp.float32)
    y = xf * lax.rsqrt(jnp.mean(xf * xf, axis=-1, keepdims=True) + RMS_EPS)
    return (y * g.astype(jnp.float32)).astype(x.dtype)


def swiglu_ffn(x, w_gate, w_up, w_down):
    return (jax.nn.silu(x @ w_gate) * (x @ w_up)) @ w_down


def pool_mixer(u, prev, pos, w_grp, scale):
    b, t, c = u.shape
    ext = jnp.concatenate([prev.astype(u.dtype), u], axis=1)
    cs = jnp.pad(jnp.cumsum(ext.astype(jnp.float32), axis=1), ((0, 0), (1, 0), (0, 0)))
    uf = u.astype(jnp.float32)
    lo = POOL_STATE + 1
    groups = []
    for g, w in enumerate(POOL_WINDOWS):
        sl = slice(g * POOL_GROUP, (g + 1) * POOL_GROUP)
        win_sum = cs[:, lo:lo + t, sl] - cs[:, lo - w:lo - w + t, sl]
        count = jnp.minimum(pos + 1, w).astype(jnp.float32)[None, :, None]
        groups.append(win_sum / count - uf[..., sl])
    d = jnp.stack(groups, axis=2)
    y = jnp.einsum('btgc,gce->btge', d, w_grp.astype(jnp.float32)).reshape(b, t, c)
    return y * scale.astype(jnp.float32), ext[:, -POOL_STATE:]


def _linear_recurrence_combine(e1, e2):
    a1, b1 = e1
    a2, b2 = e2
    return a1 * a2, a2 * b1 + b2


def s5_mixer(u, h0, lam_re, lam_im, log_dt, b_re, b_im, c_re, c_im, d_skip):
    b, t, c = u.shape
    f32 = jnp.float32
    lam = lax.complex(lam_re.astype(f32), lam_im.astype(f32))
    dt = jnp.exp(log_dt.astype(f32))[:, None]
    lam_bar = jnp.exp(lam * dt)
    b_mat = lax.complex(b_re.astype(f32), b_im.astype(f32))
    b_bar = ((lam_bar - 1.0) / lam)[..., None] * b_mat
    c_mat = lax.complex(c_re.astype(f32), c_im.astype(f32))
    uf = u.astype(f32)
    chunk = math.gcd(t, SSM_CHUNK)
    u_chunks = jnp.moveaxis(uf.reshape(b, t // chunk, chunk, SSM_GROUPS, SSM_GROUP), 1, 0)

    def step(h, u_c):
        bu = jnp.einsum('gpi,bcgi->bcgp', b_bar, u_c.astype(jnp.complex64))
        bu = bu.at[:, 0].add(lam_bar * h)
        a = jnp.broadcast_to(lam_bar, bu.shape)
        _, hs = lax.associative_scan(_linear_recurrence_combine, (a, bu), axis=1)
        y_c = jnp.einsum('gip,bcgp->bcgi', c_mat, hs).real
        return hs[:, -1], y_c

    h_last, y_chunks = lax.scan(step, h0, u_chunks)
    y = jnp.moveaxis(y_chunks, 0, 1).reshape(b, t, c) + d_skip.astype(f32) * uf
    return y, h_last


def mixer_ab(xn, pool_prev, h0, pos, w_in, pool_w, pool_scale, lam_re, lam_im, log_dt,
             b_re, b_im, c_re, c_im, d_skip, w_glu, b_glu, w_out):
    proj = xn @ w_in
    u_pool, u_ssm = proj[..., :POOL_WIDTH], proj[..., POOL_WIDTH:]
    y_pool, pool_state = pool_mixer(u_pool, pool_prev, pos, pool_w, pool_scale)
    y_ssm, h_last = s5_mixer(u_ssm, h0, lam_re, lam_im, log_dt, b_re, b_im, c_re, c_im, d_skip)
    z = jax.nn.gelu(y_ssm)
    y_ssm = z * jax.nn.sigmoid(z @ w_glu.astype(jnp.float32) + b_glu.astype(jnp.float32))
    y = jnp.concatenate([y_pool, y_ssm], axis=-1).astype(xn.dtype) @ w_out
    return y, pool_state, h_last


def _softmax_stats(sc, valid):
    sc = jnp.where(valid, sc, -jnp.inf)
    m = jnp.max(sc, axis=-1, keepdims=True)
    p = jnp.exp(sc - m)
    l = jnp.sum(p, axis=-1)
    return p, l, m[..., 0] + jnp.log(l)


def _merge_patterns(outs, lses):
    w = jax.nn.softmax(jnp.stack(lses), axis=0)
    return jnp.sum(w[..., None] * jnp.stack(outs), axis=0)


def _to_strided(a, dil):
    b, s = a.shape[:2]
    a = a.reshape((b, s // dil, dil) + a.shape[2:])
    a = jnp.moveaxis(a, 2, 1)
    return a.reshape((b * dil, s // dil) + a.shape[3:])


def _from_strided(a, b, dil):
    n, l = a.shape[:2]
    a = a.reshape((b, dil, l) + a.shape[2:])
    a = jnp.moveaxis(a, 1, 2)
    return a.reshape((b, l * dil) + a.shape[3:])


def _split_qkv(xn, w_qkv):
    b, t, _ = xn.shape
    qkv = (xn @ w_qkv).reshape(b, t, 3, ATT_HEADS, HEAD_DIM)
    return qkv[:, :, 0], qkv[:, :, 1], qkv[:, :, 2]


def dilated_branch_prompt(q, k, v, window, dil):
    b, s, h, e = q.shape
    span = window // dil
    qs, ks, vs = _to_strided(q, dil), _to_strided(k, dil), _to_strided(v, dil)
    l_sub = qs.shape[1]
    nb = -(-l_sub // ATT_BLOCK)
    extra = nb * ATT_BLOCK - l_sub
    qs = jnp.pad(qs, ((0, 0), (0, extra), (0, 0), (0, 0)))
    ks = jnp.pad(ks, ((0, 0), (ATT_BLOCK, extra), (0, 0), (0, 0)))
    vs = jnp.pad(vs, ((0, 0), (ATT_BLOCK, extra), (0, 0), (0, 0)))
    qi = jnp.arange(ATT_BLOCK)[:, None]
    kj = jnp.arange(2 * ATT_BLOCK)[None, :]
    dist = ATT_BLOCK + qi - kj
    band = (dist >= 0) & (dist <= span)
    scale = HEAD_DIM ** -0.5

    def one_block(blk):
        start = blk * ATT_BLOCK
        qb = lax.dynamic_slice_in_dim(qs, start, ATT_BLOCK, axis=1)
        kb = lax.dynamic_slice_in_dim(ks, start, 2 * ATT_BLOCK, axis=1)
        vb = lax.dynamic_slice_in_dim(vs, start, 2 * ATT_BLOCK, axis=1)
        valid = band & (start - ATT_BLOCK + kj >= 0)
        sc = jnp.einsum('nqhe,nkhe->nhqk', qb, kb, preferred_element_type=jnp.float32) * scale
        p, den, lse = _softmax_stats(sc, valid)
        o = jnp.einsum('nhqk,nkhe->nqhe', p, vb.astype(jnp.float32)) / jnp.swapaxes(den, 1, 2)[..., None]
        return o, jnp.swapaxes(lse, 1, 2)

    o_blk, lse_blk = lax.map(one_block, jnp.arange(nb))
    n = qs.shape[0]
    o = jnp.moveaxis(o_blk, 0, 1).reshape(n, nb * ATT_BLOCK, h, e)[:, :l_sub]
    lse = jnp.moveaxis(lse_blk, 0, 1).reshape(n, nb * ATT_BLOCK, h)[:, :l_sub]
    return _from_strided(o, b, dil), _from_strided(lse, b, dil)


def mixer_c_prompt(xn, w_qkv, w_o):
    q, k, v = _split_qkv(xn, w_qkv)
    outs, lses = [], []
    for window, dil in DILATED_PATTERNS:
        o, lse = dilated_branch_prompt(q, k, v, window, dil)
        outs.append(o)
        lses.append(lse)
    b, t = xn.shape[:2]
    y = _merge_patterns(outs, lses).reshape(b, t, ATT_WIDTH).astype(xn.dtype) @ w_o
    keep = min(MAX_WINDOW, t)
    return y, k[:, -keep:], v[:, -keep:]


def mixer_c_sample(xn, cache_k, cache_v, w_qkv, w_o):
    q, k, v = _split_qkv(xn, w_qkv)
    b, t = xn.shape[:2]
    buf = cache_k.shape[1]
    kk = jnp.concatenate([cache_k, k.astype(cache_k.dtype)], axis=1)
    vv = jnp.concatenate([cache_v, v.astype(cache_v.dtype)], axis=1)
    scale = HEAD_DIM ** -0.5
    outs, lses = [], []
    for window, dil in DILATED_PATTERNS:
        span = window // dil
        idx = buf + jnp.arange(t)[:, None] - dil * jnp.arange(span + 1)[None, :]
        valid = idx >= 0
        idx = jnp.maximum(idx, 0)
        kg, vg = kk[:, idx], vv[:, idx]
        sc = jnp.einsum('bthe,btjhe->bhtj', q, kg, preferred_element_type=jnp.float32) * scale
        p, den, lse = _softmax_stats(sc, valid[None, None])
        o = jnp.einsum('bhtj,btjhe->bthe', p, vg.astype(jnp.float32)) / jnp.swapaxes(den, 1, 2)[..., None]
        outs.append(o)
        lses.append(jnp.swapaxes(lse, 1, 2))
    y = _merge_patterns(outs, lses).reshape(b, t, ATT_WIDTH).astype(xn.dtype) @ w_o
    return y, k, v


def setup_inputs(seed: int = 0) -> dict:
    key = jax.random.key(seed)
    ks = jax.random.split(key, 26)
    f32 = jnp.float32

    def nrm(k, shape, scale):
        return jax.random.normal(k, shape, f32) * scale

    buf = min(MAX_WINDOW, PAST_LEN)
    return {
        'x_prompt': nrm(ks[0], (BATCH, SEQ, D_MODEL), 1.0),
        'x_sample': nrm(ks[1], (DEC_BATCH, DEC_SEQ, D_MODEL), 1.0),
        'state_pool': nrm(ks[2], (N_AB_LAYERS, DEC_BATCH, POOL_STATE, POOL_WIDTH), 1.0),
        'state_s5': nrm(ks[3], (N_AB_LAYERS, DEC_BATCH, SSM_GROUPS, SSM_STATE, 2), 0.1),
        'cache_k': nrm(ks[4], (N_C_LAYERS, DEC_BATCH, buf, ATT_HEADS, HEAD_DIM), 1.0),
        'cache_v': nrm(ks[5], (N_C_LAYERS, DEC_BATCH, buf, ATT_HEADS, HEAD_DIM), 1.0),
        'norm_gains': 1.0 + nrm(ks[6], (DEPTH, 4, D_MODEL), 0.1),
        'ab_w_in': nrm(ks[7], (N_AB_LAYERS, D_MODEL, MIX_WIDTH), D_MODEL ** -0.5),
        'ab_pool_w': nrm(ks[8], (N_AB_LAYERS, len(POOL_WINDOWS), POOL_GROUP, POOL_GROUP), POOL_GROUP ** -0.5),
        'ab_pool_scale': 1.0 + nrm(ks[9], (N_AB_LAYERS, POOL_WIDTH), 0.1),
        'ab_lambda_re': -0.5 + nrm(ks[10], (N_AB_LAYERS, SSM_GROUPS, SSM_STATE), 0.01),
        'ab_lambda_im': math.pi * jnp.arange(SSM_STATE, dtype=f32) + nrm(ks[11], (N_AB_LAYERS, SSM_GROUPS, SSM_STATE), 0.01),
        'ab_log_dt': jax.random.uniform(ks[12], (N_AB_LAYERS, SSM_GROUPS), f32, math.log(1e-3), math.log(1e-1)),
        'ab_b_re': nrm(ks[13], (N_AB_LAYERS, SSM_GROUPS, SSM_STATE, SSM_GROUP), (2 * SSM_GROUP) ** -0.5),
        'ab_b_im': nrm(ks[14], (N_AB_LAYERS, SSM_GROUPS, SSM_STATE, SSM_GROUP), (2 * SSM_GROUP) ** -0.5),
        'ab_c_re': nrm(ks[15], (N_AB_LAYERS, SSM_GROUPS, SSM_GROUP, SSM_STATE), SSM_STATE ** -0.5),
        'ab_c_im': nrm(ks[16], (N_AB_LAYERS, SSM_GROUPS, SSM_GROUP, SSM_STATE), SSM_STATE ** -0.5),
        'ab_d': nrm(ks[17], (N_AB_LAYERS, SSM_WIDTH), 1.0),
        'ab_w_glu': nrm(ks[18], (N_AB_LAYERS, SSM_WIDTH, SSM_WIDTH), SSM_WIDTH ** -0.5),
        'ab_b_glu': nrm(ks[19], (N_AB_LAYERS, SSM_WIDTH), 0.01),
        'ab_w_out': nrm(ks[20], (N_AB_LAYERS, MIX_WIDTH, D_MODEL), MIX_WIDTH ** -0.5),
        'c_w_qkv': nrm(ks[21], (N_C_LAYERS, D_MODEL, 3 * ATT_WIDTH), D_MODEL ** -0.5),
        'c_w_o': nrm(ks[22], (N_C_LAYERS, ATT_WIDTH, D_MODEL), ATT_WIDTH ** -0.5),
        'ffn_w_gate': nrm(ks[23], (DEPTH, D_MODEL, FFN_HIDDEN), D_MODEL ** -0.5),
        'ffn_w_up': nrm(ks[24], (DEPTH, D_MODEL, FFN_HIDDEN), D_MODEL ** -0.5),
        'ffn_w_down': nrm(ks[25], (DEPTH, FFN_HIDDEN, D_MODEL), FFN_HIDDEN ** -0.5),
    }


def reference(x_prompt, x_sample, state_pool, state_s5, cache_k, cache_v, norm_gains,
              ab_w_in, ab_pool_w, ab_pool_scale, ab_lambda_re, ab_lambda_im, ab_log_dt,
              ab_b_re, ab_b_im, ab_c_re, ab_c_im, ab_d, ab_w_glu, ab_b_glu, ab_w_out,
              c_w_qkv, c_w_o, ffn_w_gate, ffn_w_up, ffn_w_down):
    hp, hs = x_prompt, x_sample
    pos_p = jnp.arange(hp.shape[1])
    pos_s = PAST_LEN + jnp.arange(hs.shape[1])
    pool_p_list, s5_p_list, k_p_list, v_p_list = [], [], [], []
    pool_s_list, s5_s_list, k_s_list, v_s_list = [], [], [], []
    for layer in range(DEPTH):
        i = layer // 2
        g = norm_gains[layer]
        xp, xs = rms_norm(hp, g[0]), rms_norm(hs, g[0])
        if layer % 2 == 0:
            ab = (ab_w_in[i], ab_pool_w[i], ab_pool_scale[i], ab_lambda_re[i], ab_lambda_im[i],
                  ab_log_dt[i], ab_b_re[i], ab_b_im[i], ab_c_re[i], ab_c_im[i], ab_d[i],
                  ab_w_glu[i], ab_b_glu[i], ab_w_out[i])
            pool0 = jnp.zeros((hp.shape[0], POOL_STATE, POOL_WIDTH), hp.dtype)
            h0_p = jnp.zeros((hp.shape[0], SSM_GROUPS, SSM_STATE), jnp.complex64)
            h0_s = lax.complex(state_s5[i, ..., 0].astype(jnp.float32), state_s5[i, ..., 1].astype(jnp.float32))
            y_p, pool_p, h_p = mixer_ab(xp, pool0, h0_p, pos_p, *ab)
            y_s, pool_s, h_s = mixer_ab(xs, state_pool[i], h0_s, pos_s, *ab)
            pool_p_list.append(pool_p)
            pool_s_list.append(pool_s.astype(state_pool.dtype))
            s5_p_list.append(jnp.stack([h_p.real, h_p.imag], axis=-1).astype(state_s5.dtype))
            s5_s_list.append(jnp.stack([h_s.real, h_s.imag], axis=-1).astype(state_s5.dtype))
        else:
            y_p, k_p, v_p = mixer_c_prompt(xp, c_w_qkv[i], c_w_o[i])
            y_s, k_s, v_s = mixer_c_sample(xs, cache_k[i], cache_v[i], c_w_qkv[i], c_w_o[i])
            k_p_list.append(k_p)
            v_p_list.append(v_p)
            k_s_list.append(k_s.astype(cache_k.dtype))
            v_s_list.append(v_s.astype(cache_v.dtype))
        hp = hp + rms_norm(y_p, g[1])
        hs = hs + rms_norm(y_s, g[1])
        hp = hp + rms_norm(swiglu_ffn(rms_norm(hp, g[2]), ffn_w_gate[layer], ffn_w_up[layer], ffn_w_down[layer]), g[3])
        hs = hs + rms_norm(swiglu_ffn(rms_norm(hs, g[2]), ffn_w_gate[layer], ffn_w_up[layer], ffn_w_down[layer]), g[3])
    pool_prompt = jnp.stack(pool_p_list)
    s5_prompt = jnp.stack(s5_p_list)
    k_prompt = jnp.stack(k_p_list)
    v_prompt = jnp.stack(v_p_list)
    pool_sample = jnp.stack(pool_s_list)
    s5_sample = jnp.stack(s5_s_list)
    k_sample = jnp.stack(k_s_list)
    v_sample = jnp.stack(v_s_list)
    return (hp, hs, pool_prompt, s5_prompt, k_prompt, v_prompt, pool_sample, s5_sample, k_sample, v_sample)
```

```python
import os
import numpy as np
import concourse.bass as bass
import concourse.mybir as mybir
from concourse.bass_utils import run_bass_kernel_spmd

F32 = mybir.dt.float32
BF16 = mybir.dt.bfloat16
AF = mybir.ActivationFunctionType
ALU = mybir.AluOpType
AX = mybir.AxisListType

D = 1024
FH = 2816
NFC = FH // 128
EPS = 1e-6
SEM_CH = 20000
K_STAGE = float(os.environ.get('K_STAGE', '99'))


class _Op:
    __slots__ = ("eng", "fn", "dma_tok", "waits", "signal", "seq", "dbg")


class Sched:
    ENGS = ("pe", "act", "dve", "pool", "sp")

    def __init__(self, nc, sync_same=True):
        self.nc = nc
        self.streams = {e: [] for e in self.ENGS}
        self.lastw = {}
        self.readers = {}
        self.slot_cum = []
        self.tok_slot = {}
        self.phase_used = 0
        self.sync_same = sync_same
        self.pending = {}

    def op(self, eng, fn, reads=(), writes=(), dma_tok=None):
        o = _Op()
        if dma_tok is not None:
            if dma_tok not in self.tok_slot:
                if self.phase_used >= len(self.slot_cum):
                    self.slot_cum.append(0)
                self.tok_slot[dma_tok] = self.phase_used
                self.phase_used += 1
            dma_tok = self.tok_slot[dma_tok]
        writes = list(writes) + [r for r in reads if r.startswith("ps") and r not in writes]
        o.dbg = (tuple(reads), tuple(writes))
        o.eng, o.fn, o.dma_tok, o.waits, o.signal, o.seq = eng, fn, dma_tok, self.pending.pop(eng, []), False, 0
        deps = []
        seen = set()

        def add(d):
            if d is not None and id(d) not in seen:
                seen.add(id(d))
                deps.append(d)

        for r in reads:
            add(self.lastw.get(r))
        for w in writes:
            add(self.lastw.get(w))
            for rd in self.readers.get(w, ()):
                add(rd)
        for d in deps:
            if d.dma_tok is not None:
                o.waits.append(("dma", d.dma_tok, self.slot_cum[d.dma_tok]))
            else:
                if d.eng == eng and (eng == "pe" or not self.sync_same):
                    continue
                d.signal = True
                o.waits.append(("eng", d))
        if dma_tok is not None:
            self.slot_cum[dma_tok] += 1
        for r in reads:
            self.readers.setdefault(r, []).append(o)
        for w in writes:
            self.lastw[w] = o
            self.readers[w] = []
        self.streams[eng].append(o)
        return o

    def barrier(self):
        lasts = {e: (self.streams[e][-1] if self.streams[e] else None) for e in self.ENGS}
        toks = dict(enumerate(self.slot_cum))
        self.tok_slot = {}
        self.phase_used = 0
        for e in self.ENGS:
            pend = self.pending.setdefault(e, [])
            for e2, d in lasts.items():
                if d is None or e2 == e:
                    continue
                if d.dma_tok is None:
                    d.signal = True
                    pend.append(("eng", d))
            for t, c in toks.items():
                pend.append(("dma", t, c))
        self.lastw = {}
        self.readers = {}

    def dump(self, path):
        seqs = {}
        for eng, ops in self.streams.items():
            c = 0
            for o in ops:
                if o.dma_tok is None and o.signal:
                    c += 1
                    seqs[id(o)] = c
        with open(path, "w") as f:
            for eng, ops in self.streams.items():
                f.write(f"==== {eng}\n")
                for i, o in enumerate(ops):
                    ws = []
                    for w in o.waits:
                        if w[0] == "dma":
                            ws.append(f"D[{w[1]}]>={w[2]}")
                        else:
                            ws.append(f"{w[1].eng}>={seqs.get(id(w[1]))}")
                    f.write(f"{i}: sig={seqs.get(id(o))} dma={o.dma_tok} r/w={getattr(o, 'dbg', None)} waits={ws}\n")

    def finish(self):
        o = _Op()
        o.eng, o.fn, o.dma_tok, o.waits, o.signal, o.seq = "sp", (lambda en: en.nop()), None, [], False, 0
        for t, c in enumerate(self.slot_cum):
            o.waits.append(("dma", t, c))
        self.streams["sp"].append(o)

    def emit(self):
        nc = self.nc
        nsem = {}
        for eng, ops in self.streams.items():
            c = 0
            for o in ops:
                if o.dma_tok is None and o.signal:
                    c += 1
                    o.seq = c
            nsem[eng] = (c + SEM_CH - 1) // SEM_CH
        eng_sem = {eng: [nc.alloc_semaphore(f"s_{eng}{j}") for j in range(max(1, nsem[eng]))]
                   for eng in self.ENGS}
        dma_sem = {i: nc.alloc_semaphore(f"d_{i}") for i in range(len(self.slot_cum))}
        print("[sched] dma sems", len(self.slot_cum), "max count", max(self.slot_cum) if self.slot_cum else 0,
              "ops", {e: len(v) for e, v in self.streams.items()})
        for tok, c in enumerate(self.slot_cum):
            assert c * 16 < 60000, (tok, c)
        streams = self.streams

        def mk(eng):
            def body(e):
                waited = {}
                for o in streams[eng]:
                    for w in o.waits:
                        if w[0] == "dma":
                            key, val, sem = ("d", w[1]), 16 * w[2], dma_sem[w[1]]
                        else:
                            d = w[1]
                            j = (d.seq - 1) // SEM_CH
                            key, val, sem = ("e", d.eng, j), (d.seq - 1) % SEM_CH + 1, eng_sem[d.eng][j]
                        if waited.get(key, 0) >= val:
                            continue
                        e.wait_ge(sem, val)
                        waited[key] = val
                    ins = o.fn(e)
                    if o.dma_tok is not None:
                        ins.then_inc(dma_sem[o.dma_tok], 16)
                    elif o.signal:
                        j = (o.seq - 1) // SEM_CH
                        ins.then_inc(eng_sem[eng][j], 1)
            return body

        with nc.Block() as block:
            block.sync(mk("sp"))
            block.scalar(mk("act"))
            block.vector(mk("dve"))
            block.gpsimd(mk("pool"))
            block.tensor(mk("pe"))


class Arena:
    def __init__(self, arena_ap, nwords):
        self.a = arena_ap
        self.n = nwords
        self.off = 0
        self.mark = 0

    def f32(self, n):
        assert self.off + n <= self.n, ("arena overflow", self.off, n, self.n)
        ap = self.a[:, self.off:self.off + n]
        self.off += n
        return ap

    def bf16(self, n):
        w = (n + 1) // 2
        assert self.off + w <= self.n, ("arena overflow", self.off, w, self.n)
        ap = self.a[:, self.off:self.off + w].bitcast(BF16)
        self.off += w
        return ap

    def set_mark(self):
        self.mark = self.off

    def reset(self):
        self.off = self.mark


class Ctx:
    def __init__(self, nc):
        self.nc = nc
        self.S = Sched(nc)
        nwords = (nc.sbuf_bytes_remaining - 2048) // 4
        self.arena_t = nc.alloc_sbuf_tensor("arena", [128, nwords], F32)
        self.A = Arena(self.arena_t[:, :], nwords)
        self.ps = [nc.alloc_psum_tensor(f"psb{i}", [128, 512], F32) for i in range(8)]
        self.uid = 0
        A = self.A
        self.ident = A.bf16(128)
        self.eps_t = A.f32(1)
        self.ones_bf = A.bf16(128)
        self.qkmax = A.f32(2)
        S = self.S
        S.op("pool", lambda e: e.memset(self.ident, 0.0), writes=["ident"])
        S.op("pool", lambda e: e.affine_select(out=self.ident, in_=self.ident, pattern=[[-1, 128]],
                                               compare_op=ALU.not_equal, fill=1.0, base=0,
                                               channel_multiplier=1), reads=["ident"], writes=["ident"])
        S.op("pool", lambda e: e.memset(self.eps_t, EPS), writes=["eps"])
        S.op("pool", lambda e: e.memset(self.ones_bf, 1.0), writes=["ones"])
        A.set_mark()

    def tok(self, name):
        self.uid += 1
        return f"{name}#{self.uid}"


def load_weight_bf16(cx, dst3, w_dram, kchunks, tokname):
    S = cx.S
    for kc in range(kchunks):
        S.op("pool", (lambda e, kc=kc: e.dma_start(out=dst3[:, kc, :], in_=w_dram[kc * 128:(kc + 1) * 128, :])),
             writes=[tokname], dma_tok=tokname)


def load_bcast_row(cx, dst, row_dram, tokname, eng="sp"):
    cx.S.op(eng, lambda e: e.dma_start(out=dst, in_=row_dram.partition_broadcast(128)),
            writes=[tokname], dma_tok=tokname)


def rms_prenorm_T(cx, h3, nsub, g_b, g_tok, ssq, rstd, junk, xn3, xnT3, in_toks, names, psbanks):
    S = cx.S
    for s in range(nsub):
        S.op("act", (lambda e, s=s: e.activation(out=junk, in_=h3[:, s, :], func=AF.Square,
                                                  accum_out=ssq[:, s:s + 1])),
             reads=in_toks, writes=[names["junk"], names["ssq"]])
    S.op("act", lambda e: e.activation(out=rstd[:, 0:nsub], in_=ssq[:, 0:nsub], func=AF.Sqrt,
                                       scale=1.0 / D, bias=cx.eps_t), reads=[names["ssq"], "eps"],
         writes=[names["rstd"]])
    S.op("dve", lambda e: e.reciprocal(out=rstd[:, 0:nsub], in_=rstd[:, 0:nsub]),
         reads=[names["rstd"]], writes=[names["rstd"]])
    for s in range(nsub):
        S.op("dve", (lambda e, s=s: e.scalar_tensor_tensor(out=xn3[:, s, :], in0=h3[:, s, :],
                                                            scalar=rstd[:, s:s + 1], in1=g_b,
                                                            op0=ALU.mult, op1=ALU.mult)),
             reads=in_toks + [names["rstd"], g_tok], writes=[names["xn"] + str(s)])
    for s in range(nsub):
        bank = psbanks[s % len(psbanks)]
        pt = cx.ps[bank][:, :].bitcast(BF16).rearrange("p (k c) -> p k c", c=128)
        for kc in range(8):
            S.op("pe", (lambda e, s=s, kc=kc, pt=pt: e.transpose(pt[:, kc, :], xn3[:, s, kc * 128:(kc + 1) * 128],
                                                                  cx.ident)),
                 reads=[names["xn"] + str(s), "ident"], writes=[f"ps{bank}"])
        eng = "act"
        if eng == "act":
            S.op("act", (lambda e, s=s, pt=pt: e.activation(out=xnT3[:, :, s * 128:(s + 1) * 128], in_=pt,
                                                            func=AF.Copy)),
                 reads=[f"ps{bank}"], writes=[names["xnT"]])
        else:
            S.op("dve", (lambda e, s=s, pt=pt: e.tensor_copy(out=xnT3[:, :, s * 128:(s + 1) * 128], in_=pt)),
                 reads=[f"ps{bank}"], writes=[names["xnT"]])


def post_norm_residual(cx, ps_halves, h_in, g_b, g_tok, ssq2, rstd2, junk, tmp, h_out, in_toks, names):
    S = cx.S
    for i, (bank, ap) in enumerate(ps_halves):
        S.op("act", (lambda e, i=i, ap=ap: e.activation(out=junk[:, 0:512], in_=ap, func=AF.Square,
                                                        accum_out=ssq2[:, i:i + 1])),
             reads=[f"ps{bank}"], writes=[names["junk"], names["ssq2"]])
    S.op("dve", lambda e: e.tensor_tensor(out=ssq2[:, 2:3], in0=ssq2[:, 0:1], in1=ssq2[:, 1:2], op=ALU.add),
         reads=[names["ssq2"]], writes=[names["ssq2"]])
    S.op("act", lambda e: e.activation(out=rstd2, in_=ssq2[:, 2:3], func=AF.Sqrt, scale=1.0 / D, bias=cx.eps_t),
         reads=[names["ssq2"], "eps"], writes=[names["rstd2"]])
    S.op("dve", lambda e: e.reciprocal(out=rstd2, in_=rstd2), reads=[names["rstd2"]], writes=[names["rstd2"]])
    for i, (bank, ap) in enumerate(ps_halves):
        S.op("dve", (lambda e, i=i, ap=ap: e.scalar_tensor_tensor(out=tmp[:, i * 512:(i + 1) * 512], in0=ap,
                                                                  scalar=rstd2, in1=g_b[:, i * 512:(i + 1) * 512],
                                                                  op0=ALU.mult, op1=ALU.mult)),
             reads=[f"ps{bank}", names["rstd2"], g_tok], writes=[names["tmp"]])
    S.op("pool", lambda e: e.tensor_tensor(out=h_out, in0=tmp, in1=h_in, op=ALU.add),
         reads=[names["tmp"]] + in_toks, writes=[names["hout"]])


def ffn_phase(cx, tiles, w_gate, w_up, w_down, g_pre, g_post, pfx):
    S, A = cx.S, cx.A
    S.barrier()
    A.reset()
    wg = A.bf16(8 * FH).rearrange("p (k f) -> p k f", f=FH)
    wu = A.bf16(8 * FH).rearrange("p (k f) -> p k f", f=FH)
    wd = A.bf16(NFC * D).rearrange("p (k f) -> p k f", f=D)
    gpre_b = A.f32(D)
    gpost_b = A.f32(D)
    load_weight_bf16(cx, wg, w_gate, 8, pfx + "wg")
    load_weight_bf16(cx, wu, w_up, 8, pfx + "wu")
    load_weight_bf16(cx, wd, w_down, NFC, pfx + "wd")
    load_bcast_row(cx, gpre_b, g_pre, pfx + "gpre")
    load_bcast_row(cx, gpost_b, g_post, pfx + "gpost")
    NB = 2
    NS = max(t[2] for t in tiles)
    TT = NS * 128
    hbuf = [A.f32(NS * D).rearrange("p (s d) -> p s d", d=D) for _ in range(NB)]
    obuf = [A.f32(D) for _ in range(NB)]
    xn = A.bf16(NS * D).rearrange("p (s d) -> p s d", d=D)
    xnT = [A.bf16(8 * TT).rearrange("p (k t) -> p k t", t=TT) for _ in range(NB)]
    hid = A.bf16(NFC * TT).rearrange("p (k t) -> p k t", t=TT)
    sg = [A.f32(TT) for _ in range(2)]
    junk = A.bf16(D)
    tmp = A.f32(D)
    ssq = A.f32(4)
    rstd = A.f32(4)
    ssq2 = A.f32(4)
    rstd2 = A.f32(1)
    osc_box = [0]

    def st_load(ti, src, dst, nsub, src_name, dst_name, row0):
        b = ti % NB
        h3 = hbuf[b]
        htok = f"{pfx}h{b}"
        S.op("sp", (lambda e, h3=h3, src=src, nsub=nsub: e.dma_start(
            out=h3[:, 0:nsub, :], in_=src.rearrange("(s p) d -> p s d", p=128))),
            reads=[f"{src_name}@{row0 + 128 * s_}" for s_ in range(nsub)], writes=[htok], dma_tok=htok)

    def st_prenorm(ti, src, dst, nsub, src_name, dst_name, row0):
        b = ti % NB
        names = dict(junk=pfx + "junk", ssq=pfx + "ssq", rstd=pfx + "rstd", xn=pfx + "xn", xnT=f"{pfx}xnT{b}")
        rms_prenorm_T(cx, hbuf[b], nsub, gpre_b, pfx + "gpre", ssq, rstd, junk, xn, xnT[b], [f"{pfx}h{b}"], names, [0, 1])

    def st_gateup(ti, src, dst, nsub, src_name, dst_name, row0):
        b = ti % NB
        nt = nsub * 128
        xt = xnT[b]
        xtok = f"{pfx}xnT{b}"
        for fc in range(NFC):
            bg = 2 + (fc % 2)
            bu = 4 + (fc % 2)
            for kc in range(8):
                S.op("pe", (lambda e, fc=fc, kc=kc, bg=bg, xt=xt: e.matmul(
                    cx.ps[bg][:, 0:nt], lhsT=wg[:, kc, fc * 128:(fc + 1) * 128], rhs=xt[:, kc, 0:nt],
                    start=(kc == 0), stop=(kc == 7))),
                    reads=[xtok, pfx + "wg"], writes=[f"ps{bg}"])
            for kc in range(8):
                S.op("pe", (lambda e, fc=fc, kc=kc, bu=bu, xt=xt: e.matmul(
                    cx.ps[bu][:, 0:nt], lhsT=wu[:, kc, fc * 128:(fc + 1) * 128], rhs=xt[:, kc, 0:nt],
                    start=(kc == 0), stop=(kc == 7))),
                    reads=[xtok, pfx + "wu"], writes=[f"ps{bu}"])
            sgb = sg[fc % 2]
            S.op("act", (lambda e, bg=bg, sgb=sgb: e.activation(out=sgb[:, 0:nt], in_=cx.ps[bg][:, 0:nt],
                                                                func=AF.Silu)),
                 reads=[f"ps{bg}"], writes=[f"{pfx}sg{fc % 2}"])
            S.op("dve", (lambda e, fc=fc, bu=bu, sgb=sgb: e.tensor_tensor(
                out=hid[:, fc, 0:nt], in0=cx.ps[bu][:, 0:nt], in1=sgb[:, 0:nt], op=ALU.mult)),
                reads=[f"ps{bu}", f"{pfx}sg{fc % 2}"], writes=[pfx + "hid"])

    def st_down(ti, src, dst, nsub, src_name, dst_name, row0):
        b = ti % NB
        h3 = hbuf[b]
        htok = f"{pfx}h{b}"
        for s in range(nsub):
            halves = []
            for hf in range(2):
                bank = 6 + hf
                for fc in range(NFC):
                    S.op("pe", (lambda e, s=s, hf=hf, fc=fc, bank=bank: e.matmul(
                        cx.ps[bank][:, :], lhsT=hid[:, fc, s * 128:(s + 1) * 128],
                        rhs=wd[:, fc, hf * 512:(hf + 1) * 512], start=(fc == 0), stop=(fc == NFC - 1))),
                        reads=[pfx + "hid", pfx + "wd"], writes=[f"ps{bank}"])
                halves.append((bank, cx.ps[bank][:, :]))
            ob = obuf[osc_box[0] % NB]
            otok = f"{pfx}o{osc_box[0] % NB}"
            osc_box[0] += 1
            n2 = dict(junk=pfx + "junk", ssq2=pfx + "ssq2", rstd2=pfx + "rstd2", tmp=pfx + "tmp", hout=otok)
            post_norm_residual(cx, halves, h3[:, s, :], gpost_b, pfx + "gpost", ssq2, rstd2, junk, tmp,
                               ob, [htok], n2)
            S.op("sp", (lambda e, ob=ob, dst=dst, s=s: e.dma_start(out=dst[s * 128:(s + 1) * 128, :], in_=ob)),
                 reads=[otok], writes=[f"{dst_name}@{row0 + 128 * s}"], dma_tok=otok)

    st_load(0, *tiles[0])
    st_prenorm(0, *tiles[0])
    for ti, t in enumerate(tiles):
        if ti + 1 < len(tiles):
            st_load(ti + 1, *tiles[ti + 1])
        st_gateup(ti, *t)
        if ti + 1 < len(tiles):
            st_prenorm(ti + 1, *tiles[ti + 1])
        st_down(ti, *t)


class Buf:
    def __init__(self, ap, tok):
        self.ap = ap
        self.tok = tok


def newbuf(cx, n, name, dt=F32):
    ap = cx.A.f32(n) if dt == F32 else cx.A.bf16(n)
    return Buf(ap, cx.tok(name))


def ew(cx, op, out, a, b, o_ap=None, a_ap=None, b_ap=None, eng="dve"):
    o_ap = out.ap if o_ap is None else o_ap
    a_ap = a.ap if a_ap is None else a_ap
    b_ap = b.ap if b_ap is None else b_ap
    cx.S.op(eng, lambda e: e.tensor_tensor(out=o_ap, in0=a_ap, in1=b_ap, op=op),
            reads=[a.tok, b.tok, out.tok], writes=[out.tok])


def ews(cx, out, a, s1, op0, s2=None, op1=None, o_ap=None, a_ap=None, eng="dve"):
    o_ap = out.ap if o_ap is None else o_ap
    a_ap = a.ap if a_ap is None else a_ap
    if op1 is None:
        cx.S.op(eng, lambda e: e.tensor_scalar(out=o_ap, in0=a_ap, scalar1=s1, scalar2=None, op0=op0),
                reads=[a.tok, out.tok], writes=[out.tok])
    else:
        cx.S.op(eng, lambda e: e.tensor_scalar(out=o_ap, in0=a_ap, scalar1=s1, scalar2=s2, op0=op0, op1=op1),
                reads=[a.tok, out.tok], writes=[out.tok])


def act(cx, out, a, func, scale=1.0, bias=None, o_ap=None, a_ap=None, extra_reads=()):
    o_ap = out.ap if o_ap is None else o_ap
    a_ap = a.ap if a_ap is None else a_ap
    if bias is None:
        cx.S.op("act", lambda e: e.activation(out=o_ap, in_=a_ap, func=func, scale=scale),
                reads=[a.tok, out.tok] + list(extra_reads), writes=[out.tok])
    else:
        cx.S.op("act", lambda e: e.activation(out=o_ap, in_=a_ap, func=func, scale=scale, bias=bias),
                reads=[a.tok, out.tok] + list(extra_reads), writes=[out.tok])


def cmul(cx, outr, outi, ar, ai, br, bi, t1, t2, shape3=None, bcast=None,
         or_ap=None, oi_ap=None, ar_ap=None, ai_ap=None, br_ap=None, bi_ap=None):
    or_ap = outr.ap if or_ap is None else or_ap
    oi_ap = outi.ap if oi_ap is None else oi_ap
    ar_ap = ar.ap if ar_ap is None else ar_ap
    ai_ap = ai.ap if ai_ap is None else ai_ap
    br_ap = br.ap if br_ap is None else br_ap
    bi_ap = bi.ap if bi_ap is None else bi_ap
    t1_ap, t2_ap = t1.ap, t2.ap
    if shape3 is not None:
        t1_ap = t1.ap[:, 0:shape3[0] * shape3[1]].rearrange("p (a b) -> p a b", b=shape3[1])
        t2_ap = t2.ap[:, 0:shape3[0] * shape3[1]].rearrange("p (a b) -> p a b", b=shape3[1])
    else:
        n = or_ap.shape[-1] if len(or_ap.shape) == 2 else None
        if n is not None:
            t1_ap = t1.ap[:, 0:n]
            t2_ap = t2.ap[:, 0:n]
    ew(cx, ALU.mult, t1, ar, br, o_ap=t1_ap, a_ap=ar_ap, b_ap=br_ap)
    ew(cx, ALU.mult, t2, ai, bi, o_ap=t2_ap, a_ap=ai_ap, b_ap=bi_ap)
    ew(cx, ALU.subtract, outr, t1, t2, o_ap=or_ap, a_ap=t1_ap, b_ap=t2_ap)
    ew(cx, ALU.mult, t1, ar, bi, o_ap=t1_ap, a_ap=ar_ap, b_ap=bi_ap)
    ew(cx, ALU.mult, t2, ai, br, o_ap=t2_ap, a_ap=ai_ap, b_ap=br_ap)
    ew(cx, ALU.add, outi, t1, t2, o_ap=oi_ap, a_ap=t1_ap, b_ap=t2_ap)


def s5_precompute(cx, Pm, NBLK, NSAMP):
    S, A = cx.S, cx.A
    import math
    W = {}
    W["KB"] = newbuf(cx, 4 * 8 * 128, "KB", BF16)
    W["XW"] = newbuf(cx, 4 * 8 * 2 * 128, "XW", BF16)
    W["XWs"] = newbuf(cx, 4 * 4 * 2 * 128, "XWs", BF16)
    W["CW"] = newbuf(cx, 16 * 8 * 2 * 32, "CW", BF16)
    W["ETr"] = newbuf(cx, 16 * NBLK, "ETr")
    W["ETi"] = newbuf(cx, 16 * NBLK, "ETi")
    W["RHO"] = newbuf(cx, 16 * NBLK, "RHO")
    W["L8r"] = newbuf(cx, 16, "L8r")
    W["L8i"] = newbuf(cx, 16, "L8i")
    W["L4r"] = newbuf(cx, 16, "L4r")
    W["L4i"] = newbuf(cx, 16, "L4i")
    keep = A.off
    KB4 = W["KB"].ap.rearrange("p (k m c) -> p k m c", k=4, m=8)
    XW5 = W["XW"].ap.rearrange("p (k s r c) -> p k s r c", k=4, s=8, r=2)
    XWs5 = W["XWs"].ap.rearrange("p (k s r c) -> p k s r c", k=4, s=4, r=2)
    CW5 = W["CW"].ap.rearrange("p (q t r c) -> p q t r c", q=16, t=8, r=2)
    nb = lambda n, name, dt=F32: newbuf(cx, n, name, dt)
    LR, LI, LDT = nb(16, "LR"), nb(16, "LI"), nb(16, "LDT")
    DTe, Ar, PH, MAG = nb(16, "DTe"), nb(16, "Ar"), nb(16, "PH"), nb(16, "MAG")
    hp = nb(1, "halfpi")
    cc = [nb(16, "cc0"), nb(16, "cc1")]
    ss = [nb(16, "ss0"), nb(16, "ss1")]
    t1, t2 = nb(512, "t1"), nb(512, "t2")
    PWr, PWi = nb(9 * 16, "PWr"), nb(9 * 16, "PWi")
    gr, gi, den, ta, tb = nb(16, "gr"), nb(16, "gi"), nb(16, "den"), nb(16, "ta"), nb(16, "tb")
    BR, BI, CR, CI = nb(256, "BR"), nb(256, "BI"), nb(256, "CR"), nb(256, "CI")
    BBr, BBi = nb(256, "BBr"), nb(256, "BBi")
    Wr, Wi = nb(256, "Wr"), nb(256, "Wi")
    WZ0 = nb(16 * 2 * 32, "WZ0", BF16)
    BZ = nb(16 * 2 * 32, "BZ", BF16)
    ZX = [nb(16 * 2 * 32, f"ZX{i}", BF16) for i in range(2)]
    Ekr, Eki, Ek2r, Ek2i = nb(16, "Ekr"), nb(16, "Eki"), nb(16, "Ek2r"), nb(16, "Ek2i")
    R8 = nb(16, "R8")

    def v3(b, n=16):
        return b.ap.rearrange("p (q j) -> p q j", j=n)

    def bc(ap16):
        return ap16.rearrange("p (q o) -> p q o", o=1).to_broadcast([128, 16, 16])

    def dma(out_ap, in_ap, buf, slow=False):
        if slow:
            S.op("sp", lambda e: e.dma_start(out=out_ap, in_=in_ap, allow_slow_non_contiguous=True),
                 writes=[buf.tok], dma_tok=buf.tok)
        else:
            S.op("sp", lambda e: e.dma_start(out=out_ap, in_=in_ap), writes=[buf.tok], dma_tok=buf.tok)

    for gl in range(2):
        ps_ = slice(gl * 64, (gl + 1) * 64)
        dma(LR.ap[ps_, :], Pm["lam_re"].rearrange("(q gl) p -> gl p q", gl=2)[gl], LR, slow=True)
        dma(LI.ap[ps_, :], Pm["lam_im"].rearrange("(q gl) p -> gl p q", gl=2)[gl], LI, slow=True)
        dma(LDT.ap[ps_, :], Pm["log_dt"].rearrange("(q gl) -> gl q", gl=2)[gl].partition_broadcast(64), LDT, slow=True)
        dma(v3(BR)[ps_, :, :], Pm["b_re"].rearrange("(q gl) p j -> gl p q j", gl=2)[gl], BR)
        dma(v3(BI)[ps_, :, :], Pm["b_im"].rearrange("(q gl) p j -> gl p q j", gl=2)[gl], BI)
        for q in range(16):
            dma(v3(CR)[ps_, q, :], Pm["c_re"][2 * q + gl].rearrange("i p -> p i"), CR, slow=True)
            dma(v3(CI)[ps_, q, :], Pm["c_im"][2 * q + gl].rearrange("i p -> p i"), CI, slow=True)
    S.op("pool", lambda e: e.memset(hp.ap, math.pi / 2), writes=[hp.tok])
    act(cx, DTe, LDT, AF.Exp)
    ew(cx, ALU.mult, Ar, LR, DTe)
    ew(cx, ALU.mult, PH, LI, DTe)
    act(cx, MAG, Ar, AF.Exp)
    act(cx, ss[0], PH, AF.Sin, scale=1.0 / 64)
    act(cx, cc[0], PH, AF.Sin, scale=1.0 / 64, bias=hp.ap, extra_reads=[hp.tok])
    cur = 0
    for it in range(6):
        nx = 1 - cur
        ew(cx, ALU.mult, t1, cc[cur], cc[cur], o_ap=t1.ap[:, 0:16])
        ew(cx, ALU.mult, t2, ss[cur], ss[cur], o_ap=t2.ap[:, 0:16])
        S.op("dve", (lambda e, cur=cur, nx=nx: e.scalar_tensor_tensor(
            out=ss[nx].ap, in0=cc[cur].ap, scalar=2.0, in1=ss[cur].ap, op0=ALU.mult, op1=ALU.mult)),
            reads=[cc[cur].tok, ss[cur].tok, ss[nx].tok], writes=[ss[nx].tok])
        ew(cx, ALU.subtract, cc[nx], t1, t2, a_ap=t1.ap[:, 0:16], b_ap=t2.ap[:, 0:16])
        cur = nx
    PW3r = PWr.ap.rearrange("p (m q) -> p m q", q=16)
    PW3i = PWi.ap.rearrange("p (m q) -> p m q", q=16)
    S.op("pool", lambda e: e.memset(PW3r[:, 0, :], 1.0), writes=[PWr.tok])
    S.op("pool", lambda e: e.memset(PW3i[:, 0, :], 0.0), writes=[PWi.tok])
    ew(cx, ALU.mult, PWr, MAG, cc[cur], o_ap=PW3r[:, 1, :])
    ew(cx, ALU.mult, PWi, MAG, ss[cur], o_ap=PW3i[:, 1, :])
    for m in range(2, 9):
        cmul(cx, PWr, PWi, PWr, PWi, PWr, PWi, t1, t2,
             or_ap=PW3r[:, m, :], oi_ap=PW3i[:, m, :], ar_ap=PW3r[:, m - 1, :], ai_ap=PW3i[:, m - 1, :],
             br_ap=PW3r[:, 1, :], bi_ap=PW3i[:, 1, :])
    ews(cx, ta, PWr, -1.0, ALU.add, a_ap=PW3r[:, 1, :])
    ew(cx, ALU.mult, den, LR, LR)
    ew(cx, ALU.mult, tb, LI, LI)
    ew(cx, ALU.add, den, den, tb)
    S.op("dve", lambda e: e.reciprocal(out=den.ap, in_=den.ap), reads=[den.tok], writes=[den.tok])
    ew(cx, ALU.mult, gr, ta, LR)
    ew(cx, ALU.mult, tb, PWi, LI, a_ap=PW3i[:, 1, :])
    ew(cx, ALU.add, gr, gr, tb)
    ew(cx, ALU.mult, gr, gr, den)
    ew(cx, ALU.mult, gi, PWi, LR, a_ap=PW3i[:, 1, :])
    ew(cx, ALU.mult, tb, ta, LI)
    ew(cx, ALU.subtract, gi, gi, tb)
    ew(cx, ALU.mult, gi, gi, den)
    s3 = (16, 16)
    cmul(cx, BBr, BBi, BR, BI, gr, gi, t1, t2, shape3=s3, or_ap=v3(BBr), oi_ap=v3(BBi),
         ar_ap=v3(BR), ai_ap=v3(BI), br_ap=bc(gr.ap), bi_ap=bc(gi.ap))

    def expand(dst_ap3, dst_buf, src_buf, neg=False):
        for gl in range(2):
            ps_ = slice(gl * 64, (gl + 1) * 64)
            o_ap = dst_ap3[ps_, :, gl * 16:(gl + 1) * 16]
            i_ap = v3(src_buf)[ps_, :, :]
            S.op("act", (lambda e, o_ap=o_ap, i_ap=i_ap: e.activation(out=o_ap, in_=i_ap, func=AF.Copy,
                                                                    scale=(-1.0 if neg else 1.0))),
                 reads=[src_buf.tok, dst_buf.tok], writes=[dst_buf.tok])

    for b_ in (WZ0, BZ, ZX[0], ZX[1], W["CW"]):
        S.op("pool", (lambda e, b_=b_: e.memset(b_.ap, 0.0)), writes=[b_.tok])
    WZ04 = WZ0.ap.rearrange("p (q r c) -> p q r c", q=16, r=2)
    BZ4 = BZ.ap.rearrange("p (q r c) -> p q r c", q=16, r=2)
    expand(BZ4[:, :, 0, :], BZ, BBr)
    expand(BZ4[:, :, 1, :], BZ, BBi)
    expand(WZ04[:, :, 0, :], WZ0, CR)
    expand(WZ04[:, :, 1, :], WZ0, CI, neg=True)
    for m in range(1, 9):
        cmul(cx, Wr, Wi, CR, CI, PWr, PWi, t1, t2, shape3=s3, or_ap=v3(Wr), oi_ap=v3(Wi),
             ar_ap=v3(CR), ai_ap=v3(CI), br_ap=bc(PW3r[:, m, :]), bi_ap=bc(PW3i[:, m, :]))
        expand(CW5[:, :, m - 1, 0, :], W["CW"], Wr)
        expand(CW5[:, :, m - 1, 1, :], W["CW"], Wi, neg=True)
    for k in range(4):
        for m in range(8):
            bank = (k * 8 + m) % 2
            pk = cx.ps[bank][:, 0:128]
            S.op("dve", (lambda e, pk=pk: e.memset(pk, 0.0)), writes=[f"ps{bank}"])
            for ql in range(4):
                q = 4 * k + ql
                for ri in range(2):
                    rhs = WZ04[:, q, ri, :] if m == 0 else CW5[:, q, m - 1, ri, :]
                    rtok = WZ0.tok if m == 0 else W["CW"].tok
                    S.op("pe", (lambda e, pk=pk, ql=ql, q=q, ri=ri, rhs=rhs: e.matmul(
                        pk[32 * ql:32 * ql + 32, 32 * ql:32 * ql + 32], lhsT=BZ4[:, q, ri, :], rhs=rhs,
                        start=(ri == 0), stop=(ri == 1), tile_position=(0, 32 * ql), skip_group_check=True)),
                        reads=[BZ.tok, rtok], writes=[f"ps{bank}"])
            S.op("act", (lambda e, pk=pk, k=k, m=m: e.activation(out=KB4[:, k, m, :], in_=pk, func=AF.Copy)),
                 reads=[f"ps{bank}"], writes=[W["KB"].tok])

    def xweights(dst5, dst_buf, nsig, pw_of_sigma):
        for sg_ in range(nsig):
            m = pw_of_sigma(sg_)
            z = ZX[sg_ % 2]
            z4 = z.ap.rearrange("p (r q c) -> p r q c", q=16, r=2)
            cmul(cx, Wr, Wi, BBr, BBi, PWr, PWi, t1, t2, shape3=s3, or_ap=v3(Wr), oi_ap=v3(Wi),
                 ar_ap=v3(BBr), ai_ap=v3(BBi), br_ap=bc(PW3r[:, m, :]), bi_ap=bc(PW3i[:, m, :]))
            expand(z4[:, 0, :, :], z, Wr)
            expand(z4[:, 1, :, :], z, Wi)
            bank = 2 + (sg_ % 2)
            pt = cx.ps[bank][:, :].bitcast(BF16).rearrange("p (k r c) -> p k r c", k=4, r=2)
            for k in range(4):
                for ri in range(2):
                    S.op("pe", (lambda e, pt=pt, k=k, ri=ri, z4=z4: e.transpose(
                        pt[:, k, ri, :], z4[:, ri, 4 * k:4 * k + 4, :], cx.ident)),
                        reads=[z.tok, "ident"], writes=[f"ps{bank}"])
            S.op("act", (lambda e, pt=pt, sg_=sg_: e.activation(out=dst5[:, :, sg_, :, :], in_=pt, func=AF.Copy)),
                 reads=[f"ps{bank}"], writes=[dst_buf.tok])

    xweights(XW5, W["XW"], 8, lambda s_: 7 - s_)
    xweights(XWs5, W["XWs"], 4, lambda s_: 3 - s_)
    S.op("dve", lambda e: e.tensor_copy(out=W["L8r"].ap, in_=PW3r[:, 8, :]), reads=[PWr.tok], writes=[W["L8r"].tok])
    S.op("dve", lambda e: e.tensor_copy(out=W["L8i"].ap, in_=PW3i[:, 8, :]), reads=[PWi.tok], writes=[W["L8i"].tok])
    S.op("dve", lambda e: e.tensor_copy(out=W["L4r"].ap, in_=PW3r[:, 4, :]), reads=[PWr.tok], writes=[W["L4r"].tok])
    S.op("dve", lambda e: e.tensor_copy(out=W["L4i"].ap, in_=PW3i[:, 4, :]), reads=[PWi.tok], writes=[W["L4i"].tok])
    act(cx, R8, Ar, AF.Exp, scale=8.0)
    RH3 = W["RHO"].ap.rearrange("p (q c) -> p q c", c=NBLK)
    S.op("dve", lambda e: e.tensor_copy(out=RH3, in_=R8.ap.rearrange("p (q o) -> p q o", o=1).to_broadcast([128, 16, NBLK])),
         reads=[R8.tok], writes=[W["RHO"].tok])
    S.op("pool", lambda e: e.memset(RH3[:, :, 0:1], 0.0), reads=[W["RHO"].tok], writes=[W["RHO"].tok])
    S.op("dve", lambda e: e.reciprocal(out=den.ap, in_=R8.ap), reads=[R8.tok, den.tok], writes=[den.tok])
    ew(cx, ALU.mult, Ekr, W["L8r"], den)
    ew(cx, ALU.mult, Eki, W["L8i"], den)
    ET3r = W["ETr"].ap.rearrange("p (q c) -> p q c", c=NBLK)
    ET3i = W["ETi"].ap.rearrange("p (q c) -> p q c", c=NBLK)
    S.op("pool", lambda e: e.memset(ET3r[:, :, 0:1], 1.0), writes=[W["ETr"].tok])
    S.op("pool", lambda e: e.memset(ET3i[:, :, 0:1], 0.0), writes=[W["ETi"].tok])
    kk = 1
    ek = (Ekr, Eki)
    ek2 = (Ek2r, Ek2i)
    while kk < NBLK:
        bcr = ek[0].ap.rearrange("p (q o) -> p q o", o=1).to_broadcast([128, 16, kk])
        bci = ek[1].ap.rearrange("p (q o) -> p q o", o=1).to_broadcast([128, 16, kk])
        cmul(cx, W["ETr"], W["ETi"], W["ETr"], W["ETi"], ek[0], ek[1], t1, t2, shape3=(16, kk),
             or_ap=ET3r[:, :, kk:2 * kk], oi_ap=ET3i[:, :, kk:2 * kk],
             ar_ap=ET3r[:, :, 0:kk], ai_ap=ET3i[:, :, 0:kk], br_ap=bcr, bi_ap=bci)
        cmul(cx, ek2[0], ek2[1], ek[0], ek[1], ek[0], ek[1], t1, t2)
        ek, ek2 = ek2, ek
        kk *= 2
    W["_keep"] = keep
    W["_dbg"] = dict(PWr=PWr, PWi=PWi, BBr=BBr, BBi=BBi, gr=gr, gi=gi)
    return W


POOL_W = (2, 4, 8, 16)


def mixer_ab_phase(cx, x_dram, T, ycatT, Pm, outs, samp, n_light=0, fix_tile=0, invc_dram=None):
    S, A = cx.S, cx.A
    S.barrier()
    A.reset()
    TT = 512
    NBLK = TT // 8
    NT = T // TT
    pfx = "m0"
    w_in = A.bf16(8 * D).rearrange("p (k f) -> p k f", f=D)
    w_glu = A.bf16(4 * 512).rearrange("p (k f) -> p k f", f=512)
    pw = A.bf16(4 * 128).rearrange("p (g f) -> p g f", f=128)
    g0_b = A.f32(D)
    dcol = A.f32(4)
    bglu = A.f32(4)
    pscale = A.f32(4)
    invc = A.f32(4 * 16).rearrange("p (g t) -> p g t", t=16)
    load_weight_bf16(cx, w_in, Pm["w_in"], 8, pfx + "w_in")
    load_weight_bf16(cx, w_glu, Pm["w_glu"], 4, pfx + "w_glu")
    for g in range(4):
        S.op("pool", (lambda e, g=g: e.dma_start(out=pw[:, g, :], in_=Pm["pool_w"][g])),
             writes=[pfx + "pw"], dma_tok=pfx + "pw")
    load_bcast_row(cx, g0_b, Pm["g0"], pfx + "g0")
    S.op("sp", lambda e: e.dma_start(out=dcol, in_=Pm["d"].rearrange("(k p) -> p k", p=128),
                                     allow_slow_non_contiguous=True), writes=[pfx + "dcol"], dma_tok=pfx + "dcol")
    S.op("sp", lambda e: e.dma_start(out=bglu, in_=Pm["b_glu"].rearrange("(k p) -> p k", p=128),
                                     allow_slow_non_contiguous=True), writes=[pfx + "bglu"], dma_tok=pfx + "bglu")
    S.op("sp", lambda e: e.dma_start(out=pscale, in_=Pm["pool_scale"].rearrange("(k p) -> p k", p=128),
                                     allow_slow_non_contiguous=True), writes=[pfx + "pscale"], dma_tok=pfx + "pscale")
    if invc_dram is None:
        for g in range(4):
            w = POOL_W[g]
            S.op("pool", (lambda e, g=g, w=w: e.memset(invc[:, g, :], 1.0 / w)), writes=[pfx + "invc"])
            for t in range(w - 1):
                S.op("pool", (lambda e, g=g, t=t: e.memset(invc[:, g, t:t + 1], 1.0 / (t + 1))),
                     reads=[pfx + "invc"], writes=[pfx + "invc"])
    else:
        S.op("sp", lambda e: e.dma_start(out=invc.rearrange("p g t -> p (g t)"), in_=invc_dram.partition_broadcast(128)),
             writes=[pfx + "invc"], dma_tok=pfx + "invc")
    if K_STAGE <= -2:
        return None
    Wt = s5_precompute(cx, Pm, NBLK, 4)
    if K_STAGE <= -1:
        return None
    S.barrier()
    A.off = Wt["_keep"]
    KB4 = Wt["KB"].ap.rearrange("p (k m c) -> p k m c", k=4, m=8)
    XW5 = Wt["XW"].ap.rearrange("p (k s r c) -> p k s r c", k=4, s=8, r=2)
    XWs5 = Wt["XWs"].ap.rearrange("p (k s r c) -> p k s r c", k=4, s=4, r=2)
    CW5 = Wt["CW"].ap.rearrange("p (q t r c) -> p q t r c", q=16, t=8, r=2)
    ETr = Wt["ETr"].ap
    ETi = Wt["ETi"].ap
    RHO = Wt["RHO"].ap
    wtoks = [Wt[n].tok for n in ("KB", "XW", "XWs", "CW", "ETr", "ETi", "RHO", "L8r", "L8i", "L4r", "L4i")]
    hld = [A.f32(D) for _ in range(2)]
    xn = [A.bf16(D) for _ in range(2)]
    xnT = A.bf16(8 * TT).rearrange("p (k t) -> p k t", t=TT)
    uP = A.f32(4 * (16 + TT)).rearrange("p (g t) -> p g t", t=16 + TT)
    uS = A.f32(4 * TT).rearrange("p (k t) -> p k t", t=TT)
    uSb = A.bf16(4 * TT).rearrange("p (k t) -> p k t", t=TT)
    pa = A.f32(16 + TT)
    pb = A.f32(16 + TT)
    dT = A.bf16(4 * TT).rearrange("p (g t) -> p g t", t=TT)
    NE = 16 * NBLK
    Xr, Xi = A.f32(NE), A.f32(NE)
    Mr, Mi = A.f32(NE), A.f32(NE)
    Tt = A.f32(NE)
    Hr = A.f32(16 * (NBLK + 1))
    Hi = A.f32(16 * (NBLK + 1))
    Hb = A.bf16(2 * NE).rearrange("p (r q c) -> p r q c", r=2, q=16)
    cr_ = A.f32(16)
    ci_ = A.f32(16)
    ct_ = A.f32(16)
    ys = A.f32(4 * TT).rearrange("p (k t) -> p k t", t=TT)
    zb = A.bf16(4 * TT).rearrange("p (k t) -> p k t", t=TT)
    sig = A.f32(TT)
    ycat = A.bf16(8 * TT).rearrange("p (k t) -> p k t", t=TT)
    junk = A.bf16(D)
    ssq = A.f32(4)
    rstd = A.f32(4)
    Hr3 = Hr.rearrange("p (q c) -> p q c", c=NBLK + 1)
    Hi3 = Hi.rearrange("p (q c) -> p q c", c=NBLK + 1)
    X3r = Xr.rearrange("p (q c) -> p q c", c=NBLK)
    X3i = Xi.rearrange("p (q c) -> p q c", c=NBLK)
    M3r = Mr.rearrange("p (q c) -> p q c", c=NBLK)
    M3i = Mi.rearrange("p (q c) -> p q c", c=NBLK)
    T3 = Tt.rearrange("p (q c) -> p q c", c=NBLK)
    ycatT_v = ycatT.rearrange("(k p) t -> p k t", p=128)
    tk = lambda n: pfx + n
    S.op("pool", lambda e: e.memset(Hr, 0.0), writes=[tk("H")])
    S.op("pool", lambda e: e.memset(Hi, 0.0), writes=[tk("H")])
    S.op("pool", lambda e: e.memset(uP, 0.0), writes=[tk("uP")])

    def tt_op(eng, op, o, a, b, reads, writes):
        S.op(eng, lambda e: e.tensor_tensor(out=o, in0=a, in1=b, op=op), reads=reads, writes=writes)

    def prenorm_tile(src_rows, nsub, ncols):
        for s in range(nsub):
            hb = hld[s % 2]
            htok = tk(f"hld{s % 2}")
            xb = xn[s % 2]
            xtok = tk(f"xn{s % 2}")
            src = src_rows(s)
            S.op("sp", (lambda e, hb=hb, src=src: e.dma_start(out=hb[0:src.shape[0], :], in_=src)),
                 writes=[htok], dma_tok=htok)
            S.op("act", (lambda e, hb=hb, s=s: e.activation(out=junk, in_=hb, func=AF.Square,
                                                            accum_out=ssq[:, s:s + 1])),
                 reads=[htok], writes=[tk("junk"), tk("ssq")])
            S.op("act", (lambda e, s=s: e.activation(out=rstd[:, s:s + 1], in_=ssq[:, s:s + 1], func=AF.Sqrt,
                                                     scale=1.0 / D, bias=cx.eps_t)),
                 reads=[tk("ssq"), "eps"], writes=[tk("rstd")])
            S.op("dve", (lambda e, s=s: e.reciprocal(out=rstd[:, s:s + 1], in_=rstd[:, s:s + 1])),
                 reads=[tk("rstd")], writes=[tk("rstd")])
            S.op("dve", (lambda e, hb=hb, xb=xb, s=s: e.scalar_tensor_tensor(
                out=xb, in0=hb, scalar=rstd[:, s:s + 1], in1=g0_b, op0=ALU.mult, op1=ALU.mult)),
                reads=[htok, tk("rstd"), tk("g0")], writes=[xtok])
            bank = 4 + (s % 2)
            pt = cx.ps[bank][:, :].bitcast(BF16).rearrange("p (k c) -> p k c", c=128)
            for kc in range(8):
                S.op("pe", (lambda e, pt=pt, xb=xb, kc=kc: e.transpose(pt[:, kc, :], xb[:, kc * 128:(kc + 1) * 128],
                                                                       cx.ident)),
                     reads=[xtok, "ident"], writes=[f"ps{bank}"])
            S.op("act", (lambda e, pt=pt, s=s: e.activation(out=xnT[:, :, s * 128:(s + 1) * 128], in_=pt,
                                                            func=AF.Copy)),
                 reads=[f"ps{bank}"], writes=[tk("xnT")])

    def in_proj(ncols, uP_dst, uS_dst, uSb_dst):
        for oc in range(8):
            bank = oc % 4
            for kc in range(8):
                S.op("pe", (lambda e, oc=oc, kc=kc, bank=bank: e.matmul(
                    cx.ps[bank][:, 0:ncols], lhsT=w_in[:, kc, oc * 128:(oc + 1) * 128], rhs=xnT[:, kc, 0:ncols],
                    start=(kc == 0), stop=(kc == 7))),
                    reads=[tk("xnT"), tk("w_in")], writes=[f"ps{bank}"])
            if K_STAGE <= 0.6:
                continue
            if oc < 4:
                dst = uP_dst(oc)
                S.op("act", (lambda e, bank=bank, dst=dst: e.activation(out=dst, in_=cx.ps[bank][:, 0:ncols],
                                                                        func=AF.Copy)),
                     reads=[f"ps{bank}", tk("uP")], writes=[tk("uP")])
            else:
                d1 = uS_dst(oc - 4)
                d2 = uSb_dst(oc - 4)
                S.op("act", (lambda e, bank=bank, d1=d1: e.activation(out=d1, in_=cx.ps[bank][:, 0:ncols],
                                                                      func=AF.Copy)),
                     reads=[f"ps{bank}"], writes=[tk("uS")])
                S.op("act", (lambda e, bank=bank, d2=d2: e.activation(out=d2, in_=cx.ps[bank][:, 0:ncols], func=AF.Copy)),
                     reads=[f"ps{bank}"], writes=[tk("uSb")])

    def glu_and_store(ncols, col0):
        S.op("act", lambda e: e.activation(out=ys[:, :, 0:ncols], in_=ys[:, :, 0:ncols], func=AF.Gelu_apprx_tanh),
             reads=[tk("ys")], writes=[tk("ys")])
        S.op("act", lambda e: e.activation(out=zb[:, :, 0:ncols], in_=ys[:, :, 0:ncols], func=AF.Copy),
             reads=[tk("ys")], writes=[tk("zb")])
        for oc in range(4):
            bank = oc % 4
            for kc in range(4):
                S.op("pe", (lambda e, oc=oc, kc=kc, bank=bank: e.matmul(
                    cx.ps[bank][:, 0:ncols], lhsT=w_glu[:, kc, oc * 128:(oc + 1) * 128], rhs=zb[:, kc, 0:ncols],
                    start=(kc == 0), stop=(kc == 3))),
                    reads=[tk("zb"), tk("w_glu")], writes=[f"ps{bank}"])
            S.op("act", (lambda e, oc=oc, bank=bank: e.activation(out=sig[:, 0:ncols], in_=cx.ps[bank][:, 0:ncols],
                                                                  func=AF.Sigmoid, bias=bglu[:, oc:oc + 1])),
                 reads=[f"ps{bank}", tk("bglu")], writes=[tk("sig")])
            S.op("dve", (lambda e, oc=oc: e.tensor_tensor(out=ycat[:, 4 + oc, 0:ncols], in0=ys[:, oc, 0:ncols],
                                                          in1=sig[:, 0:ncols], op=ALU.mult)),
                 reads=[tk("ys"), tk("sig")], writes=[tk("ycat")])
        S.op("sp", lambda e: e.dma_start(out=ycatT_v[:, :, col0:col0 + ncols], in_=ycat[:, :, 0:ncols]),
             reads=[tk("ycat")], writes=[f"ycatT@{col0}"], dma_tok=tk("ycat"))

    def pool_matmul(ncols):
        for g in range(4):
            bank = g % 4
            S.op("pe", (lambda e, g=g, bank=bank: e.matmul(cx.ps[bank][:, 0:ncols], lhsT=pw[:, g, :],
                                                          rhs=dT[:, g, 0:ncols], start=True, stop=True)),
                 reads=[tk("dT"), tk("pw")], writes=[f"ps{bank}"])
            S.op("act", (lambda e, g=g, bank=bank: e.activation(out=ycat[:, g, 0:ncols], in_=cx.ps[bank][:, 0:ncols],
                                                                func=AF.Copy, scale=pscale[:, g:g + 1])),
                 reads=[f"ps{bank}", tk("pscale")], writes=[tk("ycat")])

    def prompt_tile(ti):
        t0 = ti * TT
        prenorm_tile(lambda s: x_dram[t0 + s * 128:t0 + (s + 1) * 128, :], 4, TT)
        if K_STAGE <= 0.5:
            return
        in_proj(TT, lambda g: uP[:, g, 16:16 + TT], lambda k: uS[:, k, :], lambda k: uSb[:, k, :])
        light = ti < n_light
        for g in range(4):
            if light:
                break
            cur = uP[:, g, :]
            ctoks = [tk("uP")]
            sh = 1
            for step in range(g + 1):
                dst = pa if step % 2 == 0 else pb
                dtok = tk("pa") if step % 2 == 0 else tk("pb")
                lo = 2 * sh - 1
                S.op("pool", (lambda e, dst=dst, cur=cur, lo=lo, sh=sh: e.tensor_tensor(
                    out=dst[:, lo:16 + TT], in0=cur[:, lo:16 + TT], in1=cur[:, lo - sh:16 + TT - sh], op=ALU.add)),
                    reads=ctoks, writes=[dtok])
                cur, ctoks = dst, [dtok]
                sh *= 2
            w = POOL_W[g]
            S.op("dve", (lambda e, g=g, cur=cur, w=w: e.scalar_tensor_tensor(
                out=dT[:, g, :], in0=cur[:, 16:16 + TT], scalar=1.0 / w, in1=uP[:, g, 16:16 + TT],
                op0=ALU.mult, op1=ALU.subtract)), reads=ctoks + [tk("uP")], writes=[tk("dT")])
            if ti == fix_tile:
                S.op("dve", (lambda e, g=g, cur=cur: e.tensor_tensor(out=sig[:, 0:16], in0=cur[:, 16:32],
                                                                     in1=invc[:, g, :], op=ALU.mult)),
                     reads=ctoks + [tk("invc")], writes=[tk("sig")])
                S.op("dve", (lambda e, g=g: e.tensor_tensor(out=dT[:, g, 0:16], in0=sig[:, 0:16],
                                                            in1=uP[:, g, 16:32], op=ALU.subtract)),
                     reads=[tk("sig"), tk("uP"), tk("dT")], writes=[tk("dT")])
        if not light:
            pool_matmul(TT)
        if ti == NT - 1:
            for g in range(4):
                S.op("sp", (lambda e, g=g: e.dma_start(
                    out=outs["pool_prompt"][:, g * 128:(g + 1) * 128].rearrange("t c -> c t"),
                    in_=uP[:, g, TT + 1:TT + 16], allow_slow_non_contiguous=True)),
                    reads=[tk("uP")], dma_tok=tk("pp_out"))
        S.op("pool", lambda e: e.tensor_copy(out=uP[:, :, 0:16], in_=uP[:, :, TT:TT + 16]),
             reads=[tk("uP")], writes=[tk("uP")])
        if K_STAGE <= 2:
            return
        uSb4 = uSb.rearrange("p k (c s) -> p k c s", s=8)
        for q in range(16):
            k, ql = q // 4, q % 4
            for ri in range(2):
                bank = ri * 2 + (q // 8)
                outp = cx.ps[bank][:, (q % 8) * NBLK:(q % 8 + 1) * NBLK]
                for sg_ in range(8):
                    S.op("pe", (lambda e, k=k, ql=ql, ri=ri, sg_=sg_, outp=outp: e.matmul(
                        outp, lhsT=XW5[32 * ql:32 * ql + 32, k, sg_, ri, :], rhs=uSb4[32 * ql:32 * ql + 32, k, :, sg_],
                        start=(sg_ == 0), stop=(sg_ == 7), tile_position=(32 * ql, 0))),
                        reads=[tk("uSb")] + wtoks, writes=[f"ps{bank}"])
        for ri, X_ in ((0, Xr), (1, Xi)):
            for hf in range(2):
                bank = ri * 2 + hf
                S.op("act", (lambda e, X_=X_, hf=hf, bank=bank: e.activation(
                    out=X_[:, hf * 512:(hf + 1) * 512], in_=cx.ps[bank][:, :], func=AF.Copy)),
                    reads=[f"ps{bank}"], writes=[tk("X")])
        if K_STAGE <= 3:
            return
        hpr, hpi = Hr3[:, :, NBLK], Hi3[:, :, NBLK]
        L8r, L8i = Wt["L8r"].ap, Wt["L8i"].ap
        tt_op("dve", ALU.mult, cr_, L8r, hpr, [tk("H")] + wtoks, [tk("c")])
        tt_op("dve", ALU.mult, ct_, L8i, hpi, [tk("H")] + wtoks, [tk("ct")])
        tt_op("dve", ALU.subtract, cr_, cr_, ct_, [tk("c"), tk("ct")], [tk("c")])
        tt_op("dve", ALU.mult, ci_, L8r, hpi, [tk("H")] + wtoks, [tk("ci")])
        tt_op("dve", ALU.mult, ct_, L8i, hpr, [tk("H")] + wtoks + [tk("c")], [tk("ct")])
        tt_op("dve", ALU.add, ci_, ci_, ct_, [tk("ci"), tk("ct")], [tk("ci")])
        tt_op("dve", ALU.add, X3r[:, :, 0], X3r[:, :, 0], cr_, [tk("X"), tk("c")], [tk("X")])
        tt_op("dve", ALU.add, X3i[:, :, 0], X3i[:, :, 0], ci_, [tk("X"), tk("ci")], [tk("X")])
        S.op("dve", lambda e: e.tensor_copy(out=Hr3[:, :, 0], in_=hpr), reads=[tk("H"), tk("Hb")], writes=[tk("H")])
        S.op("dve", lambda e: e.tensor_copy(out=Hi3[:, :, 0], in_=hpi), reads=[tk("H"), tk("Hb")], writes=[tk("H")])
        tt_op("dve", ALU.mult, Mr, Xr, ETr, [tk("X")] + wtoks, [tk("M")])
        tt_op("dve", ALU.mult, Tt, Xi, ETi, [tk("X")] + wtoks, [tk("T")])
        tt_op("dve", ALU.add, Mr, Mr, Tt, [tk("M"), tk("T")], [tk("M")])
        tt_op("dve", ALU.mult, Mi, Xi, ETr, [tk("X")] + wtoks, [tk("M")])
        tt_op("dve", ALU.mult, Tt, Xr, ETi, [tk("X"), tk("M")] + wtoks, [tk("T")])
        tt_op("dve", ALU.subtract, Mi, Mi, Tt, [tk("M"), tk("T")], [tk("M")])
        S.op("dve", lambda e: e.tensor_tensor_scan(out=Xr, data0=RHO, data1=Mr, initial=0.0, op0=ALU.mult, op1=ALU.add),
             reads=[tk("M"), tk("X")] + wtoks, writes=[tk("X")])
        S.op("dve", lambda e: e.tensor_tensor_scan(out=Xi, data0=RHO, data1=Mi, initial=0.0, op0=ALU.mult, op1=ALU.add),
             reads=[tk("M"), tk("X")] + wtoks, writes=[tk("X")])
        tt_op("dve", ALU.mult, Mr, Xr, ETr, [tk("X"), tk("M")] + wtoks, [tk("M")])
        tt_op("dve", ALU.mult, Tt, Xi, ETi, [tk("X"), tk("T")] + wtoks, [tk("T")])
        tt_op("dve", ALU.subtract, Hr3[:, :, 1:NBLK + 1], M3r, T3, [tk("M"), tk("T"), tk("H"), tk("c"), tk("ci")], [tk("H")])
        tt_op("dve", ALU.mult, Mi, Xi, ETr, [tk("X"), tk("M")] + wtoks, [tk("M")])
        tt_op("dve", ALU.mult, Tt, Xr, ETi, [tk("X"), tk("T"), tk("H")] + wtoks, [tk("T")])
        tt_op("dve", ALU.add, Hi3[:, :, 1:NBLK + 1], M3i, T3, [tk("M"), tk("T"), tk("H")], [tk("H")])
        S.op("act", lambda e: e.activation(out=Hb[:, 0, :, :], in_=Hr3[:, :, 0:NBLK], func=AF.Copy),
             reads=[tk("H")], writes=[tk("Hb")])
        S.op("act", lambda e: e.activation(out=Hb[:, 1, :, :], in_=Hi3[:, :, 0:NBLK], func=AF.Copy),
             reads=[tk("H")], writes=[tk("Hb")])
        if light:
            return
        for k in range(4):
            bank = 4 + k
            yv = cx.ps[bank][:, :].rearrange("p (c s) -> p c s", s=8)
            first = [True]
            for tau in range(8):
                for sg_ in range(tau + 1):
                    st = first[0]
                    first[0] = False
                    S.op("pe", (lambda e, k=k, tau=tau, sg_=sg_, yv=yv, st=st: e.matmul(
                        yv[:, :, tau], lhsT=KB4[:, k, tau - sg_, :], rhs=uSb4[:, k, :, sg_],
                        start=st, stop=False, skip_group_check=True)),
                        reads=[tk("uSb")] + wtoks, writes=[f"ps{bank}"])
                for ql in range(4):
                    q = 4 * k + ql
                    for ri in range(2):
                        S.op("pe", (lambda e, q=q, ql=ql, tau=tau, ri=ri, yv=yv: e.matmul(
                            yv[32 * ql:32 * ql + 32, :, tau], lhsT=CW5[:, q, tau, ri, :], rhs=Hb[:, ri, q, :],
                            start=False, stop=(ri == 1 and ql == 3), tile_position=(0, 32 * ql),
                            skip_group_check=True)),
                            reads=[tk("Hb")] + wtoks, writes=[f"ps{bank}"])
            S.op("dve", (lambda e, k=k, bank=bank: e.scalar_tensor_tensor(
                out=ys[:, k, :], in0=uS[:, k, :], scalar=dcol[:, k:k + 1], in1=cx.ps[bank][:, :],
                op0=ALU.mult, op1=ALU.add)), reads=[tk("uS"), tk("dcol"), f"ps{bank}"], writes=[tk("ys")])
        if K_STAGE <= 5:
            return
        glu_and_store(TT, t0)
        if ti == NT - 1:
            for gl in range(2):
                for ri, H3 in ((0, Hr3), (1, Hi3)):
                    S.op("sp", (lambda e, gl=gl, ri=ri, H3=H3: e.dma_start(
                        out=outs["s5_prompt"].rearrange("(q gl) p r -> gl p q r", gl=2)[gl][:, :, ri],
                        in_=H3[gl * 64:(gl + 1) * 64, :, NBLK], allow_slow_non_contiguous=True)),
                        reads=[tk("H")], dma_tok=tk("s5_out"))

    def sample_mixer():
        NBS = 4
        prenorm_tile(lambda s: samp["x_s"], 1, 128)
        uPs = pa[:, 0:4 * NBS * 20]
        uPs4 = uPs.rearrange("p (g b t) -> p g b t", g=4, b=NBS)
        pbs = pb[:, 0:NBS * 20].rearrange("p (b t) -> p b t", t=20)
        pcs = sig[:, 0:NBS * 20].rearrange("p (b t) -> p b t", t=20)
        S.op("pool", lambda e: e.memset(uPs, 0.0), reads=[tk("pa")], writes=[tk("pa")])
        for b in range(NBS):
            for g in range(4):
                S.op("sp", (lambda e, b=b, g=g: e.dma_start(
                    out=uPs4[:, g, b, 1:16], in_=samp["state_pool"][b, :, g * 128:(g + 1) * 128].rearrange("t c -> c t"),
                    allow_slow_non_contiguous=True)), reads=[tk("pa")], writes=[tk("pa")], dma_tok=tk("spl"))
        for oc in range(8):
            bank = oc % 4
            for kc in range(8):
                S.op("pe", (lambda e, oc=oc, kc=kc, bank=bank: e.matmul(
                    cx.ps[bank][:, 0:16], lhsT=w_in[:, kc, oc * 128:(oc + 1) * 128], rhs=xnT[:, kc, 0:16],
                    start=(kc == 0), stop=(kc == 7))), reads=[tk("xnT"), tk("w_in")], writes=[f"ps{bank}"])
            if oc < 4:
                S.op("act", (lambda e, oc=oc, bank=bank: e.activation(
                    out=uPs4[:, oc, :, 16:20], in_=cx.ps[bank][:, 0:16].rearrange("p (b t) -> p b t", t=4), func=AF.Copy)),
                    reads=[f"ps{bank}", tk("pa")], writes=[tk("pa")])
            else:
                S.op("act", (lambda e, oc=oc, bank=bank: e.activation(out=uS[:, oc - 4, 0:16], in_=cx.ps[bank][:, 0:16],
                                                                      func=AF.Copy)),
                     reads=[f"ps{bank}", tk("uS")], writes=[tk("uS")])
                S.op("act", (lambda e, oc=oc, bank=bank: e.activation(out=uSb[:, oc - 4, 0:16], in_=cx.ps[bank][:, 0:16],
                                                                      func=AF.Copy)),
                     reads=[f"ps{bank}", tk("uSb")], writes=[tk("uSb")])
        S.op("pool", lambda e: e.memset(ycat[:, :, 0:128], 0.0), reads=[tk("ycat")], writes=[tk("ycat")])
        S.op("pool", lambda e: e.memset(ys[:, :, 0:128], 0.0), reads=[tk("ys")], writes=[tk("ys")])
        for g in range(4):
            cur = uPs4[:, g, :, :]
            ctoks = [tk("pa")]
            sh = 1
            for step in range(g + 1):
                dst = pbs if step % 2 == 0 else pcs
                dtok = tk("pb") if step % 2 == 0 else tk("sig")
                lo = 2 * sh - 1
                S.op("pool", (lambda e, dst=dst, cur=cur, lo=lo, sh=sh: e.tensor_tensor(
                    out=dst[:, :, lo:20], in0=cur[:, :, lo:20], in1=cur[:, :, lo - sh:20 - sh], op=ALU.add)),
                    reads=ctoks + [dtok], writes=[dtok])
                cur, ctoks = dst, [dtok]
                sh *= 2
            w = POOL_W[g]
            S.op("dve", (lambda e, g=g, cur=cur, w=w: e.scalar_tensor_tensor(
                out=dT[:, g, 0:16].rearrange("p (b t) -> p b t", t=4), in0=cur[:, :, 16:20], scalar=1.0 / w,
                in1=uPs4[:, g, :, 16:20], op0=ALU.mult, op1=ALU.subtract)),
                reads=ctoks + [tk("pa"), tk("dT")], writes=[tk("dT")])
        pool_matmul(16)
        for b in range(NBS):
            for g in range(4):
                S.op("sp", (lambda e, b=b, g=g: e.dma_start(
                    out=samp["pool_out"][b, :, g * 128:(g + 1) * 128].rearrange("t c -> c t"), in_=uPs4[:, g, b, 5:20],
                    allow_slow_non_contiguous=True)), reads=[tk("pa")], dma_tok=tk("pso"))
        H0r = Hr[:, 0:16 * NBS].rearrange("p (q b) -> p q b", b=NBS)
        H0i = Hi[:, 0:16 * NBS].rearrange("p (q b) -> p q b", b=NBS)
        Hbs = Hb.rearrange("p r q c -> p (r q c)")[:, 0:2 * 16 * NBS].rearrange("p (r q b) -> p r q b", r=2, q=16)
        for b in range(NBS):
            for gl in range(2):
                for ri, H0 in ((0, H0r), (1, H0i)):
                    S.op("sp", (lambda e, b=b, gl=gl, ri=ri, H0=H0: e.dma_start(
                        out=H0[gl * 64:(gl + 1) * 64, :, b],
                        in_=samp["state_s5"][b].rearrange("(q gl) p r -> gl p q r", gl=2)[gl][:, :, ri],
                        allow_slow_non_contiguous=True)), reads=[tk("H"), tk("Hb")], writes=[tk("H")], dma_tok=tk("s5l"))
        S.op("act", lambda e: e.activation(out=Hbs[:, 0, :, :], in_=H0r, func=AF.Copy), reads=[tk("H"), tk("Hb")], writes=[tk("Hb")])
        S.op("act", lambda e: e.activation(out=Hbs[:, 1, :, :], in_=H0i, func=AF.Copy), reads=[tk("H"), tk("Hb")], writes=[tk("Hb")])
        uSb4s = uSb[:, :, 0:16].rearrange("p k (b s) -> p k b s", s=4)
        for q in range(16):
            k, ql = q // 4, q % 4
            for ri in range(2):
                bank = 2 * ri
                for sg_ in range(4):
                    S.op("pe", (lambda e, k=k, ql=ql, q=q, ri=ri, sg_=sg_, bank=bank: e.matmul(
                        cx.ps[bank][:, q * NBS:(q + 1) * NBS], lhsT=XWs5[32 * ql:32 * ql + 32, k, sg_, ri, :],
                        rhs=uSb4s[32 * ql:32 * ql + 32, k, :, sg_], start=(sg_ == 0), stop=(sg_ == 3),
                        tile_position=(32 * ql, 0))), reads=[tk("uSb")] + wtoks, writes=[f"ps{bank}"])
        Xs_r = Xr[:, 0:16 * NBS].rearrange("p (q b) -> p q b", b=NBS)
        Xs_i = Xi[:, 0:16 * NBS].rearrange("p (q b) -> p q b", b=NBS)
        S.op("act", lambda e: e.activation(out=Xr[:, 0:16 * NBS], in_=cx.ps[0][:, 0:16 * NBS], func=AF.Copy),
             reads=["ps0", tk("X")], writes=[tk("X")])
        S.op("act", lambda e: e.activation(out=Xi[:, 0:16 * NBS], in_=cx.ps[2][:, 0:16 * NBS], func=AF.Copy),
             reads=["ps2", tk("X")], writes=[tk("X")])
        bc4 = lambda ap16: ap16.rearrange("p (q o) -> p q o", o=1).to_broadcast([128, 16, NBS])
        L4r, L4i = bc4(Wt["L4r"].ap), bc4(Wt["L4i"].ap)
        M4r = Mr[:, 0:16 * NBS].rearrange("p (q b) -> p q b", b=NBS)
        M4i = Mi[:, 0:16 * NBS].rearrange("p (q b) -> p q b", b=NBS)
        T4 = Tt[:, 0:16 * NBS].rearrange("p (q b) -> p q b", b=NBS)
        tt_op("dve", ALU.mult, M4r, H0r, L4r, [tk("H"), tk("M")] + wtoks, [tk("M")])
        tt_op("dve", ALU.mult, T4, H0i, L4i, [tk("H"), tk("T")] + wtoks, [tk("T")])
        tt_op("dve", ALU.subtract, M4r, M4r, T4, [tk("M"), tk("T")], [tk("M")])
        tt_op("dve", ALU.add, Xs_r, Xs_r, M4r, [tk("M"), tk("X")], [tk("X")])
        tt_op("dve", ALU.mult, M4i, H0r, L4i, [tk("H"), tk("M")] + wtoks, [tk("M")])
        tt_op("dve", ALU.mult, T4, H0i, L4r, [tk("H"), tk("T"), tk("M")] + wtoks, [tk("T")])
        tt_op("dve", ALU.add, M4i, M4i, T4, [tk("M"), tk("T")], [tk("M")])
        tt_op("dve", ALU.add, Xs_i, Xs_i, M4i, [tk("M"), tk("X")], [tk("X")])
        for b in range(NBS):
            for gl in range(2):
                for ri, X_ in ((0, Xs_r), (1, Xs_i)):
                    S.op("sp", (lambda e, b=b, gl=gl, ri=ri, X_=X_: e.dma_start(
                        out=samp["s5_out"][b].rearrange("(q gl) p r -> gl p q r", gl=2)[gl][:, :, ri],
                        in_=X_[gl * 64:(gl + 1) * 64, :, b], allow_slow_non_contiguous=True)),
                        reads=[tk("X")], dma_tok=tk("s5so"))
        for k in range(4):
            bank = 4 + k
            yv = cx.ps[bank][:, 0:16].rearrange("p (b s) -> p b s", s=4)
            first = [True]
            for tau in range(4):
                for sg_ in range(tau + 1):
                    st = first[0]
                    first[0] = False
                    S.op("pe", (lambda e, k=k, tau=tau, sg_=sg_, yv=yv, st=st: e.matmul(
                        yv[:, :, tau], lhsT=KB4[:, k, tau - sg_, :], rhs=uSb4s[:, k, :, sg_],
                        start=st, stop=False, skip_group_check=True)),
                        reads=[tk("uSb")] + wtoks, writes=[f"ps{bank}"])
                for ql in range(4):
                    q = 4 * k + ql
                    for ri in range(2):
                        S.op("pe", (lambda e, q=q, ql=ql, tau=tau, ri=ri, yv=yv: e.matmul(
                            yv[32 * ql:32 * ql + 32, :, tau], lhsT=CW5[:, q, tau, ri, :], rhs=Hbs[:, ri, q, :],
                            start=False, stop=(ri == 1 and ql == 3), tile_position=(0, 32 * ql),
                            skip_group_check=True)),
                            reads=[tk("Hb")] + wtoks, writes=[f"ps{bank}"])
            S.op("dve", (lambda e, k=k, bank=bank: e.scalar_tensor_tensor(
                out=ys[:, k, 0:16], in0=uS[:, k, 0:16], scalar=dcol[:, k:k + 1], in1=cx.ps[bank][:, 0:16],
                op0=ALU.mult, op1=ALU.add)), reads=[tk("uS"), tk("dcol"), f"ps{bank}", tk("ys")], writes=[tk("ys")])
        glu_and_store(128, T)

    for ti in range(NT):
        if K_STAGE <= 0.1:
            break
        prompt_tile(ti)
    if samp is not None:
        sample_mixer()
    return dict(prenorm_tile=prenorm_tile, in_proj=in_proj, glu_and_store=glu_and_store, pool_matmul=pool_matmul,
                bufs=dict(uP=uP, uS=uS, uSb=uSb, pa=pa, pb=pb, dT=dT, ys=ys, Hr3=Hr3, Hi3=Hi3, Hb=Hb, Xr=Xr, Xi=Xi,
                          ycat=ycat, sig=sig, Wt=Wt, XWs5=XWs5, KB4=KB4, CW5=CW5, invc=invc, dcol=dcol),
                tk=tk, TT=TT, NBLK=NBLK, wtoks=wtoks)


def proj_res_phase(cx, aT, tiles, W_dram, g_post, pfx, row_perm=None):
    S, A = cx.S, cx.A
    S.barrier()
    A.reset()
    Wb = A.bf16(8 * D).rearrange("p (k f) -> p k f", f=D)
    g_b = A.f32(D)
    load_weight_bf16(cx, Wb, W_dram, 8, pfx + "W")
    load_bcast_row(cx, g_b, g_post, pfx + "g")
    NS = max(t[1] for t in tiles)
    TT = NS * 128
    NB = 2
    abuf = [A.bf16(8 * TT).rearrange("p (k t) -> p k t", t=TT) for _ in range(NB)]
    hbuf = [A.f32(D) for _ in range(3)]
    junk = A.bf16(D)
    tmp = A.f32(D)
    ssq2 = A.f32(4)
    rstd2 = A.f32(1)
    aT_v = aT.rearrange("(k p) t -> p k t", p=128)
    cnt = [0]

    def tile_body(ti, col0, nsub, h_src, h_dst, a_name, hsrc_name, dst_name, row0):
        ab = abuf[ti % NB]
        atok = f"{pfx}a{ti % NB}"
        nc_ = nsub * 128
        S.op("sp", lambda e: e.dma_start(out=ab[:, :, 0:nc_], in_=aT_v[:, :, col0:col0 + nc_]),
             reads=[f"{a_name}@{col0}"], writes=[atok], dma_tok=atok)
        for s in range(nsub):
            i = cnt[0] % 3
            cnt[0] += 1
            hb = hbuf[i]
            htok = f"{pfx}h{i}"
            S.op("sp", (lambda e, hb=hb, s=s: e.dma_start(out=hb, in_=h_src[s * 128:(s + 1) * 128, :])),
                 reads=[f"{hsrc_name}@{row0 + 128 * s}"], writes=[htok], dma_tok=htok)
            halves = []
            for hf in range(2):
                bank = 2 * (s % 2) + hf
                for kc in range(8):
                    S.op("pe", (lambda e, s=s, hf=hf, kc=kc, bank=bank: e.matmul(
                        cx.ps[bank][:, :], lhsT=ab[:, kc, s * 128:(s + 1) * 128],
                        rhs=Wb[:, kc, hf * 512:(hf + 1) * 512], start=(kc == 0), stop=(kc == 7))),
                        reads=[atok, pfx + "W"], writes=[f"ps{bank}"])
                halves.append((bank, cx.ps[bank][:, :]))
            n2 = dict(junk=pfx + "junk", ssq2=pfx + "ssq2", rstd2=pfx + "rstd2", tmp=pfx + "tmp", hout=htok)
            post_norm_residual(cx, halves, hb, g_b, pfx + "g", ssq2, rstd2, junk, tmp, hb, [htok], n2)
            S.op("sp", (lambda e, hb=hb, s=s: e.dma_start(out=h_dst[s * 128:(s + 1) * 128, :], in_=hb)),
                 reads=[htok], writes=[f"{dst_name}@{row0 + 128 * s}"], dma_tok=htok)

    for ti, t in enumerate(tiles):
        tile_body(ti, *t)


def qkv_phase(cx, hsrc, hname, tiles, w_qkv, g_pre, QT, KT, V, pfx="q1"):
    S, A = cx.S, cx.A
    S.barrier()
    A.reset()
    W3 = A.bf16(8 * 3 * D).rearrange("p (k f) -> p k f", f=3 * D)
    g_b = A.f32(D)
    load_weight_bf16(cx, W3, w_qkv, 8, pfx + "W")
    load_bcast_row(cx, g_b, g_pre, pfx + "g")
    TT = 512
    h3 = A.f32(4 * D).rearrange("p (s d) -> p s d", d=D)
    xn = A.bf16(4 * D).rearrange("p (s d) -> p s d", d=D)
    xnT = A.bf16(8 * TT).rearrange("p (k t) -> p k t", t=TT)
    qt = [A.bf16(8 * TT).rearrange("p (k t) -> p k t", t=TT) for _ in range(2)]
    sq = [A.bf16(TT) for _ in range(2)]
    vb = [A.bf16(D) for _ in range(2)]
    vf = [A.f32(D) for _ in range(2)]
    junk = A.bf16(D)
    ssq = A.f32(4)
    rstd = A.f32(4)
    m1 = A.f32(2)
    SEL = A.bf16(8 * 16).rearrange("p (k h) -> p k h", h=16)
    S.op("pool", lambda e: e.memset(SEL, 0.0), writes=[pfx + "SEL"])
    for kc in range(8):
        for hh in range(2):
            S.op("pool", (lambda e, kc=kc, hh=hh: e.memset(SEL[64 * hh:64 * hh + 64, kc, 2 * kc + hh:2 * kc + hh + 1], 1.0)),
                 reads=[pfx + "SEL"], writes=[pfx + "SEL"])
    S.op("pool", lambda e: e.memset(cx.qkmax, 0.0), writes=["qkmax"])
    QT_v = QT.rearrange("(k p) t -> p k t", p=128)
    KT_v = KT.rearrange("(k p) t -> p k t", p=128)
    cnt = [0, 0]

    def tile_body(ti, row0, nsub, k_out, v_out, nvalid):
        ncols = nsub * 128
        htok = pfx + "h"
        S.op("sp", lambda e: e.dma_start(out=h3[:, 0:nsub, :],
                                         in_=hsrc[row0:row0 + ncols, :].rearrange("(s p) d -> p s d", p=128)),
             reads=[f"{hname}@{row0 + 128 * s_}" for s_ in range(nsub)], writes=[htok], dma_tok=htok)
        names = dict(junk=pfx + "junk", ssq=pfx + "ssq", rstd=pfx + "rstd", xn=pfx + "xn", xnT=pfx + "xnT")
        rms_prenorm_T(cx, h3, nsub, g_b, pfx + "g", ssq, rstd, junk, xn, xnT, [htok], names, [0, 1])
        for which, dst_v, dname in ((0, QT_v, "QT"), (1, KT_v, "KT")):
            qb = qt[cnt[0] % 2]
            qtok = f"{pfx}qt{cnt[0] % 2}"
            cnt[0] += 1
            for oc in range(8):
                bank = 2 + (oc % 2)
                for kc in range(8):
                    S.op("pe", (lambda e, oc=oc, kc=kc, bank=bank, which=which: e.matmul(
                        cx.ps[bank][:, 0:ncols], lhsT=W3[:, kc, which * D + oc * 128:which * D + (oc + 1) * 128],
                        rhs=xnT[:, kc, 0:ncols], start=(kc == 0), stop=(kc == 7))),
                        reads=[pfx + "xnT", pfx + "W"], writes=[f"ps{bank}"])
                S.op("act", (lambda e, oc=oc, bank=bank, qb=qb: e.activation(out=qb[:, oc, 0:ncols],
                                                                             in_=cx.ps[bank][:, 0:ncols], func=AF.Copy)),
                     reads=[f"ps{bank}"], writes=[qtok])
                if K_STAGE <= 6:
                    continue
                sb = sq[oc % 2]
                S.op("act", (lambda e, oc=oc, bank=bank, sb=sb: e.activation(out=sb[:, 0:ncols],
                                                                             in_=cx.ps[bank][:, 0:ncols], func=AF.Square)),
                     reads=[f"ps{bank}"], writes=[f"{pfx}sq{oc % 2}"])
                S.op("pe", (lambda e, oc=oc, sb=sb: e.matmul(cx.ps[4][0:16, 0:ncols], lhsT=SEL[:, oc, :], rhs=sb[:, 0:ncols],
                                                             start=(oc == 0), stop=(oc == 7))),
                     reads=[f"{pfx}sq{oc % 2}", pfx + "SEL"], writes=["ps4"])
            if K_STAGE > 6.5:
              S.op("dve", (lambda e: e.reduce_max(out=m1[0:16, 0:1], in_=cx.ps[4][0:16, 0:ncols], axis=AX.X)),
                 reads=["ps4"], writes=[pfx + "m1"])
            if K_STAGE > 6.7:
              S.op("dve", (lambda e, which=which: e.tensor_tensor(out=cx.qkmax[0:16, which:which + 1],
                                                                in0=cx.qkmax[0:16, which:which + 1], in1=m1[0:16, 0:1],
                                                                op=ALU.max)),
                 reads=[pfx + "m1", "qkmax"], writes=["qkmax"])
            S.op("sp", (lambda e, qb=qb, dst_v=dst_v: e.dma_start(out=dst_v[:, :, row0:row0 + ncols], in_=qb[:, :, 0:ncols])),
                 reads=[qtok], writes=[f"{dname}@{row0}"], dma_tok=qtok)
        for which, out_ap in ((2, v_out), (1, k_out)):
            if which == 1 and k_out is None:
                continue
            if K_STAGE <= 7:
                out_ap = None
                if which == 1:
                    continue
            for s in range(nsub):
                i = cnt[1] % 2
                cnt[1] += 1
                vbb, vff = vb[i], vf[i]
                vtok, ftok = f"{pfx}vb{i}", f"{pfx}vf{i}"
                for hf in range(2):
                    bank = 5 + hf
                    for kc in range(8):
                        S.op("pe", (lambda e, s=s, hf=hf, kc=kc, bank=bank, which=which: e.matmul(
                            cx.ps[bank][:, :], lhsT=xnT[:, kc, s * 128:(s + 1) * 128],
                            rhs=W3[:, kc, which * D + hf * 512:which * D + (hf + 1) * 512],
                            start=(kc == 0), stop=(kc == 7))),
                            reads=[pfx + "xnT", pfx + "W"], writes=[f"ps{bank}"])
                    if which == 2:
                        S.op("act", (lambda e, hf=hf, bank=bank, vbb=vbb: e.activation(
                            out=vbb[:, hf * 512:(hf + 1) * 512], in_=cx.ps[bank][:, :], func=AF.Copy)),
                            reads=[f"ps{bank}"], writes=[vtok])
                    if out_ap is not None:
                        S.op("dve", (lambda e, hf=hf, bank=bank, vff=vff: e.tensor_copy(
                            out=vff[:, hf * 512:(hf + 1) * 512], in_=cx.ps[bank][:, :])),
                            reads=[f"ps{bank}"], writes=[ftok])
                if which == 2:
                    S.op("sp", (lambda e, s=s, vbb=vbb: e.dma_start(out=V[row0 + s * 128:row0 + (s + 1) * 128, :], in_=vbb)),
                         reads=[vtok], writes=[f"V@{row0 + s * 128}"], dma_tok=vtok)
                if out_ap is not None:
                    nv = min(128, nvalid - s * 128)
                    if nv > 0:
                        S.op("sp", (lambda e, s=s, vff=vff, nv=nv, out_ap=out_ap: e.dma_start(
                            out=out_ap[s * 128:s * 128 + nv, :], in_=vff[0:nv, :])),
                            reads=[ftok], dma_tok=ftok)

    for ti, t in enumerate(tiles):
        tile_body(ti, *t)


DILS = tuple(int(x) for x in os.environ.get("K_DILS", "1,4,16").split(","))


def attn_setup(cx, pfx="a1"):
    S, A = cx.S, cx.A
    mask = A.bf16(512).rearrange("p (h k q) -> p h k q", h=2, k=2)
    cneg = A.f32(8)
    cb = A.f32(16)
    c16 = A.f32(2)
    dg = A.bf16(16)
    on16 = A.bf16(128)
    S.op("pool", lambda e: e.memset(mask, 1.0), writes=[pfx + "mask"])
    for hh in range(2):
        S.op("pool", (lambda e, hh=hh: e.affine_select(out=mask[:, hh, 0, :], in_=mask[:, hh, 0, :], pattern=[[-1, 128]],
                                                       compare_op=ALU.is_ge, fill=0.0, base=0, channel_multiplier=1)),
             reads=[pfx + "mask"], writes=[pfx + "mask"])
        S.op("pool", (lambda e, hh=hh: e.affine_select(out=mask[:, hh, 1, :], in_=mask[:, hh, 1, :], pattern=[[1, 128]],
                                                       compare_op=ALU.is_ge, fill=0.0, base=0, channel_multiplier=-1)),
             reads=[pfx + "mask"], writes=[pfx + "mask"])
    S.op("dve", lambda e: e.tensor_tensor(out=c16[0:16, 0:1], in0=cx.qkmax[0:16, 0:1], in1=cx.qkmax[0:16, 1:2], op=ALU.mult),
         reads=["qkmax"], writes=[pfx + "c16"])
    S.op("act", lambda e: e.activation(out=c16[0:16, 0:1], in_=c16[0:16, 0:1], func=AF.Sqrt, scale=(1.02 / 8) ** 2),
         reads=[pfx + "c16"], writes=[pfx + "c16"])
    S.op("dve", lambda e: e.tensor_scalar(out=dg[0:16, :], in0=cx.ident[0:16, 0:16], scalar1=c16[0:16, 0:1], scalar2=None,
                                          op0=ALU.mult), reads=[pfx + "c16", "ident"], writes=[pfx + "dg"])
    S.op("pool", lambda e: e.memset(on16, 1.0), writes=[pfx + "on16"])
    S.op("pe", lambda e: e.matmul(cx.ps[7][:, 0:16], lhsT=on16[0:16, :], rhs=dg[0:16, :], start=True, stop=True),
         reads=[pfx + "on16", pfx + "dg"], writes=["ps7"])
    S.op("dve", lambda e: e.tensor_copy(out=cb, in_=cx.ps[7][:, 0:16]), reads=["ps7"], writes=[pfx + "cb"])
    cb3 = cb.rearrange("p (k h) -> p k h", h=2)
    S.op("dve", lambda e: e.tensor_tensor(out=cneg, in0=cb3[:, :, 0], in1=cb3[:, :, 1], op=ALU.max),
         reads=[pfx + "cb"], writes=[pfx + "cneg"])
    S.op("dve", lambda e: e.tensor_scalar(out=cneg, in0=cneg, scalar1=-1.0, scalar2=None, op0=ALU.mult),
         reads=[pfx + "cneg"], writes=[pfx + "cneg"])
    return dict(mask=mask, cneg=cneg, pfx=pfx)


def attn_phase(cx, T, QT, KT, V, attnT, pfx="a1", st_list=None, halo_end=0, hflag=None):
    S, A = cx.S, cx.A
    S.barrier()
    A.reset()
    C = attn_setup(cx, pfx)
    mask, cneg = C["mask"], C["cneg"]
    ST = 2048
    NST = T // ST
    if st_list is None:
        st_list = list(range(NST))
    fcol = A.f32(1)
    if hflag is not None:
        S.op("sp", lambda e: e.dma_start(out=fcol, in_=hflag.partition_broadcast(128)), writes=[pfx + "fcol"], dma_tok=pfx + "fcol")
    qtb = [A.bf16(ST + 16) for _ in range(2)]
    ktw = [A.bf16(2 * ST + 16) for _ in range(2)]
    acc = [A.f32(2 * (ST + 16)).rearrange("p (n t) -> p n t", n=2) for _ in range(2)]
    rec = A.f32(ST)
    ob = [A.bf16(ST) for _ in range(2)]
    NV = 10
    vbuf = [A.bf16(128) for _ in range(NV)]
    NP = 4
    pT = [A.bf16(512).rearrange("p (h k q) -> p h k q", h=2, k=2) for _ in range(NP)]
    cnt = dict(v=0, p=0, s=0, n=0, par=0, m=0)

    def unit(kc, par, T0, d, r, beta, nb_first, qtile, kwin, accb, vmap, first_d):
        vmap = dict(vmap)
        if K_STAGE <= 12:
            return
        kblocks = [b for b in (beta - 1, beta) if b >= 0]
        qoff = r + d * 128 * beta - T0
        qv = qtile[:, qoff:qoff + 128 * d].rearrange("p (m s) -> p m s", s=d)[:, :, 0]
        pi = cnt["p"] % NP
        cnt["p"] += 1
        pt = pT[pi]
        ptok = f"{pfx}pT{pi}"
        sb0 = 2 * (cnt["s"] % 2)
        cnt["s"] += 1
        for hh in range(2):
            sbank = sb0 + hh
            for b in kblocks:
                kb = b - (beta - 1)
                koff = r + d * 128 * b - (T0 - ST)
                kv = kwin[:, koff:koff + 128 * d].rearrange("p (m s) -> p m s", s=d)[:, :, 0]
                S.op("pe", (lambda e, hh=hh, kb=kb, kv=kv, qv=qv, sbank=sbank: e.matmul(
                    cx.ps[sbank][:, kb * 128:(kb + 1) * 128],
                    lhsT=kv[64 * hh:64 * hh + 64, :], rhs=qv[64 * hh:64 * hh + 64, :], start=True, stop=True)),
                    reads=[f"{pfx}qt{par}", f"{pfx}kt{par}"], writes=[f"ps{sbank}"])
        if K_STAGE <= 12.5:
            return
        for hh in range(2):
            sbank = sb0 + hh
            S.op("act", (lambda e, sbank=sbank, pt=pt, hh=hh: e.activation(
                out=pt[:, hh, :, :].rearrange("p k q -> p (k q)"), in_=cx.ps[sbank][:, 0:256], func=AF.Exp, scale=0.125,
                bias=cneg[:, kc:kc + 1])), reads=[f"ps{sbank}", pfx + "cneg"], writes=[ptok])
        if K_STAGE <= 13:
            return
        halo_kb0 = (hflag is not None) and (beta - 1 >= 0) and (r + d * (128 * (beta - 1) + 127) < halo_end)
        if halo_kb0:
            S.op("dve", (lambda e, pt=pt: e.scalar_tensor_tensor(out=pt[:, :, 0, :], in0=pt[:, :, 0, :], scalar=fcol[:, 0:1],
                                                                 in1=mask[:, :, 0, :], op0=ALU.mult, op1=ALU.mult)),
                 reads=[ptok, pfx + "mask", pfx + "fcol"], writes=[ptok])
            S.op("dve", (lambda e, pt=pt: e.tensor_tensor(out=pt[:, :, 1, :], in0=pt[:, :, 1, :], in1=mask[:, :, 1, :], op=ALU.mult)),
                 reads=[ptok, pfx + "mask"], writes=[ptok])
        else:
            S.op("dve", (lambda e, pt=pt: e.tensor_tensor(out=pt.rearrange("p h k q -> p (h k q)"), in0=pt.rearrange("p h k q -> p (h k q)"),
                                                          in1=mask.rearrange("p h k q -> p (h k q)"), op=ALU.mult)),
                 reads=[ptok, pfx + "mask"], writes=[ptok])
        def stage_b():
            nbank = 4 + cnt["n"] % 3
            cnt["n"] += 1
            for hh in range(2):
                for j, b in enumerate(kblocks):
                    kb = b - (beta - 1)
                    vi = vmap[b]
                    S.op("pe", (lambda e, hh=hh, kb=kb, vi=vi, j=j, nbank=nbank, pt=pt: e.matmul(
                        cx.ps[nbank][64 * hh:64 * hh + 64, 0:128], lhsT=vbuf[vi][:, 64 * hh:64 * hh + 64], rhs=pt[:, hh, kb, :],
                        start=(j == 0), stop=(j == len(kblocks) - 1), tile_position=(0, 64 * hh))),
                        reads=[ptok, f"{pfx}v{vi}"], writes=[f"ps{nbank}"])
                for j, b in enumerate(kblocks):
                    kb = b - (beta - 1)
                    S.op("pe", (lambda e, hh=hh, kb=kb, j=j, nbank=nbank, pt=pt: e.matmul(
                        cx.ps[nbank][64 * hh:64 * hh + 64, 128:256], lhsT=cx.ones_bf[:, 0:64], rhs=pt[:, hh, kb, :],
                        start=(j == 0), stop=(j == len(kblocks) - 1), tile_position=(0, 64 * hh))),
                        reads=[ptok, "ones"], writes=[f"ps{nbank}"])
            av = accb[:, :, qoff:qoff + 128 * d].rearrange("p n (m s) -> p n m s", s=d)[:, :, :, 0]
            nd = cx.ps[nbank][:, 0:256].rearrange("p (n q) -> p n q", n=2)
            if first_d:
                S.op("act", (lambda e, av=av, nd=nd: e.activation(out=av, in_=nd, func=AF.Copy)),
                     reads=[f"ps{nbank}"], writes=[f"{pfx}acc{par}"])
            else:
                S.op("dve", (lambda e, av=av, nd=nd: e.tensor_tensor(out=av, in0=nd, in1=av, op=ALU.add)),
                     reads=[f"ps{nbank}", f"{pfx}acc{par}"], writes=[f"{pfx}acc{par}"])


        return stage_b

    def pair_body(st, kc):
        T0 = st * ST
        par = cnt["par"] % 2
        cnt["par"] += 1
        qtile, kwin, accb, obb = qtb[par], ktw[par], acc[par], ob[par]
        S.op("sp", lambda e: e.dma_start(out=qtile[:, 0:ST], in_=QT[kc * 128:(kc + 1) * 128, T0:T0 + ST]),
             reads=[f"QT@{T0 + 512 * i}" for i in range(4)], writes=[f"{pfx}qt{par}"], dma_tok=f"{pfx}qt{par}")
        k0 = max(0, T0 - ST)
        S.op("sp", lambda e: e.dma_start(out=kwin[:, k0 - (T0 - ST):2 * ST], in_=KT[kc * 128:(kc + 1) * 128, k0:T0 + ST]),
             reads=[f"KT@{k0 + 512 * i}" for i in range((T0 + ST - k0) // 512)], writes=[f"{pfx}kt{par}"],
             dma_tok=f"{pfx}kt{par}")
        pend = []
        SKEW = 2
        for di, d in enumerate(DILS):
            nbq = ST // (128 * d)
            for r in range(d):
                beta0 = T0 // (128 * d)
                vmap = {}
                for b in range(beta0 - 1, beta0 + nbq):
                    if b < 0:
                        continue
                    vi = cnt["v"] % NV
                    cnt["v"] += 1
                    vmap[b] = vi
                    vsrc = V[0:T, :].rearrange("(m s) c -> m s c", s=d)[128 * b:128 * b + 128, r, kc * 128:(kc + 1) * 128]
                    S.op("sp", (lambda e, vi=vi, vsrc=vsrc: e.dma_start(out=vbuf[vi], in_=vsrc)),
                         reads=[f"V@{128 * i}" for i in range((r + d * 128 * b) // 128, (r + d * (128 * b + 127)) // 128 + 1)],
                         writes=[f"{pfx}v{vi}"], dma_tok=f"{pfx}v{vi}")
                    if b >= beta0:
                        pend.append(unit(kc, par, T0, d, r, b, beta0, qtile, kwin, accb, vmap, di == 0))
                        if len(pend) > SKEW:
                            pend.pop(0)()
        while pend:
            pend.pop(0)()
        S.op("dve", lambda e: e.reciprocal(out=rec, in_=accb[:, 1, 0:ST]), reads=[f"{pfx}acc{par}"], writes=[pfx + "rec"])
        S.op("dve", lambda e: e.tensor_tensor(out=obb, in0=accb[:, 0, 0:ST], in1=rec, op=ALU.mult),
             reads=[f"{pfx}acc{par}", pfx + "rec"], writes=[f"{pfx}ob{par}"])
        S.op("sp", lambda e: e.dma_start(out=attnT[kc * 128:(kc + 1) * 128, T0:T0 + ST], in_=obb),
             reads=[f"{pfx}ob{par}"], writes=[f"attnT@{T0}k{kc}"], dma_tok=f"{pfx}ob{par}")

    for st in st_list:
        for kc in range(8):
            pair_body(st, kc)


def attn_sample_phase(cx, T, QT, KT, V, ck, cv, attnT, pfx="as"):
    S, A = cx.S, cx.A
    S.barrier()
    A.reset()
    NBS = 4
    tk = lambda n: pfx + n
    QTs = A.bf16(8 * 16).rearrange("p (k t) -> p k t", t=16)
    KTs = A.bf16(8 * 16).rearrange("p (k t) -> p k t", t=16)
    Vn = A.bf16(D)
    ktok = [A.bf16(D) for _ in range(3)]
    KTt = [A.bf16(8 * 128).rearrange("p (k t) -> p k t", t=128) for _ in range(9)]
    Vt = [A.bf16(D) for _ in range(9)]
    sq = A.bf16(D)
    SEL = A.bf16(8 * 16).rearrange("p (k h) -> p k h", h=16)
    on16 = A.bf16(128)
    pTs = A.bf16(2 * 16).rearrange("p (h c) -> p h c", h=2)
    mT1 = A.bf16(2 * 4).rearrange("p (h c) -> p h c", h=2)
    mnew = A.bf16(2 * 4).rearrange("p (h c) -> p h c", h=2)
    attn_s = A.bf16(8 * 128).rearrange("p (k t) -> p k t", t=128)
    m1 = A.f32(2)
    qk = A.f32(2)
    c16 = A.f32(2)
    dg = A.bf16(16)
    cb = A.f32(16)
    cneg = A.f32(8)
    rs = A.f32(4)
    QT_v = QT.rearrange("(k p) t -> p k t", p=128)
    KT_v = KT.rearrange("(k p) t -> p k t", p=128)
    attnT_v = attnT.rearrange("(k p) t -> p k t", p=128)
    S.op("pool", lambda e: e.memset(SEL, 0.0), writes=[tk("SEL")])
    for kc in range(8):
        for hh in range(2):
            S.op("pool", (lambda e, kc=kc, hh=hh: e.memset(SEL[64 * hh:64 * hh + 64, kc, 2 * kc + hh:2 * kc + hh + 1], 1.0)),
                 reads=[tk("SEL")], writes=[tk("SEL")])
    S.op("pool", lambda e: e.memset(on16, 1.0), writes=[tk("on16")])
    S.op("pool", lambda e: e.memset(attn_s, 0.0), writes=[tk("attn_s")])
    S.op("pool", lambda e: e.memset(mT1, 1.0), writes=[tk("mT1")])
    S.op("pool", lambda e: e.memset(mnew, 1.0), writes=[tk("mnew")])
    for hh in range(2):
        S.op("pool", (lambda e, hh=hh: e.affine_select(out=mT1[:, hh, :], in_=mT1[:, hh, :], pattern=[[-1, 4]],
                                                       compare_op=ALU.is_ge, fill=0.0, base=0, channel_multiplier=1)),
             reads=[tk("mT1")], writes=[tk("mT1")])
        S.op("pool", (lambda e, hh=hh: e.affine_select(out=mnew[0:4, hh, :], in_=mnew[0:4, hh, :], pattern=[[1, 4]],
                                                       compare_op=ALU.is_ge, fill=0.0, base=0, channel_multiplier=-1)),
             reads=[tk("mnew")], writes=[tk("mnew")])
        S.op("dve", (lambda e, hh=hh: e.scalar_tensor_tensor(out=mnew[0:4, hh, :], in0=cx.ident[0:4, 0:4], scalar=2.0,
                                                             in1=mnew[0:4, hh, :], op0=ALU.mult, op1=ALU.add)),
             reads=[tk("mnew"), "ident"], writes=[tk("mnew")])
    S.op("sp", lambda e: e.dma_start(out=QTs, in_=QT_v[:, :, T:T + 16]), writes=[tk("QTs")], dma_tok=tk("QTs"))
    S.op("sp", lambda e: e.dma_start(out=KTs, in_=KT_v[:, :, T:T + 16]), writes=[tk("KTs")], dma_tok=tk("KTs"))

    def norms(src3, ncols, dst_col, first, stok):
        sq3 = sq[:, 0:8 * ncols].rearrange("p (k t) -> p k t", t=ncols)
        S.op("act", lambda e: e.activation(out=sq3, in_=src3, func=AF.Square), reads=[stok, tk("sq")], writes=[tk("sq")])
        for kc in range(8):
            S.op("pe", (lambda e, kc=kc: e.matmul(cx.ps[2][0:16, 0:ncols], lhsT=SEL[:, kc, :], rhs=sq3[:, kc, :],
                                                  start=(kc == 0), stop=(kc == 7))),
                 reads=[tk("sq"), tk("SEL")], writes=["ps2"])
        if first:
            S.op("dve", lambda e: e.reduce_max(out=qk[0:16, dst_col:dst_col + 1], in_=cx.ps[2][0:16, 0:ncols], axis=AX.X),
                 reads=["ps2", tk("qk")], writes=[tk("qk")])
        else:
            S.op("dve", lambda e: e.reduce_max(out=m1[0:16, 0:1], in_=cx.ps[2][0:16, 0:ncols], axis=AX.X),
                 reads=["ps2"], writes=[tk("m1")])
            S.op("dve", lambda e: e.tensor_tensor(out=qk[0:16, dst_col:dst_col + 1], in0=qk[0:16, dst_col:dst_col + 1],
                                                  in1=m1[0:16, 0:1], op=ALU.max), reads=[tk("m1"), tk("qk")], writes=[tk("qk")])

    def batch_body(b):
        cs = slice(4 * b, 4 * b + 4)
        ckb, cvb = ck[b], cv[b]
        srcs = [(ckb[1920:2048, :], cvb[1920:2048, :])]
        for t in range(4):
            srcs.append((ckb.rearrange("(m s) c -> m s c", s=4)[384:512, t, :], cvb.rearrange("(m s) c -> m s c", s=4)[384:512, t, :]))
        for t in range(4):
            srcs.append((ckb.rearrange("(m s) c -> m s c", s=16)[0:128, t, :], cvb.rearrange("(m s) c -> m s c", s=16)[0:128, t, :]))
        S.op("sp", lambda e: e.dma_start(out=Vn[0:4, :], in_=V[T + 4 * b:T + 4 * b + 4, :]), reads=[tk("Vn")], writes=[tk("Vn")],
             dma_tok=tk("Vn"))
        norms(QTs[:, :, cs], 4, 0, True, tk("QTs"))
        norms(KTs[:, :, cs], 4, 1, True, tk("KTs"))
        for i, (ks, vs) in enumerate(srcs):
            kt_ = ktok[i % 3]
            ktk = tk(f"ktok{i % 3}")
            S.op("pool", (lambda e, kt_=kt_, ks=ks: e.dma_start(out=kt_, in_=ks)), writes=[ktk], dma_tok=ktk)
            S.op("pool", (lambda e, i=i, vs=vs: e.dma_start(out=Vt[i], in_=vs)), writes=[tk(f"Vt{i}")], dma_tok=tk(f"Vt{i}"))
            bank = i % 2
            pt = cx.ps[bank][:, :].bitcast(BF16).rearrange("p (k c) -> p k c", c=128)
            for kc in range(8):
                S.op("pe", (lambda e, pt=pt, kt_=kt_, kc=kc: e.transpose(pt[:, kc, :], kt_[:, kc * 128:(kc + 1) * 128], cx.ident)),
                     reads=[ktk, "ident"], writes=[f"ps{bank}"])
            S.op("act", (lambda e, pt=pt, i=i: e.activation(out=KTt[i], in_=pt, func=AF.Copy)),
                 reads=[f"ps{bank}"], writes=[tk(f"KTt{i}")])
            norms(KTt[i], 128, 1, False, tk(f"KTt{i}"))
        S.op("dve", lambda e: e.tensor_tensor(out=c16[0:16, 0:1], in0=qk[0:16, 0:1], in1=qk[0:16, 1:2], op=ALU.mult),
             reads=[tk("qk")], writes=[tk("c16")])
        S.op("act", lambda e: e.activation(out=c16[0:16, 0:1], in_=c16[0:16, 0:1], func=AF.Sqrt, scale=(1.02 / 8) ** 2),
             reads=[tk("c16")], writes=[tk("c16")])
        S.op("dve", lambda e: e.tensor_scalar(out=dg[0:16, :], in0=cx.ident[0:16, 0:16], scalar1=c16[0:16, 0:1], scalar2=None,
                                              op0=ALU.mult), reads=[tk("c16"), "ident"], writes=[tk("dg")])
        S.op("pe", lambda e: e.matmul(cx.ps[3][:, 0:16], lhsT=on16[0:16, :], rhs=dg[0:16, :], start=True, stop=True),
             reads=[tk("on16"), tk("dg")], writes=["ps3"])
        S.op("dve", lambda e: e.tensor_copy(out=cb, in_=cx.ps[3][:, 0:16]), reads=["ps3"], writes=[tk("cb")])
        cb3 = cb.rearrange("p (k h) -> p k h", h=2)
        S.op("dve", lambda e: e.tensor_tensor(out=cneg, in0=cb3[:, :, 0], in1=cb3[:, :, 1], op=ALU.max),
             reads=[tk("cb")], writes=[tk("cneg")])
        S.op("dve", lambda e: e.tensor_scalar(out=cneg, in0=cneg, scalar1=-1.0, scalar2=None, op0=ALU.mult),
             reads=[tk("cneg")], writes=[tk("cneg")])
        ktoks = [tk(f"KTt{i}") for i in range(9)]
        vtoks = [tk(f"Vt{i}") for i in range(9)]

        def pair(kc):
            for hh in range(2):
                bank = 4 + hh
                hs = slice(64 * hh, 64 * hh + 64)
                S.op("pe", (lambda e, hs=hs, bank=bank: e.matmul(cx.ps[bank][:, 0:4], lhsT=KTt[0][hs, kc, :], rhs=QTs[hs, kc, cs],
                                                                 start=True, stop=True)),
                     reads=ktoks + [tk("QTs")], writes=[f"ps{bank}"])
                for t in range(4):
                    for base, ti in ((4, 1 + t), (8, 5 + t)):
                        S.op("pe", (lambda e, hs=hs, bank=bank, t=t, base=base, ti=ti: e.matmul(
                            cx.ps[bank][:, base + t:base + t + 1], lhsT=KTt[ti][hs, kc, :],
                            rhs=QTs[hs, kc, 4 * b + t:4 * b + t + 1], start=True, stop=True)),
                            reads=ktoks + [tk("QTs")], writes=[f"ps{bank}"])
                S.op("pe", (lambda e, hs=hs, bank=bank: e.matmul(cx.ps[bank][0:4, 12:16], lhsT=KTs[hs, kc, cs], rhs=QTs[hs, kc, cs],
                                                                 start=True, stop=True)),
                     reads=[tk("KTs"), tk("QTs")], writes=[f"ps{bank}"])
            for hh in range(2):
                bank = 4 + hh
                S.op("act", (lambda e, hh=hh, bank=bank: e.activation(out=pTs[:, hh, :], in_=cx.ps[bank][:, 0:16], func=AF.Exp,
                                                                      scale=0.125, bias=cneg[:, kc:kc + 1])),
                     reads=[f"ps{bank}", tk("cneg")], writes=[tk("pTs")])
            S.op("dve", lambda e: e.tensor_tensor(out=pTs[:, :, 0:4], in0=pTs[:, :, 0:4], in1=mT1, op=ALU.mult),
                 reads=[tk("pTs"), tk("mT1")], writes=[tk("pTs")])
            S.op("dve", lambda e: e.tensor_tensor(out=pTs[0:4, :, 12:16], in0=pTs[0:4, :, 12:16], in1=mnew[0:4, :, :], op=ALU.mult),
                 reads=[tk("pTs"), tk("mnew")], writes=[tk("pTs")])
            for hh in range(2):
                hs = slice(64 * hh, 64 * hh + 64)
                vc = slice(kc * 128 + 64 * hh, kc * 128 + 64 * hh + 64)
                for which in range(2):
                    oc0 = 4 * which
                    lw = (lambda ti, vc=vc: Vt[ti][:, vc]) if which == 0 else (lambda ti: cx.ones_bf[:, 0:64])
                    ln = Vn[0:4, vc] if which == 0 else cx.ones_bf[0:4, 0:64]
                    S.op("pe", (lambda e, hs=hs, hh=hh, oc0=oc0, lw=lw: e.matmul(
                        cx.ps[6][hs, oc0:oc0 + 4], lhsT=lw(0), rhs=pTs[:, hh, 0:4], start=True, stop=False,
                        tile_position=(0, 64 * hh), skip_group_check=True)),
                        reads=vtoks + [tk("pTs"), "ones"], writes=["ps6"])
                    for t in range(4):
                        for base, ti in ((4, 1 + t), (8, 5 + t)):
                            S.op("pe", (lambda e, hs=hs, hh=hh, oc0=oc0, lw=lw, t=t, base=base, ti=ti: e.matmul(
                                cx.ps[6][hs, oc0 + t:oc0 + t + 1], lhsT=lw(ti), rhs=pTs[:, hh, base + t:base + t + 1],
                                start=False, stop=False, tile_position=(0, 64 * hh), skip_group_check=True)),
                                reads=vtoks + [tk("pTs"), "ones"], writes=["ps6"])
                    S.op("pe", (lambda e, hs=hs, hh=hh, oc0=oc0, ln=ln: e.matmul(
                        cx.ps[6][hs, oc0:oc0 + 4], lhsT=ln, rhs=pTs[0:4, hh, 12:16], start=False, stop=True,
                        tile_position=(0, 64 * hh), skip_group_check=True)),
                        reads=[tk("Vn"), tk("pTs"), "ones"], writes=["ps6"])
            S.op("dve", lambda e: e.reciprocal(out=rs, in_=cx.ps[6][:, 4:8]), reads=["ps6"], writes=[tk("rs")])
            S.op("dve", lambda e: e.tensor_tensor(out=attn_s[:, kc, cs], in0=cx.ps[6][:, 0:4], in1=rs, op=ALU.mult),
                 reads=["ps6", tk("rs"), tk("attn_s")], writes=[tk("attn_s")])

        for kc in range(8):
            pair(kc)

    for b in range(NBS):
        batch_body(b)
    S.op("sp", lambda e: e.dma_start(out=attnT_v[:, :, T:T + 128], in_=attn_s), reads=[tk("attn_s")], dma_tok=tk("attn_s"))


T_PROMPT = int(os.environ.get("K_T", "8192"))
NCORES = 8
WEIGHT_SHAPES = {
    "norm_gains": [2, 4, 1024], "ab_w_in": [1, 1024, 1024], "ab_pool_w": [1, 4, 128, 128], "ab_pool_scale": [1, 512],
    "ab_lambda_re": [1, 32, 64], "ab_lambda_im": [1, 32, 64], "ab_log_dt": [1, 32], "ab_b_re": [1, 32, 64, 16],
    "ab_b_im": [1, 32, 64, 16], "ab_c_re": [1, 32, 16, 64], "ab_c_im": [1, 32, 16, 64], "ab_d": [1, 512],
    "ab_w_glu": [1, 512, 512], "ab_b_glu": [1, 512], "ab_w_out": [1, 1024, 1024], "c_w_qkv": [1, 1024, 3072],
    "c_w_o": [1, 1024, 1024], "ffn_w_gate": [2, 1024, FH], "ffn_w_up": [2, 1024, FH], "ffn_w_down": [2, FH, 1024],
}


def build_program(T=T_PROMPT, upto=99):
    nc = bass.Bass("TRN2", target_bir_lowering=False)
    TA = T + 128
    OWN0 = T // 2
    HALO0 = T // 4
    NOWN = T - OWN0
    din = lambda n, s: nc.dram_tensor(n, s, F32, kind="ExternalInput").ap()
    dout = lambda n, s: nc.dram_tensor(n, s, F32, kind="ExternalOutput").ap()
    x = din("x", [T, 1024])
    xs = din("xs", [128, 1024])
    invc_in = din("invc", [64])
    hflag = din("hflag", [1])
    sp_in = din("state_pool_s", [4, 15, 512])
    s5_in = din("state_s5_s", [4, 32, 64, 2])
    ck = din("cache_k_s", [4, 2048, 1024])
    cv = din("cache_v_s", [4, 2048, 1024])
    Wd = {n: din(n, s) for n, s in WEIGHT_SHAPES.items()}
    y_p = dout("y_p", [NOWN, 1024])
    y_s = dout("y_s", [128, 1024])
    pool_p = dout("pool_p", [15, 512])
    s5_p = dout("s5_p", [32, 64, 2])
    KW = 2048
    k_p = dout("k_p", [KW, 1024])
    v_p = dout("v_p", [KW, 1024])
    pool_s = dout("pool_s", [4, 15, 512])
    s5_s = dout("s5_s", [4, 32, 64, 2])
    k_s = dout("k_s", [16, 1024])
    v_s = dout("v_s", [16, 1024])
    scr = lambda n, s, dt: nc.dram_tensor(n, s, dt, kind="Internal").ap()
    H1 = scr("H1", [TA, 1024], F32)
    H2 = scr("H2", [TA, 1024], F32)
    H3 = scr("H3", [TA, 1024], F32)
    ycatT = scr("ycatT", [1024, TA], BF16)
    attnT = scr("attnT", [1024, TA], BF16)
    QT = scr("QT", [1024, TA], BF16)
    KT = scr("KT", [1024, TA], BF16)
    V = scr("V", [TA, 1024], BF16)
    cx = Ctx(nc)
    g = Wd["norm_gains"]
    Pm = dict(lam_re=Wd["ab_lambda_re"][0], lam_im=Wd["ab_lambda_im"][0], log_dt=Wd["ab_log_dt"][0],
              b_re=Wd["ab_b_re"][0], b_im=Wd["ab_b_im"][0], c_re=Wd["ab_c_re"][0], c_im=Wd["ab_c_im"][0],
              w_in=Wd["ab_w_in"][0], w_glu=Wd["ab_w_glu"][0], pool_w=Wd["ab_pool_w"][0], g0=g[0, 0, :],
              d=Wd["ab_d"][0], b_glu=Wd["ab_b_glu"][0], pool_scale=Wd["ab_pool_scale"][0])
    samp = dict(x_s=xs, state_pool=sp_in, state_s5=s5_in, pool_out=pool_s, s5_out=s5_s)
    mixer_ab_phase(cx, x, T, ycatT, Pm, dict(pool_prompt=pool_p, s5_prompt=s5_p), samp,
                   n_light=HALO0 // 512, fix_tile=OWN0 // 512, invc_dram=invc_in)
    if upto >= 2:
        tiles = [(t0, 4, x[t0:t0 + 512, :], H1[t0:t0 + 512, :], "ycatT", "x", "H1", t0) for t0 in range(HALO0, T, 512)]
        tiles.append((T, 1, xs, H1[T:TA, :], "ycatT", "xs", "H1", T))
        proj_res_phase(cx, ycatT, tiles, Wd["ab_w_out"][0], g[0, 1, :], "p0")
    if upto >= 3:
        tiles = [(H1[t0:t0 + 256, :], H2[t0:t0 + 256, :], 2, "H1", "H2", t0) for t0 in range(HALO0, T, 256)]
        tiles.append((H1[T:TA, :], H2[T:TA, :], 1, "H1", "H2", T))
        ffn_phase(cx, tiles, Wd["ffn_w_gate"][0], Wd["ffn_w_up"][0], Wd["ffn_w_down"][0], g[0, 2, :], g[0, 3, :], "f0")
    if upto >= 4:
        tiles = []
        for t0 in range(HALO0, T, 512):
            if t0 >= T - KW:
                o = t0 - (T - KW)
                tiles.append((t0, 4, k_p[o:o + 512, :], v_p[o:o + 512, :], 512))
            else:
                tiles.append((t0, 4, None, None, 0))
        tiles.append((T, 1, k_s, v_s, 16))
        qkv_phase(cx, H2, "H2", tiles, Wd["c_w_qkv"][0], g[1, 0, :], QT, KT, V)
    if upto >= 5:
        attn_phase(cx, T, QT, KT, V, attnT, st_list=list(range(OWN0 // 2048, T // 2048)), halo_end=OWN0, hflag=hflag)
        attn_sample_phase(cx, T, QT, KT, V, ck, cv, attnT)
    if upto >= 6:
        tiles = [(t0, 4, H2[t0:t0 + 512, :], H3[t0:t0 + 512, :], "attnT", "H2", "H3", t0) for t0 in range(OWN0, T, 512)]
        tiles.append((T, 1, H2[T:TA, :], H3[T:TA, :], "attnT", "H2", "H3", T))
        proj_res_phase(cx, attnT, tiles, Wd["c_w_o"][0], g[1, 1, :], "p1")
    if upto >= 7:
        tiles = [(H3[t0:t0 + 256, :], y_p[t0 - OWN0:t0 - OWN0 + 256, :], 2, "H3", "y_p", t0) for t0 in range(OWN0, T, 256)]
        tiles.append((H3[T:TA, :], y_s, 1, "H3", "y_s", T))
        ffn_phase(cx, tiles, Wd["ffn_w_gate"][1], Wd["ffn_w_up"][1], Wd["ffn_w_down"][1], g[1, 2, :], g[1, 3, :], "f1")
    cx.S.finish()
    cx.S.emit()
    return nc


def kernel(**inputs):
    T = T_PROMPT
    OWN0 = T // 2
    NOWN = T - OWN0
    f32 = lambda a: np.ascontiguousarray(np.asarray(a, dtype=np.float32))
    xp = f32(inputs["x_prompt"])
    xsm = f32(inputs["x_sample"])
    spool = f32(inputs["state_pool"])
    ss5 = f32(inputs["state_s5"])
    ckk = np.asarray(inputs["cache_k"], dtype=np.float32)
    cvv = np.asarray(inputs["cache_v"], dtype=np.float32)
    weights = {n: f32(inputs[n]) for n in WEIGHT_SHAPES}
    nb = xp.shape[0]
    SEQ = xp.shape[1]
    assert SEQ == T and nb * 2 == NCORES
    invc_first = np.zeros((4, 16), np.float32)
    invc_plain = np.zeros((4, 16), np.float32)
    for gi, w in enumerate(POOL_W):
        invc_first[gi] = 1.0 / np.minimum(np.arange(16) + 1, w)
        invc_plain[gi] = 1.0 / w
    in_maps = []
    for c in range(NCORES):
        b, half = c // 2, c % 2
        m = dict(weights)
        xl = np.zeros((T, 1024), np.float32)
        if half == 0:
            xl[OWN0:] = xp[b, 0:NOWN]
        else:
            xl[:] = xp[b]
        m["x"] = xl
        m["invc"] = (invc_first if half == 0 else invc_plain).reshape(64).copy()
        m["hflag"] = np.array([float(half)], np.float32)
        xs_pad = np.zeros((128, 1024), np.float32)
        xs_pad[:16] = xsm[4 * c:4 * c + 4].reshape(16, 1024)
        m["xs"] = xs_pad
        m["state_pool_s"] = np.ascontiguousarray(spool[0, 4 * c:4 * c + 4])
        m["state_s5_s"] = np.ascontiguousarray(ss5[0, 4 * c:4 * c + 4])
        m["cache_k_s"] = np.ascontiguousarray(ckk[0, 4 * c:4 * c + 4].reshape(4, 2048, 1024))
        m["cache_v_s"] = np.ascontiguousarray(cvv[0, 4 * c:4 * c + 4].reshape(4, 2048, 1024))
        in_maps.append(m)
    nc = build_program(T)
    res = run_bass_kernel_spmd(nc, in_maps, core_ids=list(range(NCORES)))
    R = res.results
    KW = 2048
    y_prompt = np.stack([np.concatenate([np.asarray(R[2 * b]["y_p"]), np.asarray(R[2 * b + 1]["y_p"])]) for b in range(nb)])
    y_sample = np.concatenate([np.asarray(R[c]["y_s"])[:16].reshape(4, 4, 1024) for c in range(NCORES)])
    last = lambda b: R[2 * b + 1]
    pool_prompt = np.stack([np.asarray(last(b)["pool_p"]) for b in range(nb)])[None]
    s5_prompt = np.stack([np.asarray(last(b)["s5_p"]) for b in range(nb)])[None]
    k_prompt = np.stack([np.asarray(last(b)["k_p"]).reshape(KW, 16, 64) for b in range(nb)])[None]
    v_prompt = np.stack([np.asarray(last(b)["v_p"]).reshape(KW, 16, 64) for b in range(nb)])[None]
    pool_sample = np.concatenate([np.asarray(R[c]["pool_s"]) for c in range(NCORES)])[None]
    s5_sample = np.concatenate([np.asarray(R[c]["s5_s"]) for c in range(NCORES)])[None]
    k_sample = np.concatenate([np.asarray(R[c]["k_s"]).reshape(4, 4, 16, 64) for c in range(NCORES)])[None]
    v_sample = np.concatenate([np.asarray(R[c]["v_s"]).reshape(4, 4, 16, 64) for c in range(NCORES)])[None]
    outs = (y_prompt, y_sample, pool_prompt, s5_prompt, k_prompt, v_prompt, pool_sample, s5_sample, k_sample, v_sample)
    return tuple(np.ascontiguousarray(o, dtype=np.float32) for o in outs)
```

```python
import os
import numpy as np
import concourse.bass as bass
import concourse.mybir as mybir
from concourse.bass_utils import run_bass_kernel_spmd

F32 = mybir.dt.float32
BF16 = mybir.dt.bfloat16
AF = mybir.ActivationFunctionType
ALU = mybir.AluOpType
AX = mybir.AxisListType

D = 1024
FH = 2816
NFC = FH // 128
EPS = 1e-6
SEM_CH = 20000
K_STAGE = float(os.environ.get('K_STAGE', '99'))


class _Op:
    __slots__ = ("eng", "fn", "dma_tok", "waits", "signal", "seq", "dbg")


class Sched:
    ENGS = ("pe", "act", "dve", "pool", "sp")

    def __init__(self, nc, sync_same=True):
        self.nc = nc
        self.streams = {e: [] for e in self.ENGS}
        self.lastw = {}
        self.readers = {}
        self.slot_cum = []
        self.tok_slot = {}
        self.phase_used = 0
        self.sync_same = sync_same
        self.pending = {}

    def op(self, eng, fn, reads=(), writes=(), dma_tok=None):
        o = _Op()
        if dma_tok is not None:
            if dma_tok not in self.tok_slot:
                if self.phase_used >= len(self.slot_cum):
                    self.slot_cum.append(0)
                self.tok_slot[dma_tok] = self.phase_used
                self.phase_used += 1
            dma_tok = self.tok_slot[dma_tok]
        writes = list(writes) + [r for r in reads if r.startswith("ps") and r not in writes]
        o.dbg = (tuple(reads), tuple(writes))
        o.eng, o.fn, o.dma_tok, o.waits, o.signal, o.seq = eng, fn, dma_tok, self.pending.pop(eng, []), False, 0
        deps = []
        seen = set()

        def add(d):
            if d is not None and id(d) not in seen:
                seen.add(id(d))
                deps.append(d)

        for r in reads:
            add(self.lastw.get(r))
        for w in writes:
            add(self.lastw.get(w))
            for rd in self.readers.get(w, ()):
                add(rd)
        for d in deps:
            if d.dma_tok is not None:
                o.waits.append(("dma", d.dma_tok, self.slot_cum[d.dma_tok]))
            else:
                if d.eng == eng and (eng == "pe" or not self.sync_same):
                    continue
                d.signal = True
                o.waits.append(("eng", d))
        if dma_tok is not None:
            self.slot_cum[dma_tok] += 1
        for r in reads:
            self.readers.setdefault(r, []).append(o)
        for w in writes:
            self.lastw[w] = o
            self.readers[w] = []
        self.streams[eng].append(o)
        return o

    def barrier(self):
        lasts = {e: (self.streams[e][-1] if self.streams[e] else None) for e in self.ENGS}
        toks = dict(enumerate(self.slot_cum))
        self.tok_slot = {}
        self.phase_used = 0
        for e in self.ENGS:
            pend = self.pending.setdefault(e, [])
            for e2, d in lasts.items():
                if d is None or e2 == e:
                    continue
                if d.dma_tok is None:
                    d.signal = True
                    pend.append(("eng", d))
            for t, c in toks.items():
                pend.append(("dma", t, c))
        self.lastw = {}
        self.readers = {}

    def dump(self, path):
        seqs = {}
        for eng, ops in self.streams.items():
            c = 0
            for o in ops:
                if o.dma_tok is None and o.signal:
                    c += 1
                    seqs[id(o)] = c
        with open(path, "w") as f:
            for eng, ops in self.streams.items():
                f.write(f"==== {eng}\n")
                for i, o in enumerate(ops):
                    ws = []
                    for w in o.waits:
                        if w[0] == "dma":
                            ws.append(f"D[{w[1]}]>={w[2]}")
                        else:
                            ws.append(f"{w[1].eng}>={seqs.get(id(w[1]))}")
                    f.write(f"{i}: sig={seqs.get(id(o))} dma={o.dma_tok} r/w={getattr(o, 'dbg', None)} waits={ws}\n")

    def finish(self):
        o = _Op()
        o.eng, o.fn, o.dma_tok, o.waits, o.signal, o.seq = "sp", (lambda en: en.nop()), None, [], False, 0
        for t, c in enumerate(self.slot_cum):
            o.waits.append(("dma", t, c))
        self.streams["sp"].append(o)

    def emit(self):
        nc = self.nc
        nsem = {}
        for eng, ops in self.streams.items():
            c = 0
            for o in ops:
                if o.dma_tok is None and o.signal:
                    c += 1
                    o.seq = c
            nsem[eng] = (c + SEM_CH - 1) // SEM_CH
        eng_sem = {eng: [nc.alloc_semaphore(f"s_{eng}{j}") for j in range(max(1, nsem[eng]))]
                   for eng in self.ENGS}
        dma_sem = {i: nc.alloc_semaphore(f"d_{i}") for i in range(len(self.slot_cum))}
        print("[sched] dma sems", len(self.slot_cum), "max count", max(self.slot_cum) if self.slot_cum else 0,
              "ops", {e: len(v) for e, v in self.streams.items()})
        for tok, c in enumerate(self.slot_cum):
            assert c * 16 < 60000, (tok, c)
        streams = self.streams

        def mk(eng):
            def body(e):
                waited = {}
                for o in streams[eng]:
                    for w in o.waits:
                        if w[0] == "dma":
                            key, val, sem = ("d", w[1]), 16 * w[2], dma_sem[w[1]]
                        else:
                            d = w[1]
                            j = (d.seq - 1) // SEM_CH
                            key, val, sem = ("e", d.eng, j), (d.seq - 1) % SEM_CH + 1, eng_sem[d.eng][j]
                        if waited.get(key, 0) >= val:
                            continue
                        e.wait_ge(sem, val)
                        waited[key] = val
                    ins = o.fn(e)
                    if o.dma_tok is not None:
                        ins.then_inc(dma_sem[o.dma_tok], 16)
                    elif o.signal:
                        j = (o.seq - 1) // SEM_CH
                        ins.then_inc(eng_sem[eng][j], 1)
            return body

        with nc.Block() as block:
            block.sync(mk("sp"))
            block.scalar(mk("act"))
            block.vector(mk("dve"))
            block.gpsimd(mk("pool"))
            block.tensor(mk("pe"))


class Arena:
    def __init__(self, arena_ap, nwords):
        self.a = arena_ap
        self.n = nwords
        self.off = 0
        self.mark = 0

    def f32(self, n):
        assert self.off + n <= self.n, ("arena overflow", self.off, n, self.n)
        ap = self.a[:, self.off:self.off + n]
        self.off += n
        return ap

    def bf16(self, n):
        w = (n + 1) // 2
        assert self.off + w <= self.n, ("arena overflow", self.off, w, self.n)
        ap = self.a[:, self.off:self.off + w].bitcast(BF16)
        self.off += w
        return ap

    def set_mark(self):
        self.mark = self.off

    def reset(self):
        self.off = self.mark


class Ctx:
    def __init__(self, nc):
        self.nc = nc
        self.S = Sched(nc, sync_same=(os.environ.get('K_SYNC_SAME', '1') == '1'))
        nwords = (nc.sbuf_bytes_remaining - 2048) // 4
        self.arena_t = nc.alloc_sbuf_tensor("arena", [128, nwords], F32)
        self.A = Arena(self.arena_t[:, :], nwords)
        self.ps = [nc.alloc_psum_tensor(f"psb{i}", [128, 512], F32) for i in range(8)]
        self.uid = 0
        A = self.A
        self.ident = A.bf16(128)
        self.eps_t = A.f32(1)
        self.ones_bf = A.bf16(128)
        self.qkmax = A.f32(2)
        S = self.S
        S.op("pool", lambda e: e.memset(self.ident, 0.0), writes=["ident"])
        S.op("pool", lambda e: e.affine_select(out=self.ident, in_=self.ident, pattern=[[-1, 128]],
                                               compare_op=ALU.not_equal, fill=1.0, base=0,
                                               channel_multiplier=1), reads=["ident"], writes=["ident"])
        S.op("pool", lambda e: e.memset(self.eps_t, EPS), writes=["eps"])
        S.op("pool", lambda e: e.memset(self.ones_bf, 1.0), writes=["ones"])
        A.set_mark()

    def tok(self, name):
        self.uid += 1
        return f"{name}#{self.uid}"


def load_weight_bf16(cx, dst3, w_dram, kchunks, tokname):
    S = cx.S
    for kc in range(kchunks):
        S.op("pool", (lambda e, kc=kc: e.dma_start(out=dst3[:, kc, :], in_=w_dram[kc * 128:(kc + 1) * 128, :])),
             writes=[tokname], dma_tok=tokname)


def load_bcast_row(cx, dst, row_dram, tokname, eng="sp"):
    cx.S.op(eng, lambda e: e.dma_start(out=dst, in_=row_dram.partition_broadcast(128)),
            writes=[tokname], dma_tok=tokname)


def rms_prenorm_T(cx, h3, nsub, g_b, g_tok, ssq, rstd, junk, xn3, xnT3, in_toks, names, psbanks):
    S = cx.S
    for s in range(nsub):
        S.op("act", (lambda e, s=s: e.activation(out=junk, in_=h3[:, s, :], func=AF.Square,
                                                  accum_out=ssq[:, s:s + 1])),
             reads=in_toks, writes=[names["junk"], names["ssq"]])
    S.op("act", lambda e: e.activation(out=rstd[:, 0:nsub], in_=ssq[:, 0:nsub], func=AF.Sqrt,
                                       scale=1.0 / D, bias=cx.eps_t), reads=[names["ssq"], "eps"],
         writes=[names["rstd"]])
    S.op("dve", lambda e: e.reciprocal(out=rstd[:, 0:nsub], in_=rstd[:, 0:nsub]),
         reads=[names["rstd"]], writes=[names["rstd"]])
    for s in range(nsub):
        S.op("dve", (lambda e, s=s: e.scalar_tensor_tensor(out=xn3[:, s, :], in0=h3[:, s, :],
                                                            scalar=rstd[:, s:s + 1], in1=g_b,
                                                            op0=ALU.mult, op1=ALU.mult)),
             reads=in_toks + [names["rstd"], g_tok], writes=[names["xn"] + str(s)])
    for s in range(nsub):
        bank = psbanks[s % len(psbanks)]
        pt = cx.ps[bank][:, :].bitcast(BF16).rearrange("p (k c) -> p k c", c=128)
        for kc in range(8):
            S.op("pe", (lambda e, s=s, kc=kc, pt=pt: e.transpose(pt[:, kc, :], xn3[:, s, kc * 128:(kc + 1) * 128],
                                                                  cx.ident)),
                 reads=[names["xn"] + str(s), "ident"], writes=[f"ps{bank}"])
        eng = "act"
        if eng == "act":
            S.op("act", (lambda e, s=s, pt=pt: e.activation(out=xnT3[:, :, s * 128:(s + 1) * 128], in_=pt,
                                                            func=AF.Copy)),
                 reads=[f"ps{bank}"], writes=[names["xnT"]])
        else:
            S.op("dve", (lambda e, s=s, pt=pt: e.tensor_copy(out=xnT3[:, :, s * 128:(s + 1) * 128], in_=pt)),
                 reads=[f"ps{bank}"], writes=[names["xnT"]])


def post_norm_residual(cx, ps_halves, h_in, g_b, g_tok, ssq2, rstd2, junk, tmp, h_out, in_toks, names):
    S = cx.S
    for i, (bank, ap) in enumerate(ps_halves):
        S.op("act", (lambda e, i=i, ap=ap: e.activation(out=junk[:, 0:512], in_=ap, func=AF.Square,
                                                        accum_out=ssq2[:, i:i + 1])),
             reads=[f"ps{bank}"], writes=[names["junk"], names["ssq2"]])
    S.op("dve", lambda e: e.tensor_tensor(out=ssq2[:, 2:3], in0=ssq2[:, 0:1], in1=ssq2[:, 1:2], op=ALU.add),
         reads=[names["ssq2"]], writes=[names["ssq2"]])
    S.op("act", lambda e: e.activation(out=rstd2, in_=ssq2[:, 2:3], func=AF.Sqrt, scale=1.0 / D, bias=cx.eps_t),
         reads=[names["ssq2"], "eps"], writes=[names["rstd2"]])
    S.op("dve", lambda e: e.reciprocal(out=rstd2, in_=rstd2), reads=[names["rstd2"]], writes=[names["rstd2"]])
    for i, (bank, ap) in enumerate(ps_halves):
        S.op("dve", (lambda e, i=i, ap=ap: e.scalar_tensor_tensor(out=tmp[:, i * 512:(i + 1) * 512], in0=ap,
                                                                  scalar=rstd2, in1=g_b[:, i * 512:(i + 1) * 512],
                                                                  op0=ALU.mult, op1=ALU.mult)),
             reads=[f"ps{bank}", names["rstd2"], g_tok], writes=[names["tmp"]])
    S.op("pool", lambda e: e.tensor_tensor(out=h_out, in0=tmp, in1=h_in, op=ALU.add),
         reads=[names["tmp"]] + in_toks, writes=[names["hout"]])


def ffn_phase(cx, tiles, w_gate, w_up, w_down, g_pre, g_post, pfx):
    S, A = cx.S, cx.A
    S.barrier()
    A.reset()
    wg = A.bf16(8 * FH).rearrange("p (k f) -> p k f", f=FH)
    wu = A.bf16(8 * FH).rearrange("p (k f) -> p k f", f=FH)
    wd = A.bf16(NFC * D).rearrange("p (k f) -> p k f", f=D)
    gpre_b = A.f32(D)
    gpost_b = A.f32(D)
    load_weight_bf16(cx, wg, w_gate, 8, pfx + "wg")
    load_weight_bf16(cx, wu, w_up, 8, pfx + "wu")
    load_weight_bf16(cx, wd, w_down, NFC, pfx + "wd")
    load_bcast_row(cx, gpre_b, g_pre, pfx + "gpre")
    load_bcast_row(cx, gpost_b, g_post, pfx + "gpost")
    NB = 2
    NS = max(t[2] for t in tiles)
    TT = NS * 128
    hbuf = [A.f32(NS * D).rearrange("p (s d) -> p s d", d=D) for _ in range(NB)]
    obuf = [A.f32(D) for _ in range(NB)]
    xn = A.bf16(NS * D).rearrange("p (s d) -> p s d", d=D)
    xnT = [A.bf16(8 * TT).rearrange("p (k t) -> p k t", t=TT) for _ in range(NB)]
    hid = A.bf16(NFC * TT).rearrange("p (k t) -> p k t", t=TT)
    sg = [A.f32(TT) for _ in range(2)]
    junk = A.bf16(D)
    tmp = A.f32(D)
    ssq = A.f32(4)
    rstd = A.f32(4)
    ssq2 = A.f32(4)
    rstd2 = A.f32(1)
    osc_box = [0]

    def st_load(ti, src, dst, nsub, src_name, dst_name, row0):
        b = ti % NB
        h3 = hbuf[b]
        htok = f"{pfx}h{b}"
        S.op("sp", (lambda e, h3=h3, src=src, nsub=nsub: e.dma_start(
            out=h3[:, 0:nsub, :], in_=src.rearrange("(s p) d -> p s d", p=128))),
            reads=[f"{src_name}@{row0 + 128 * s_}" for s_ in range(nsub)], writes=[htok], dma_tok=htok)

    def st_prenorm(ti, src, dst, nsub, src_name, dst_name, row0):
        b = ti % NB
        names = dict(junk=pfx + "junk", ssq=pfx + "ssq", rstd=pfx + "rstd", xn=pfx + "xn", xnT=f"{pfx}xnT{b}")
        rms_prenorm_T(cx, hbuf[b], nsub, gpre_b, pfx + "gpre", ssq, rstd, junk, xn, xnT[b], [f"{pfx}h{b}"], names, [0, 1])

    def st_gateup(ti, src, dst, nsub, src_name, dst_name, row0):
        b = ti % NB
        nt = nsub * 128
        xt = xnT[b]
        xtok = f"{pfx}xnT{b}"
        for fc in range(NFC):
            bg = 2 + (fc % 2)
            bu = 4 + (fc % 2)
            for kc in range(8):
                S.op("pe", (lambda e, fc=fc, kc=kc, bg=bg, xt=xt: e.matmul(
                    cx.ps[bg][:, 0:nt], lhsT=wg[:, kc, fc * 128:(fc + 1) * 128], rhs=xt[:, kc, 0:nt],
                    start=(kc == 0), stop=(kc == 7))),
                    reads=[xtok, pfx + "wg"], writes=[f"ps{bg}"])
            for kc in range(8):
                S.op("pe", (lambda e, fc=fc, kc=kc, bu=bu, xt=xt: e.matmul(
                    cx.ps[bu][:, 0:nt], lhsT=wu[:, kc, fc * 128:(fc + 1) * 128], rhs=xt[:, kc, 0:nt],
                    start=(kc == 0), stop=(kc == 7))),
                    reads=[xtok, pfx + "wu"], writes=[f"ps{bu}"])
            sgb = sg[fc % 2]
            S.op("act", (lambda e, bg=bg, sgb=sgb: e.activation(out=sgb[:, 0:nt], in_=cx.ps[bg][:, 0:nt],
                                                                func=AF.Silu)),
                 reads=[f"ps{bg}"], writes=[f"{pfx}sg{fc % 2}"])
            S.op("dve", (lambda e, fc=fc, bu=bu, sgb=sgb: e.tensor_tensor(
                out=hid[:, fc, 0:nt], in0=cx.ps[bu][:, 0:nt], in1=sgb[:, 0:nt], op=ALU.mult)),
                reads=[f"ps{bu}", f"{pfx}sg{fc % 2}"], writes=[pfx + "hid"])

    def st_down(ti, src, dst, nsub, src_name, dst_name, row0):
        b = ti % NB
        h3 = hbuf[b]
        htok = f"{pfx}h{b}"
        for s in range(nsub):
            halves = []
            for hf in range(2):
                bank = (6 + hf) if s % 2 == 0 else hf
                for fc in range(NFC):
                    S.op("pe", (lambda e, s=s, hf=hf, fc=fc, bank=bank: e.matmul(
                        cx.ps[bank][:, :], lhsT=hid[:, fc, s * 128:(s + 1) * 128],
                        rhs=wd[:, fc, hf * 512:(hf + 1) * 512], start=(fc == 0), stop=(fc == NFC - 1))),
                        reads=[pfx + "hid", pfx + "wd"], writes=[f"ps{bank}"])
                halves.append((bank, cx.ps[bank][:, :]))
            ob = obuf[osc_box[0] % NB]
            otok = f"{pfx}o{osc_box[0] % NB}"
            osc_box[0] += 1
            n2 = dict(junk=pfx + "junk", ssq2=pfx + "ssq2", rstd2=pfx + "rstd2", tmp=pfx + "tmp", hout=otok)
            post_norm_residual(cx, halves, h3[:, s, :], gpost_b, pfx + "gpost", ssq2, rstd2, junk, tmp,
                               ob, [htok], n2)
            S.op("pool", (lambda e, ob=ob, dst=dst, s=s: e.dma_start(out=dst[s * 128:(s + 1) * 128, :], in_=ob)),
                 reads=[otok], writes=[f"{dst_name}@{row0 + 128 * s}"], dma_tok=otok)

    st_load(0, *tiles[0])
    st_prenorm(0, *tiles[0])
    for ti, t in enumerate(tiles):
        if ti + 1 < len(tiles):
            st_load(ti + 1, *tiles[ti + 1])
        st_gateup(ti, *t)
        if ti + 1 < len(tiles):
            st_prenorm(ti + 1, *tiles[ti + 1])
        st_down(ti, *t)


class Buf:
    def __init__(self, ap, tok):
        self.ap = ap
        self.tok = tok


def newbuf(cx, n, name, dt=F32):
    ap = cx.A.f32(n) if dt == F32 else cx.A.bf16(n)
    return Buf(ap, cx.tok(name))


def ew(cx, op, out, a, b, o_ap=None, a_ap=None, b_ap=None, eng="dve"):
    o_ap = out.ap if o_ap is None else o_ap
    a_ap = a.ap if a_ap is None else a_ap
    b_ap = b.ap if b_ap is None else b_ap
    cx.S.op(eng, lambda e: e.tensor_tensor(out=o_ap, in0=a_ap, in1=b_ap, op=op),
            reads=[a.tok, b.tok, out.tok], writes=[out.tok])


def ews(cx, out, a, s1, op0, s2=None, op1=None, o_ap=None, a_ap=None, eng="dve"):
    o_ap = out.ap if o_ap is None else o_ap
    a_ap = a.ap if a_ap is None else a_ap
    if op1 is None:
        cx.S.op(eng, lambda e: e.tensor_scalar(out=o_ap, in0=a_ap, scalar1=s1, scalar2=None, op0=op0),
                reads=[a.tok, out.tok], writes=[out.tok])
    else:
        cx.S.op(eng, lambda e: e.tensor_scalar(out=o_ap, in0=a_ap, scalar1=s1, scalar2=s2, op0=op0, op1=op1),
                reads=[a.tok, out.tok], writes=[out.tok])


def act(cx, out, a, func, scale=1.0, bias=None, o_ap=None, a_ap=None, extra_reads=()):
    o_ap = out.ap if o_ap is None else o_ap
    a_ap = a.ap if a_ap is None else a_ap
    if bias is None:
        cx.S.op("act", lambda e: e.activation(out=o_ap, in_=a_ap, func=func, scale=scale),
                reads=[a.tok, out.tok] + list(extra_reads), writes=[out.tok])
    else:
        cx.S.op("act", lambda e: e.activation(out=o_ap, in_=a_ap, func=func, scale=scale, bias=bias),
                reads=[a.tok, out.tok] + list(extra_reads), writes=[out.tok])


def cmul(cx, outr, outi, ar, ai, br, bi, t1, t2, shape3=None, bcast=None,
         or_ap=None, oi_ap=None, ar_ap=None, ai_ap=None, br_ap=None, bi_ap=None):
    or_ap = outr.ap if or_ap is None else or_ap
    oi_ap = outi.ap if oi_ap is None else oi_ap
    ar_ap = ar.ap if ar_ap is None else ar_ap
    ai_ap = ai.ap if ai_ap is None else ai_ap
    br_ap = br.ap if br_ap is None else br_ap
    bi_ap = bi.ap if bi_ap is None else bi_ap
    t1_ap, t2_ap = t1.ap, t2.ap
    if shape3 is not None:
        t1_ap = t1.ap[:, 0:shape3[0] * shape3[1]].rearrange("p (a b) -> p a b", b=shape3[1])
        t2_ap = t2.ap[:, 0:shape3[0] * shape3[1]].rearrange("p (a b) -> p a b", b=shape3[1])
    else:
        n = or_ap.shape[-1] if len(or_ap.shape) == 2 else None
        if n is not None:
            t1_ap = t1.ap[:, 0:n]
            t2_ap = t2.ap[:, 0:n]
    ew(cx, ALU.mult, t1, ar, br, o_ap=t1_ap, a_ap=ar_ap, b_ap=br_ap)
    ew(cx, ALU.mult, t2, ai, bi, o_ap=t2_ap, a_ap=ai_ap, b_ap=bi_ap)
    ew(cx, ALU.subtract, outr, t1, t2, o_ap=or_ap, a_ap=t1_ap, b_ap=t2_ap)
    ew(cx, ALU.mult, t1, ar, bi, o_ap=t1_ap, a_ap=ar_ap, b_ap=bi_ap)
    ew(cx, ALU.mult, t2, ai, br, o_ap=t2_ap, a_ap=ai_ap, b_ap=br_ap)
    ew(cx, ALU.add, outi, t1, t2, o_ap=oi_ap, a_ap=t1_ap, b_ap=t2_ap)


def s5_precompute(cx, Pm, NBLK, NSAMP):
    S, A = cx.S, cx.A
    import math
    W = {}
    W["KB"] = newbuf(cx, 4 * 8 * 128, "KB", BF16)
    W["XW"] = newbuf(cx, 4 * 8 * 2 * 128, "XW", BF16)
    W["XWs"] = newbuf(cx, 4 * 4 * 2 * 128, "XWs", BF16)
    W["CW"] = newbuf(cx, 16 * 8 * 2 * 32, "CW", BF16)
    W["ETr"] = newbuf(cx, 16 * NBLK, "ETr")
    W["ETi"] = newbuf(cx, 16 * NBLK, "ETi")
    W["RHO"] = newbuf(cx, 16 * NBLK, "RHO")
    W["L8r"] = newbuf(cx, 16, "L8r")
    W["L8i"] = newbuf(cx, 16, "L8i")
    W["L4r"] = newbuf(cx, 16, "L4r")
    W["L4i"] = newbuf(cx, 16, "L4i")
    keep = A.off
    KB4 = W["KB"].ap.rearrange("p (k m c) -> p k m c", k=4, m=8)
    XW5 = W["XW"].ap.rearrange("p (k s r c) -> p k s r c", k=4, s=8, r=2)
    XWs5 = W["XWs"].ap.rearrange("p (k s r c) -> p k s r c", k=4, s=4, r=2)
    CW5 = W["CW"].ap.rearrange("p (q t r c) -> p q t r c", q=16, t=8, r=2)
    nb = lambda n, name, dt=F32: newbuf(cx, n, name, dt)
    LR, LI, LDT = nb(16, "LR"), nb(16, "LI"), nb(16, "LDT")
    DTe, Ar, PH, MAG = nb(16, "DTe"), nb(16, "Ar"), nb(16, "PH"), nb(16, "MAG")
    hp = nb(1, "halfpi")
    cc = [nb(16, "cc0"), nb(16, "cc1")]
    ss = [nb(16, "ss0"), nb(16, "ss1")]
    t1, t2 = nb(512, "t1"), nb(512, "t2")
    PWr, PWi = nb(9 * 16, "PWr"), nb(9 * 16, "PWi")
    gr, gi, den, ta, tb = nb(16, "gr"), nb(16, "gi"), nb(16, "den"), nb(16, "ta"), nb(16, "tb")
    BR, BI, CR, CI = nb(256, "BR"), nb(256, "BI"), nb(256, "CR"), nb(256, "CI")
    BBr, BBi = nb(256, "BBr"), nb(256, "BBi")
    Wr, Wi = nb(256, "Wr"), nb(256, "Wi")
    WZ0 = nb(16 * 2 * 32, "WZ0", BF16)
    BZ = nb(16 * 2 * 32, "BZ", BF16)
    ZX = [nb(16 * 2 * 32, f"ZX{i}", BF16) for i in range(2)]
    Ekr, Eki, Ek2r, Ek2i = nb(16, "Ekr"), nb(16, "Eki"), nb(16, "Ek2r"), nb(16, "Ek2i")
    R8 = nb(16, "R8")

    def v3(b, n=16):
        return b.ap.rearrange("p (q j) -> p q j", j=n)

    def bc(ap16):
        return ap16.rearrange("p (q o) -> p q o", o=1).to_broadcast([128, 16, 16])

    def dma(out_ap, in_ap, buf, slow=False):
        if slow:
            S.op("sp", lambda e: e.dma_start(out=out_ap, in_=in_ap, allow_slow_non_contiguous=True),
                 writes=[buf.tok], dma_tok=buf.tok)
        else:
            S.op("sp", lambda e: e.dma_start(out=out_ap, in_=in_ap), writes=[buf.tok], dma_tok=buf.tok)

    for gl in range(2):
        ps_ = slice(gl * 64, (gl + 1) * 64)
        dma(LR.ap[ps_, :], Pm["lam_re"].rearrange("(q gl) p -> gl p q", gl=2)[gl], LR, slow=True)
        dma(LI.ap[ps_, :], Pm["lam_im"].rearrange("(q gl) p -> gl p q", gl=2)[gl], LI, slow=True)
        dma(LDT.ap[ps_, :], Pm["log_dt"].rearrange("(q gl) -> gl q", gl=2)[gl].partition_broadcast(64), LDT, slow=True)
        dma(v3(BR)[ps_, :, :], Pm["b_re"].rearrange("(q gl) p j -> gl p q j", gl=2)[gl], BR)
        dma(v3(BI)[ps_, :, :], Pm["b_im"].rearrange("(q gl) p j -> gl p q j", gl=2)[gl], BI)
        for q in range(16):
            dma(v3(CR)[ps_, q, :], Pm["c_re"][2 * q + gl].rearrange("i p -> p i"), CR, slow=True)
            dma(v3(CI)[ps_, q, :], Pm["c_im"][2 * q + gl].rearrange("i p -> p i"), CI, slow=True)
    S.op("pool", lambda e: e.memset(hp.ap, math.pi / 2), writes=[hp.tok])
    act(cx, DTe, LDT, AF.Exp)
    ew(cx, ALU.mult, Ar, LR, DTe)
    ew(cx, ALU.mult, PH, LI, DTe)
    act(cx, MAG, Ar, AF.Exp)
    act(cx, ss[0], PH, AF.Sin, scale=1.0 / 64)
    act(cx, cc[0], PH, AF.Sin, scale=1.0 / 64, bias=hp.ap, extra_reads=[hp.tok])
    cur = 0
    for it in range(6):
        nx = 1 - cur
        ew(cx, ALU.mult, t1, cc[cur], cc[cur], o_ap=t1.ap[:, 0:16])
        ew(cx, ALU.mult, t2, ss[cur], ss[cur], o_ap=t2.ap[:, 0:16])
        S.op("dve", (lambda e, cur=cur, nx=nx: e.scalar_tensor_tensor(
            out=ss[nx].ap, in0=cc[cur].ap, scalar=2.0, in1=ss[cur].ap, op0=ALU.mult, op1=ALU.mult)),
            reads=[cc[cur].tok, ss[cur].tok, ss[nx].tok], writes=[ss[nx].tok])
        ew(cx, ALU.subtract, cc[nx], t1, t2, a_ap=t1.ap[:, 0:16], b_ap=t2.ap[:, 0:16])
        cur = nx
    PW3r = PWr.ap.rearrange("p (m q) -> p m q", q=16)
    PW3i = PWi.ap.rearrange("p (m q) -> p m q", q=16)
    S.op("pool", lambda e: e.memset(PW3r[:, 0, :], 1.0), writes=[PWr.tok])
    S.op("pool", lambda e: e.memset(PW3i[:, 0, :], 0.0), writes=[PWi.tok])
    ew(cx, ALU.mult, PWr, MAG, cc[cur], o_ap=PW3r[:, 1, :])
    ew(cx, ALU.mult, PWi, MAG, ss[cur], o_ap=PW3i[:, 1, :])
    for m in range(2, 9):
        cmul(cx, PWr, PWi, PWr, PWi, PWr, PWi, t1, t2,
             or_ap=PW3r[:, m, :], oi_ap=PW3i[:, m, :], ar_ap=PW3r[:, m - 1, :], ai_ap=PW3i[:, m - 1, :],
             br_ap=PW3r[:, 1, :], bi_ap=PW3i[:, 1, :])
    ews(cx, ta, PWr, -1.0, ALU.add, a_ap=PW3r[:, 1, :])
    ew(cx, ALU.mult, den, LR, LR)
    ew(cx, ALU.mult, tb, LI, LI)
    ew(cx, ALU.add, den, den, tb)
    S.op("dve", lambda e: e.reciprocal(out=den.ap, in_=den.ap), reads=[den.tok], writes=[den.tok])
    ew(cx, ALU.mult, gr, ta, LR)
    ew(cx, ALU.mult, tb, PWi, LI, a_ap=PW3i[:, 1, :])
    ew(cx, ALU.add, gr, gr, tb)
    ew(cx, ALU.mult, gr, gr, den)
    ew(cx, ALU.mult, gi, PWi, LR, a_ap=PW3i[:, 1, :])
    ew(cx, ALU.mult, tb, ta, LI)
    ew(cx, ALU.subtract, gi, gi, tb)
    ew(cx, ALU.mult, gi, gi, den)
    s3 = (16, 16)
    cmul(cx, BBr, BBi, BR, BI, gr, gi, t1, t2, shape3=s3, or_ap=v3(BBr), oi_ap=v3(BBi),
         ar_ap=v3(BR), ai_ap=v3(BI), br_ap=bc(gr.ap), bi_ap=bc(gi.ap))

    def expand(dst_ap3, dst_buf, src_buf, neg=False):
        for gl in range(2):
            ps_ = slice(gl * 64, (gl + 1) * 64)
            o_ap = dst_ap3[ps_, :, gl * 16:(gl + 1) * 16]
            i_ap = v3(src_buf)[ps_, :, :]
            S.op("act", (lambda e, o_ap=o_ap, i_ap=i_ap: e.activation(out=o_ap, in_=i_ap, func=AF.Copy,
                                                                    scale=(-1.0 if neg else 1.0))),
                 reads=[src_buf.tok, dst_buf.tok], writes=[dst_buf.tok])

    for b_ in (WZ0, BZ, ZX[0], ZX[1], W["CW"]):
        S.op("pool", (lambda e, b_=b_: e.memset(b_.ap, 0.0)), writes=[b_.tok])
    WZ04 = WZ0.ap.rearrange("p (q r c) -> p q r c", q=16, r=2)
    BZ4 = BZ.ap.rearrange("p (q r c) -> p q r c", q=16, r=2)
    expand(BZ4[:, :, 0, :], BZ, BBr)
    expand(BZ4[:, :, 1, :], BZ, BBi)
    expand(WZ04[:, :, 0, :], WZ0, CR)
    expand(WZ04[:, :, 1, :], WZ0, CI, neg=True)
    for m in range(1, 9):
        cmul(cx, Wr, Wi, CR, CI, PWr, PWi, t1, t2, shape3=s3, or_ap=v3(Wr), oi_ap=v3(Wi),
             ar_ap=v3(CR), ai_ap=v3(CI), br_ap=bc(PW3r[:, m, :]), bi_ap=bc(PW3i[:, m, :]))
        expand(CW5[:, :, m - 1, 0, :], W["CW"], Wr)
        expand(CW5[:, :, m - 1, 1, :], W["CW"], Wi, neg=True)
    for k in range(4):
        for m in range(8):
            bank = (k * 8 + m) % 2
            pk = cx.ps[bank][:, 0:128]
            S.op("dve", (lambda e, pk=pk: e.memset(pk, 0.0)), writes=[f"ps{bank}"])
            for ql in range(4):
                q = 4 * k + ql
                for ri in range(2):
                    rhs = WZ04[:, q, ri, :] if m == 0 else CW5[:, q, m - 1, ri, :]
                    rtok = WZ0.tok if m == 0 else W["CW"].tok
                    S.op("pe", (lambda e, pk=pk, ql=ql, q=q, ri=ri, rhs=rhs: e.matmul(
                        pk[32 * ql:32 * ql + 32, 32 * ql:32 * ql + 32], lhsT=BZ4[:, q, ri, :], rhs=rhs,
                        start=(ri == 0), stop=(ri == 1), tile_position=(0, 32 * ql), skip_group_check=True)),
                        reads=[BZ.tok, rtok], writes=[f"ps{bank}"])
            S.op("act", (lambda e, pk=pk, k=k, m=m: e.activation(out=KB4[:, k, m, :], in_=pk, func=AF.Copy)),
                 reads=[f"ps{bank}"], writes=[W["KB"].tok])

    def xweights(dst5, dst_buf, nsig, pw_of_sigma):
        for sg_ in range(nsig):
            m = pw_of_sigma(sg_)
            z = ZX[sg_ % 2]
            z4 = z.ap.rearrange("p (r q c) -> p r q c", q=16, r=2)
            cmul(cx, Wr, Wi, BBr, BBi, PWr, PWi, t1, t2, shape3=s3, or_ap=v3(Wr), oi_ap=v3(Wi),
                 ar_ap=v3(BBr), ai_ap=v3(BBi), br_ap=bc(PW3r[:, m, :]), bi_ap=bc(PW3i[:, m, :]))
            expand(z4[:, 0, :, :], z, Wr)
            expand(z4[:, 1, :, :], z, Wi)
            bank = 2 + (sg_ % 2)
            pt = cx.ps[bank][:, :].bitcast(BF16).rearrange("p (k r c) -> p k r c", k=4, r=2)
            for k in range(4):
                for ri in range(2):
                    S.op("pe", (lambda e, pt=pt, k=k, ri=ri, z4=z4: e.transpose(
                        pt[:, k, ri, :], z4[:, ri, 4 * k:4 * k + 4, :], cx.ident)),
                        reads=[z.tok, "ident"], writes=[f"ps{bank}"])
            S.op("act", (lambda e, pt=pt, sg_=sg_: e.activation(out=dst5[:, :, sg_, :, :], in_=pt, func=AF.Copy)),
                 reads=[f"ps{bank}"], writes=[dst_buf.tok])

    xweights(XW5, W["XW"], 8, lambda s_: 7 - s_)
    xweights(XWs5, W["XWs"], 4, lambda s_: 3 - s_)
    S.op("dve", lambda e: e.tensor_copy(out=W["L8r"].ap, in_=PW3r[:, 8, :]), reads=[PWr.tok], writes=[W["L8r"].tok])
    S.op("dve", lambda e: e.tensor_copy(out=W["L8i"].ap, in_=PW3i[:, 8, :]), reads=[PWi.tok], writes=[W["L8i"].tok])
    S.op("dve", lambda e: e.tensor_copy(out=W["L4r"].ap, in_=PW3r[:, 4, :]), reads=[PWr.tok], writes=[W["L4r"].tok])
    S.op("dve", lambda e: e.tensor_copy(out=W["L4i"].ap, in_=PW3i[:, 4, :]), reads=[PWi.tok], writes=[W["L4i"].tok])
    act(cx, R8, Ar, AF.Exp, scale=8.0)
    RH3 = W["RHO"].ap.rearrange("p (q c) -> p q c", c=NBLK)
    S.op("dve", lambda e: e.tensor_copy(out=RH3, in_=R8.ap.rearrange("p (q o) -> p q o", o=1).to_broadcast([128, 16, NBLK])),
         reads=[R8.tok], writes=[W["RHO"].tok])
    S.op("pool", lambda e: e.memset(RH3[:, :, 0:1], 0.0), reads=[W["RHO"].tok], writes=[W["RHO"].tok])
    S.op("dve", lambda e: e.reciprocal(out=den.ap, in_=R8.ap), reads=[R8.tok, den.tok], writes=[den.tok])
    ew(cx, ALU.mult, Ekr, W["L8r"], den)
    ew(cx, ALU.mult, Eki, W["L8i"], den)
    ET3r = W["ETr"].ap.rearrange("p (q c) -> p q c", c=NBLK)
    ET3i = W["ETi"].ap.rearrange("p (q c) -> p q c", c=NBLK)
    S.op("pool", lambda e: e.memset(ET3r[:, :, 0:1], 1.0), writes=[W["ETr"].tok])
    S.op("pool", lambda e: e.memset(ET3i[:, :, 0:1], 0.0), writes=[W["ETi"].tok])
    kk = 1
    ek = (Ekr, Eki)
    ek2 = (Ek2r, Ek2i)
    while kk < NBLK:
        bcr = ek[0].ap.rearrange("p (q o) -> p q o", o=1).to_broadcast([128, 16, kk])
        bci = ek[1].ap.rearrange("p (q o) -> p q o", o=1).to_broadcast([128, 16, kk])
        cmul(cx, W["ETr"], W["ETi"], W["ETr"], W["ETi"], ek[0], ek[1], t1, t2, shape3=(16, kk),
             or_ap=ET3r[:, :, kk:2 * kk], oi_ap=ET3i[:, :, kk:2 * kk],
             ar_ap=ET3r[:, :, 0:kk], ai_ap=ET3i[:, :, 0:kk], br_ap=bcr, bi_ap=bci)
        cmul(cx, ek2[0], ek2[1], ek[0], ek[1], ek[0], ek[1], t1, t2)
        ek, ek2 = ek2, ek
        kk *= 2
    W["_keep"] = keep
    W["_dbg"] = dict(PWr=PWr, PWi=PWi, BBr=BBr, BBi=BBi, gr=gr, gi=gi)
    return W


POOL_W = (2, 4, 8, 16)


def mixer_ab_phase(cx, x_dram, T, ycatT, Pm, outs, samp, n_light=0, fix_tile=0, invc_dram=None):
    S, A = cx.S, cx.A
    S.barrier()
    A.reset()
    TT = 512
    NBLK = TT // 8
    NT = T // TT
    pfx = "m0"
    w_in = A.bf16(8 * D).rearrange("p (k f) -> p k f", f=D)
    w_glu = A.bf16(4 * 512).rearrange("p (k f) -> p k f", f=512)
    pw = A.bf16(4 * 128).rearrange("p (g f) -> p g f", f=128)
    g0_b = A.f32(D)
    dcol = A.f32(4)
    bglu = A.f32(4)
    pscale = A.f32(4)
    invc = A.f32(4 * 16).rearrange("p (g t) -> p g t", t=16)
    load_weight_bf16(cx, w_in, Pm["w_in"], 8, pfx + "w_in")
    load_weight_bf16(cx, w_glu, Pm["w_glu"], 4, pfx + "w_glu")
    for g in range(4):
        S.op("pool", (lambda e, g=g: e.dma_start(out=pw[:, g, :], in_=Pm["pool_w"][g])),
             writes=[pfx + "pw"], dma_tok=pfx + "pw")
    load_bcast_row(cx, g0_b, Pm["g0"], pfx + "g0")
    S.op("sp", lambda e: e.dma_start(out=dcol, in_=Pm["d"].rearrange("(k p) -> p k", p=128),
                                     allow_slow_non_contiguous=True), writes=[pfx + "dcol"], dma_tok=pfx + "dcol")
    S.op("sp", lambda e: e.dma_start(out=bglu, in_=Pm["b_glu"].rearrange("(k p) -> p k", p=128),
                                     allow_slow_non_contiguous=True), writes=[pfx + "bglu"], dma_tok=pfx + "bglu")
    S.op("sp", lambda e: e.dma_start(out=pscale, in_=Pm["pool_scale"].rearrange("(k p) -> p k", p=128),
                                     allow_slow_non_contiguous=True), writes=[pfx + "pscale"], dma_tok=pfx + "pscale")
    if invc_dram is None:
        for g in range(4):
            w = POOL_W[g]
            S.op("pool", (lambda e, g=g, w=w: e.memset(invc[:, g, :], 1.0 / w)), writes=[pfx + "invc"])
            for t in range(w - 1):
                S.op("pool", (lambda e, g=g, t=t: e.memset(invc[:, g, t:t + 1], 1.0 / (t + 1))),
                     reads=[pfx + "invc"], writes=[pfx + "invc"])
    else:
        S.op("sp", lambda e: e.dma_start(out=invc.rearrange("p g t -> p (g t)"), in_=invc_dram.partition_broadcast(128)),
             writes=[pfx + "invc"], dma_tok=pfx + "invc")
    if K_STAGE <= -2:
        return None
    Wt = s5_precompute(cx, Pm, NBLK, 4)
    if K_STAGE <= -1:
        return None
    S.barrier()
    A.off = Wt["_keep"]
    KB4 = Wt["KB"].ap.rearrange("p (k m c) -> p k m c", k=4, m=8)
    XW5 = Wt["XW"].ap.rearrange("p (k s r c) -> p k s r c", k=4, s=8, r=2)
    XWs5 = Wt["XWs"].ap.rearrange("p (k s r c) -> p k s r c", k=4, s=4, r=2)
    CW5 = Wt["CW"].ap.rearrange("p (q t r c) -> p q t r c", q=16, t=8, r=2)
    ETr = Wt["ETr"].ap
    ETi = Wt["ETi"].ap
    RHO = Wt["RHO"].ap
    wtoks = [Wt[n].tok for n in ("KB", "XW", "XWs", "CW", "ETr", "ETi", "RHO", "L8r", "L8i", "L4r", "L4i")]
    hld = [A.f32(D) for _ in range(2)]
    xn = [A.bf16(D) for _ in range(2)]
    xnT = A.bf16(8 * TT).rearrange("p (k t) -> p k t", t=TT)
    uP = A.f32(4 * (16 + TT)).rearrange("p (g t) -> p g t", t=16 + TT)
    uS = A.f32(4 * TT).rearrange("p (k t) -> p k t", t=TT)
    uSb = A.bf16(4 * TT).rearrange("p (k t) -> p k t", t=TT)
    pa = A.f32(16 + TT)
    pb = A.f32(16 + TT)
    dT = A.bf16(4 * TT).rearrange("p (g t) -> p g t", t=TT)
    NE = 16 * NBLK
    Xr, Xi = A.f32(NE), A.f32(NE)
    Mr, Mi = A.f32(NE), A.f32(NE)
    Tt = A.f32(NE)
    Hr = A.f32(16 * (NBLK + 1))
    Hi = A.f32(16 * (NBLK + 1))
    Hb = A.bf16(2 * NE).rearrange("p (r q c) -> p r q c", r=2, q=16)
    cr_ = A.f32(16)
    ci_ = A.f32(16)
    ct_ = A.f32(16)
    ys = A.f32(4 * TT).rearrange("p (k t) -> p k t", t=TT)
    zb = A.bf16(4 * TT).rearrange("p (k t) -> p k t", t=TT)
    sig = A.f32(TT)
    ycat = A.bf16(8 * TT).rearrange("p (k t) -> p k t", t=TT)
    junk = A.bf16(D)
    ssq = A.f32(4)
    rstd = A.f32(4)
    Hr3 = Hr.rearrange("p (q c) -> p q c", c=NBLK + 1)
    Hi3 = Hi.rearrange("p (q c) -> p q c", c=NBLK + 1)
    X3r = Xr.rearrange("p (q c) -> p q c", c=NBLK)
    X3i = Xi.rearrange("p (q c) -> p q c", c=NBLK)
    M3r = Mr.rearrange("p (q c) -> p q c", c=NBLK)
    M3i = Mi.rearrange("p (q c) -> p q c", c=NBLK)
    T3 = Tt.rearrange("p (q c) -> p q c", c=NBLK)
    ycatT_v = ycatT.rearrange("(k p) t -> p k t", p=128)
    tk = lambda n: pfx + n
    S.op("pool", lambda e: e.memset(Hr, 0.0), writes=[tk("H")])
    S.op("pool", lambda e: e.memset(Hi, 0.0), writes=[tk("H")])
    S.op("pool", lambda e: e.memset(uP, 0.0), writes=[tk("uP")])

    def tt_op(eng, op, o, a, b, reads, writes):
        S.op(eng, lambda e: e.tensor_tensor(out=o, in0=a, in1=b, op=op), reads=reads, writes=writes)

    def prenorm_tile(src_rows, nsub, ncols):
        for s in range(nsub):
            hb = hld[s % 2]
            htok = tk(f"hld{s % 2}")
            xb = xn[s % 2]
            xtok = tk(f"xn{s % 2}")
            src = src_rows(s)
            S.op("sp", (lambda e, hb=hb, src=src: e.dma_start(out=hb[0:src.shape[0], :], in_=src)),
                 writes=[htok], dma_tok=htok)
            S.op("act", (lambda e, hb=hb, s=s: e.activation(out=junk, in_=hb, func=AF.Square,
                                                            accum_out=ssq[:, s:s + 1])),
                 reads=[htok], writes=[tk("junk"), tk("ssq")])
            S.op("act", (lambda e, s=s: e.activation(out=rstd[:, s:s + 1], in_=ssq[:, s:s + 1], func=AF.Sqrt,
                                                     scale=1.0 / D, bias=cx.eps_t)),
                 reads=[tk("ssq"), "eps"], writes=[tk("rstd")])
            S.op("dve", (lambda e, s=s: e.reciprocal(out=rstd[:, s:s + 1], in_=rstd[:, s:s + 1])),
                 reads=[tk("rstd")], writes=[tk("rstd")])
            S.op("dve", (lambda e, hb=hb, xb=xb, s=s: e.scalar_tensor_tensor(
                out=xb, in0=hb, scalar=rstd[:, s:s + 1], in1=g0_b, op0=ALU.mult, op1=ALU.mult)),
                reads=[htok, tk("rstd"), tk("g0")], writes=[xtok])
            bank = 4 + (s % 2)
            pt = cx.ps[bank][:, :].bitcast(BF16).rearrange("p (k c) -> p k c", c=128)
            for kc in range(8):
                S.op("pe", (lambda e, pt=pt, xb=xb, kc=kc: e.transpose(pt[:, kc, :], xb[:, kc * 128:(kc + 1) * 128],
                                                                       cx.ident)),
                     reads=[xtok, "ident"], writes=[f"ps{bank}"])
            S.op("act", (lambda e, pt=pt, s=s: e.activation(out=xnT[:, :, s * 128:(s + 1) * 128], in_=pt,
                                                            func=AF.Copy)),
                 reads=[f"ps{bank}"], writes=[tk("xnT")])

    def in_proj(ncols, uP_dst, uS_dst, uSb_dst):
        for oc in range(8):
            bank = oc % 4
            for kc in range(8):
                S.op("pe", (lambda e, oc=oc, kc=kc, bank=bank: e.matmul(
                    cx.ps[bank][:, 0:ncols], lhsT=w_in[:, kc, oc * 128:(oc + 1) * 128], rhs=xnT[:, kc, 0:ncols],
                    start=(kc == 0), stop=(kc == 7))),
                    reads=[tk("xnT"), tk("w_in")], writes=[f"ps{bank}"])
            if K_STAGE <= 0.6:
                continue
            if oc < 4:
                dst = uP_dst(oc)
                S.op("act", (lambda e, bank=bank, dst=dst: e.activation(out=dst, in_=cx.ps[bank][:, 0:ncols],
                                                                        func=AF.Copy)),
                     reads=[f"ps{bank}", tk("uP")], writes=[tk("uP")])
            else:
                d1 = uS_dst(oc - 4)
                d2 = uSb_dst(oc - 4)
                S.op("act", (lambda e, bank=bank, d1=d1: e.activation(out=d1, in_=cx.ps[bank][:, 0:ncols],
                                                                      func=AF.Copy)),
                     reads=[f"ps{bank}"], writes=[tk("uS")])
                S.op("act", (lambda e, bank=bank, d2=d2: e.activation(out=d2, in_=cx.ps[bank][:, 0:ncols], func=AF.Copy)),
                     reads=[f"ps{bank}"], writes=[tk("uSb")])

    def glu_and_store(ncols, col0):
        S.op("act", lambda e: e.activation(out=ys[:, :, 0:ncols], in_=ys[:, :, 0:ncols], func=AF.Gelu_apprx_tanh),
             reads=[tk("ys")], writes=[tk("ys")])
        S.op("act", lambda e: e.activation(out=zb[:, :, 0:ncols], in_=ys[:, :, 0:ncols], func=AF.Copy),
             reads=[tk("ys")], writes=[tk("zb")])
        for oc in range(4):
            bank = oc % 4
            for kc in range(4):
                S.op("pe", (lambda e, oc=oc, kc=kc, bank=bank: e.matmul(
                    cx.ps[bank][:, 0:ncols], lhsT=w_glu[:, kc, oc * 128:(oc + 1) * 128], rhs=zb[:, kc, 0:ncols],
                    start=(kc == 0), stop=(kc == 3))),
                    reads=[tk("zb"), tk("w_glu")], writes=[f"ps{bank}"])
            S.op("act", (lambda e, oc=oc, bank=bank: e.activation(out=sig[:, 0:ncols], in_=cx.ps[bank][:, 0:ncols],
                                                                  func=AF.Sigmoid, bias=bglu[:, oc:oc + 1])),
                 reads=[f"ps{bank}", tk("bglu")], writes=[tk("sig")])
            S.op("dve", (lambda e, oc=oc: e.tensor_tensor(out=ycat[:, 4 + oc, 0:ncols], in0=ys[:, oc, 0:ncols],
                                                          in1=sig[:, 0:ncols], op=ALU.mult)),
                 reads=[tk("ys"), tk("sig")], writes=[tk("ycat")])
        S.op("sp", lambda e: e.dma_start(out=ycatT_v[:, :, col0:col0 + ncols], in_=ycat[:, :, 0:ncols]),
             reads=[tk("ycat")], writes=[f"ycatT@{col0}"], dma_tok=tk("ycat"))

    def pool_matmul(ncols):
        for g in range(4):
            bank = g % 4
            S.op("pe", (lambda e, g=g, bank=bank: e.matmul(cx.ps[bank][:, 0:ncols], lhsT=pw[:, g, :],
                                                          rhs=dT[:, g, 0:ncols], start=True, stop=True)),
                 reads=[tk("dT"), tk("pw")], writes=[f"ps{bank}"])
            S.op("act", (lambda e, g=g, bank=bank: e.activation(out=ycat[:, g, 0:ncols], in_=cx.ps[bank][:, 0:ncols],
                                                                func=AF.Copy, scale=pscale[:, g:g + 1])),
                 reads=[f"ps{bank}", tk("pscale")], writes=[tk("ycat")])

    def prompt_tile(ti):
        t0 = ti * TT
        prenorm_tile(lambda s: x_dram[t0 + s * 128:t0 + (s + 1) * 128, :], 4, TT)
        if K_STAGE <= 0.5:
            return
        in_proj(TT, lambda g: uP[:, g, 16:16 + TT], lambda k: uS[:, k, :], lambda k: uSb[:, k, :])
        light = ti < n_light
        for g in range(4):
            if light:
                break
            cur = uP[:, g, :]
            ctoks = [tk("uP")]
            sh = 1
            for step in range(g + 1):
                dst = pa if step % 2 == 0 else pb
                dtok = tk("pa") if step % 2 == 0 else tk("pb")
                lo = 2 * sh - 1
                S.op("pool", (lambda e, dst=dst, cur=cur, lo=lo, sh=sh: e.tensor_tensor(
                    out=dst[:, lo:16 + TT], in0=cur[:, lo:16 + TT], in1=cur[:, lo - sh:16 + TT - sh], op=ALU.add)),
                    reads=ctoks, writes=[dtok])
                cur, ctoks = dst, [dtok]
                sh *= 2
            w = POOL_W[g]
            S.op("dve", (lambda e, g=g, cur=cur, w=w: e.scalar_tensor_tensor(
                out=dT[:, g, :], in0=cur[:, 16:16 + TT], scalar=1.0 / w, in1=uP[:, g, 16:16 + TT],
                op0=ALU.mult, op1=ALU.subtract)), reads=ctoks + [tk("uP")], writes=[tk("dT")])
            if ti == fix_tile:
                S.op("dve", (lambda e, g=g, cur=cur: e.tensor_tensor(out=sig[:, 0:16], in0=cur[:, 16:32],
                                                                     in1=invc[:, g, :], op=ALU.mult)),
                     reads=ctoks + [tk("invc")], writes=[tk("sig")])
                S.op("dve", (lambda e, g=g: e.tensor_tensor(out=dT[:, g, 0:16], in0=sig[:, 0:16],
                                                            in1=uP[:, g, 16:32], op=ALU.subtract)),
                     reads=[tk("sig"), tk("uP"), tk("dT")], writes=[tk("dT")])
        if not light:
            pool_matmul(TT)
        if ti == NT - 1:
            for g in range(4):
                S.op("sp", (lambda e, g=g: e.dma_start(
                    out=outs["pool_prompt"][:, g * 128:(g + 1) * 128].rearrange("t c -> c t"),
                    in_=uP[:, g, TT + 1:TT + 16], allow_slow_non_contiguous=True)),
                    reads=[tk("uP")], dma_tok=tk("pp_out"))
        S.op("pool", lambda e: e.tensor_copy(out=uP[:, :, 0:16], in_=uP[:, :, TT:TT + 16]),
             reads=[tk("uP")], writes=[tk("uP")])
        if K_STAGE <= 2:
            return
        uSb4 = uSb.rearrange("p k (c s) -> p k c s", s=8)
        for q in range(16):
            k, ql = q // 4, q % 4
            for ri in range(2):
                bank = ri * 2 + (q // 8)
                outp = cx.ps[bank][:, (q % 8) * NBLK:(q % 8 + 1) * NBLK]
                for sg_ in range(8):
                    S.op("pe", (lambda e, k=k, ql=ql, ri=ri, sg_=sg_, outp=outp: e.matmul(
                        outp, lhsT=XW5[32 * ql:32 * ql + 32, k, sg_, ri, :], rhs=uSb4[32 * ql:32 * ql + 32, k, :, sg_],
                        start=(sg_ == 0), stop=(sg_ == 7), tile_position=(32 * ql, 0))),
                        reads=[tk("uSb")] + wtoks, writes=[f"ps{bank}"])
        for ri, X_ in ((0, Xr), (1, Xi)):
            for hf in range(2):
                bank = ri * 2 + hf
                S.op("act", (lambda e, X_=X_, hf=hf, bank=bank: e.activation(
                    out=X_[:, hf * 512:(hf + 1) * 512], in_=cx.ps[bank][:, :], func=AF.Copy)),
                    reads=[f"ps{bank}"], writes=[tk("X")])
        if K_STAGE <= 3:
            return
        hpr, hpi = Hr3[:, :, NBLK], Hi3[:, :, NBLK]
        L8r, L8i = Wt["L8r"].ap, Wt["L8i"].ap
        tt_op("dve", ALU.mult, cr_, L8r, hpr, [tk("H")] + wtoks, [tk("c")])
        tt_op("dve", ALU.mult, ct_, L8i, hpi, [tk("H")] + wtoks, [tk("ct")])
        tt_op("dve", ALU.subtract, cr_, cr_, ct_, [tk("c"), tk("ct")], [tk("c")])
        tt_op("dve", ALU.mult, ci_, L8r, hpi, [tk("H")] + wtoks, [tk("ci")])
        tt_op("dve", ALU.mult, ct_, L8i, hpr, [tk("H")] + wtoks + [tk("c")], [tk("ct")])
        tt_op("dve", ALU.add, ci_, ci_, ct_, [tk("ci"), tk("ct")], [tk("ci")])
        tt_op("dve", ALU.add, X3r[:, :, 0], X3r[:, :, 0], cr_, [tk("X"), tk("c")], [tk("X")])
        tt_op("dve", ALU.add, X3i[:, :, 0], X3i[:, :, 0], ci_, [tk("X"), tk("ci")], [tk("X")])
        S.op("dve", lambda e: e.tensor_copy(out=Hr3[:, :, 0], in_=hpr), reads=[tk("H"), tk("Hb")], writes=[tk("H")])
        S.op("dve", lambda e: e.tensor_copy(out=Hi3[:, :, 0], in_=hpi), reads=[tk("H"), tk("Hb")], writes=[tk("H")])
        tt_op("dve", ALU.mult, Mr, Xr, ETr, [tk("X")] + wtoks, [tk("M")])
        tt_op("dve", ALU.mult, Tt, Xi, ETi, [tk("X")] + wtoks, [tk("T")])
        tt_op("dve", ALU.add, Mr, Mr, Tt, [tk("M"), tk("T")], [tk("M")])
        tt_op("dve", ALU.mult, Mi, Xi, ETr, [tk("X")] + wtoks, [tk("M")])
        tt_op("dve", ALU.mult, Tt, Xr, ETi, [tk("X"), tk("M")] + wtoks, [tk("T")])
        tt_op("dve", ALU.subtract, Mi, Mi, Tt, [tk("M"), tk("T")], [tk("M")])
        S.op("dve", lambda e: e.tensor_tensor_scan(out=Xr, data0=RHO, data1=Mr, initial=0.0, op0=ALU.mult, op1=ALU.add),
             reads=[tk("M"), tk("X")] + wtoks, writes=[tk("X")])
        S.op("dve", lambda e: e.tensor_tensor_scan(out=Xi, data0=RHO, data1=Mi, initial=0.0, op0=ALU.mult, op1=ALU.add),
             reads=[tk("M"), tk("X")] + wtoks, writes=[tk("X")])
        tt_op("dve", ALU.mult, Mr, Xr, ETr, [tk("X"), tk("M")] + wtoks, [tk("M")])
        tt_op("dve", ALU.mult, Tt, Xi, ETi, [tk("X"), tk("T")] + wtoks, [tk("T")])
        tt_op("dve", ALU.subtract, Hr3[:, :, 1:NBLK + 1], M3r, T3, [tk("M"), tk("T"), tk("H"), tk("c"), tk("ci")], [tk("H")])
        tt_op("dve", ALU.mult, Mi, Xi, ETr, [tk("X"), tk("M")] + wtoks, [tk("M")])
        tt_op("dve", ALU.mult, Tt, Xr, ETi, [tk("X"), tk("T"), tk("H")] + wtoks, [tk("T")])
        tt_op("dve", ALU.add, Hi3[:, :, 1:NBLK + 1], M3i, T3, [tk("M"), tk("T"), tk("H")], [tk("H")])
        S.op("act", lambda e: e.activation(out=Hb[:, 0, :, :], in_=Hr3[:, :, 0:NBLK], func=AF.Copy),
             reads=[tk("H")], writes=[tk("Hb")])
        S.op("act", lambda e: e.activation(out=Hb[:, 1, :, :], in_=Hi3[:, :, 0:NBLK], func=AF.Copy),
             reads=[tk("H")], writes=[tk("Hb")])
        if light:
            return
        for k in range(4):
            bank = 4 + k
            yv = cx.ps[bank][:, :].rearrange("p (c s) -> p c s", s=8)
            first = [True]
            for tau in range(8):
                for sg_ in range(tau + 1):
                    st = first[0]
                    first[0] = False
                    S.op("pe", (lambda e, k=k, tau=tau, sg_=sg_, yv=yv, st=st: e.matmul(
                        yv[:, :, tau], lhsT=KB4[:, k, tau - sg_, :], rhs=uSb4[:, k, :, sg_],
                        start=st, stop=False, skip_group_check=True)),
                        reads=[tk("uSb")] + wtoks, writes=[f"ps{bank}"])
                for ql in range(4):
                    q = 4 * k + ql
                    for ri in range(2):
                        S.op("pe", (lambda e, q=q, ql=ql, tau=tau, ri=ri, yv=yv: e.matmul(
                            yv[32 * ql:32 * ql + 32, :, tau], lhsT=CW5[:, q, tau, ri, :], rhs=Hb[:, ri, q, :],
                            start=False, stop=(ri == 1 and ql == 3), tile_position=(0, 32 * ql),
                            skip_group_check=True)),
                            reads=[tk("Hb")] + wtoks, writes=[f"ps{bank}"])
            S.op("dve", (lambda e, k=k, bank=bank: e.scalar_tensor_tensor(
                out=ys[:, k, :], in0=uS[:, k, :], scalar=dcol[:, k:k + 1], in1=cx.ps[bank][:, :],
                op0=ALU.mult, op1=ALU.add)), reads=[tk("uS"), tk("dcol"), f"ps{bank}"], writes=[tk("ys")])
        if K_STAGE <= 5:
            return
        glu_and_store(TT, t0)
        if ti == NT - 1:
            for gl in range(2):
                for ri, H3 in ((0, Hr3), (1, Hi3)):
                    S.op("sp", (lambda e, gl=gl, ri=ri, H3=H3: e.dma_start(
                        out=outs["s5_prompt"].rearrange("(q gl) p r -> gl p q r", gl=2)[gl][:, :, ri],
                        in_=H3[gl * 64:(gl + 1) * 64, :, NBLK], allow_slow_non_contiguous=True)),
                        reads=[tk("H")], dma_tok=tk("s5_out"))

    def sample_mixer():
        NBS = 4
        prenorm_tile(lambda s: samp["x_s"], 1, 128)
        uPs = pa[:, 0:4 * NBS * 20]
        uPs4 = uPs.rearrange("p (g b t) -> p g b t", g=4, b=NBS)
        pbs = pb[:, 0:NBS * 20].rearrange("p (b t) -> p b t", t=20)
        pcs = sig[:, 0:NBS * 20].rearrange("p (b t) -> p b t", t=20)
        S.op("pool", lambda e: e.memset(uPs, 0.0), reads=[tk("pa")], writes=[tk("pa")])
        for b in range(NBS):
            for g in range(4):
                S.op("sp", (lambda e, b=b, g=g: e.dma_start(
                    out=uPs4[:, g, b, 1:16], in_=samp["state_pool"][b, :, g * 128:(g + 1) * 128].rearrange("t c -> c t"),
                    allow_slow_non_contiguous=True)), reads=[tk("pa")], writes=[tk("pa")], dma_tok=tk("spl"))
        for oc in range(8):
            bank = oc % 4
            for kc in range(8):
                S.op("pe", (lambda e, oc=oc, kc=kc, bank=bank: e.matmul(
                    cx.ps[bank][:, 0:16], lhsT=w_in[:, kc, oc * 128:(oc + 1) * 128], rhs=xnT[:, kc, 0:16],
                    start=(kc == 0), stop=(kc == 7))), reads=[tk("xnT"), tk("w_in")], writes=[f"ps{bank}"])
            if oc < 4:
                S.op("act", (lambda e, oc=oc, bank=bank: e.activation(
                    out=uPs4[:, oc, :, 16:20], in_=cx.ps[bank][:, 0:16].rearrange("p (b t) -> p b t", t=4), func=AF.Copy)),
                    reads=[f"ps{bank}", tk("pa")], writes=[tk("pa")])
            else:
                S.op("act", (lambda e, oc=oc, bank=bank: e.activation(out=uS[:, oc - 4, 0:16], in_=cx.ps[bank][:, 0:16],
                                                                      func=AF.Copy)),
                     reads=[f"ps{bank}", tk("uS")], writes=[tk("uS")])
                S.op("act", (lambda e, oc=oc, bank=bank: e.activation(out=uSb[:, oc - 4, 0:16], in_=cx.ps[bank][:, 0:16],
                                                                      func=AF.Copy)),
                     reads=[f"ps{bank}", tk("uSb")], writes=[tk("uSb")])
        S.op("pool", lambda e: e.memset(ycat[:, :, 0:128], 0.0), reads=[tk("ycat")], writes=[tk("ycat")])
        S.op("pool", lambda e: e.memset(ys[:, :, 0:128], 0.0), reads=[tk("ys")], writes=[tk("ys")])
        for g in range(4):
            cur = uPs4[:, g, :, :]
            ctoks = [tk("pa")]
            sh = 1
            for step in range(g + 1):
                dst = pbs if step % 2 == 0 else pcs
                dtok = tk("pb") if step % 2 == 0 else tk("sig")
                lo = 2 * sh - 1
                S.op("pool", (lambda e, dst=dst, cur=cur, lo=lo, sh=sh: e.tensor_tensor(
                    out=dst[:, :, lo:20], in0=cur[:, :, lo:20], in1=cur[:, :, lo - sh:20 - sh], op=ALU.add)),
                    reads=ctoks + [dtok], writes=[dtok])
                cur, ctoks = dst, [dtok]
                sh *= 2
            w = POOL_W[g]
            S.op("dve", (lambda e, g=g, cur=cur, w=w: e.scalar_tensor_tensor(
                out=dT[:, g, 0:16].rearrange("p (b t) -> p b t", t=4), in0=cur[:, :, 16:20], scalar=1.0 / w,
                in1=uPs4[:, g, :, 16:20], op0=ALU.mult, op1=ALU.subtract)),
                reads=ctoks + [tk("pa"), tk("dT")], writes=[tk("dT")])
        pool_matmul(16)
        for b in range(NBS):
            for g in range(4):
                S.op("sp", (lambda e, b=b, g=g: e.dma_start(
                    out=samp["pool_out"][b, :, g * 128:(g + 1) * 128].rearrange("t c -> c t"), in_=uPs4[:, g, b, 5:20],
                    allow_slow_non_contiguous=True)), reads=[tk("pa")], dma_tok=tk("pso"))
        H0r = Hr[:, 0:16 * NBS].rearrange("p (q b) -> p q b", b=NBS)
        H0i = Hi[:, 0:16 * NBS].rearrange("p (q b) -> p q b", b=NBS)
        Hbs = Hb.rearrange("p r q c -> p (r q c)")[:, 0:2 * 16 * NBS].rearrange("p (r q b) -> p r q b", r=2, q=16)
        for b in range(NBS):
            for gl in range(2):
                for ri, H0 in ((0, H0r), (1, H0i)):
                    S.op("sp", (lambda e, b=b, gl=gl, ri=ri, H0=H0: e.dma_start(
                        out=H0[gl * 64:(gl + 1) * 64, :, b],
                        in_=samp["state_s5"][b].rearrange("(q gl) p r -> gl p q r", gl=2)[gl][:, :, ri],
                        allow_slow_non_contiguous=True)), reads=[tk("H"), tk("Hb")], writes=[tk("H")], dma_tok=tk("s5l"))
        S.op("act", lambda e: e.activation(out=Hbs[:, 0, :, :], in_=H0r, func=AF.Copy), reads=[tk("H"), tk("Hb")], writes=[tk("Hb")])
        S.op("act", lambda e: e.activation(out=Hbs[:, 1, :, :], in_=H0i, func=AF.Copy), reads=[tk("H"), tk("Hb")], writes=[tk("Hb")])
        uSb4s = uSb[:, :, 0:16].rearrange("p k (b s) -> p k b s", s=4)
        for q in range(16):
            k, ql = q // 4, q % 4
            for ri in range(2):
                bank = 2 * ri
                for sg_ in range(4):
                    S.op("pe", (lambda e, k=k, ql=ql, q=q, ri=ri, sg_=sg_, bank=bank: e.matmul(
                        cx.ps[bank][:, q * NBS:(q + 1) * NBS], lhsT=XWs5[32 * ql:32 * ql + 32, k, sg_, ri, :],
                        rhs=uSb4s[32 * ql:32 * ql + 32, k, :, sg_], start=(sg_ == 0), stop=(sg_ == 3),
                        tile_position=(32 * ql, 0))), reads=[tk("uSb")] + wtoks, writes=[f"ps{bank}"])
        Xs_r = Xr[:, 0:16 * NBS].rearrange("p (q b) -> p q b", b=NBS)
        Xs_i = Xi[:, 0:16 * NBS].rearrange("p (q b) -> p q b", b=NBS)
        S.op("act", lambda e: e.activation(out=Xr[:, 0:16 * NBS], in_=cx.ps[0][:, 0:16 * NBS], func=AF.Copy),
             reads=["ps0", tk("X")], writes=[tk("X")])
        S.op("act", lambda e: e.activation(out=Xi[:, 0:16 * NBS], in_=cx.ps[2][:, 0:16 * NBS], func=AF.Copy),
             reads=["ps2", tk("X")], writes=[tk("X")])
        bc4 = lambda ap16: ap16.rearrange("p (q o) -> p q o", o=1).to_broadcast([128, 16, NBS])
        L4r, L4i = bc4(Wt["L4r"].ap), bc4(Wt["L4i"].ap)
        M4r = Mr[:, 0:16 * NBS].rearrange("p (q b) -> p q b", b=NBS)
        M4i = Mi[:, 0:16 * NBS].rearrange("p (q b) -> p q b", b=NBS)
        T4 = Tt[:, 0:16 * NBS].rearrange("p (q b) -> p q b", b=NBS)
        tt_op("dve", ALU.mult, M4r, H0r, L4r, [tk("H"), tk("M")] + wtoks, [tk("M")])
        tt_op("dve", ALU.mult, T4, H0i, L4i, [tk("H"), tk("T")] + wtoks, [tk("T")])
        tt_op("dve", ALU.subtract, M4r, M4r, T4, [tk("M"), tk("T")], [tk("M")])
        tt_op("dve", ALU.add, Xs_r, Xs_r, M4r, [tk("M"), tk("X")], [tk("X")])
        tt_op("dve", ALU.mult, M4i, H0r, L4i, [tk("H"), tk("M")] + wtoks, [tk("M")])
        tt_op("dve", ALU.mult, T4, H0i, L4r, [tk("H"), tk("T"), tk("M")] + wtoks, [tk("T")])
        tt_op("dve", ALU.add, M4i, M4i, T4, [tk("M"), tk("T")], [tk("M")])
        tt_op("dve", ALU.add, Xs_i, Xs_i, M4i, [tk("M"), tk("X")], [tk("X")])
        for b in range(NBS):
            for gl in range(2):
                for ri, X_ in ((0, Xs_r), (1, Xs_i)):
                    S.op("sp", (lambda e, b=b, gl=gl, ri=ri, X_=X_: e.dma_start(
                        out=samp["s5_out"][b].rearrange("(q gl) p r -> gl p q r", gl=2)[gl][:, :, ri],
                        in_=X_[gl * 64:(gl + 1) * 64, :, b], allow_slow_non_contiguous=True)),
                        reads=[tk("X")], dma_tok=tk("s5so"))
        for k in range(4):
            bank = 4 + k
            yv = cx.ps[bank][:, 0:16].rearrange("p (b s) -> p b s", s=4)
            first = [True]
            for tau in range(4):
                for sg_ in range(tau + 1):
                    st = first[0]
                    first[0] = False
                    S.op("pe", (lambda e, k=k, tau=tau, sg_=sg_, yv=yv, st=st: e.matmul(
                        yv[:, :, tau], lhsT=KB4[:, k, tau - sg_, :], rhs=uSb4s[:, k, :, sg_],
                        start=st, stop=False, skip_group_check=True)),
                        reads=[tk("uSb")] + wtoks, writes=[f"ps{bank}"])
                for ql in range(4):
                    q = 4 * k + ql
                    for ri in range(2):
                        S.op("pe", (lambda e, q=q, ql=ql, tau=tau, ri=ri, yv=yv: e.matmul(
                            yv[32 * ql:32 * ql + 32, :, tau], lhsT=CW5[:, q, tau, ri, :], rhs=Hbs[:, ri, q, :],
                            start=False, stop=(ri == 1 and ql == 3), tile_position=(0, 32 * ql),
                            skip_group_check=True)),
                            reads=[tk("Hb")] + wtoks, writes=[f"ps{bank}"])
            S.op("dve", (lambda e, k=k, bank=bank: e.scalar_tensor_tensor(
                out=ys[:, k, 0:16], in0=uS[:, k, 0:16], scalar=dcol[:, k:k + 1], in1=cx.ps[bank][:, 0:16],
                op0=ALU.mult, op1=ALU.add)), reads=[tk("uS"), tk("dcol"), f"ps{bank}", tk("ys")], writes=[tk("ys")])
        glu_and_store(128, T)

    for ti in range(NT):
        if K_STAGE <= 0.1:
            break
        prompt_tile(ti)
    if samp is not None:
        sample_mixer()
    return dict(prenorm_tile=prenorm_tile, in_proj=in_proj, glu_and_store=glu_and_store, pool_matmul=pool_matmul,
                bufs=dict(uP=uP, uS=uS, uSb=uSb, pa=pa, pb=pb, dT=dT, ys=ys, Hr3=Hr3, Hi3=Hi3, Hb=Hb, Xr=Xr, Xi=Xi,
                          ycat=ycat, sig=sig, Wt=Wt, XWs5=XWs5, KB4=KB4, CW5=CW5, invc=invc, dcol=dcol),
                tk=tk, TT=TT, NBLK=NBLK, wtoks=wtoks)


def proj_res_phase(cx, aT, tiles, W_dram, g_post, pfx, row_perm=None):
    S, A = cx.S, cx.A
    S.barrier()
    A.reset()
    Wb = A.bf16(8 * D).rearrange("p (k f) -> p k f", f=D)
    g_b = A.f32(D)
    load_weight_bf16(cx, Wb, W_dram, 8, pfx + "W")
    load_bcast_row(cx, g_b, g_post, pfx + "g")
    NS = max(t[1] for t in tiles)
    TT = NS * 128
    NB = 2
    abuf = [A.bf16(8 * TT).rearrange("p (k t) -> p k t", t=TT) for _ in range(NB)]
    hbuf = [A.f32(D) for _ in range(3)]
    junk = A.bf16(D)
    tmp = A.f32(D)
    ssq2 = A.f32(4)
    rstd2 = A.f32(1)
    aT_v = aT.rearrange("(k p) t -> p k t", p=128)
    cnt = [0]

    def tile_body(ti, col0, nsub, h_src, h_dst, a_name, hsrc_name, dst_name, row0):
        ab = abuf[ti % NB]
        atok = f"{pfx}a{ti % NB}"
        nc_ = nsub * 128
        S.op("sp", lambda e: e.dma_start(out=ab[:, :, 0:nc_], in_=aT_v[:, :, col0:col0 + nc_]),
             reads=[f"{a_name}@{col0}"], writes=[atok], dma_tok=atok)
        for s in range(nsub):
            i = cnt[0] % 3
            cnt[0] += 1
            hb = hbuf[i]
            htok = f"{pfx}h{i}"
            S.op("sp", (lambda e, hb=hb, s=s: e.dma_start(out=hb, in_=h_src[s * 128:(s + 1) * 128, :])),
                 reads=[f"{hsrc_name}@{row0 + 128 * s}"], writes=[htok], dma_tok=htok)
            halves = []
            for hf in range(2):
                bank = 2 * (s % 2) + hf
                for kc in range(8):
                    S.op("pe", (lambda e, s=s, hf=hf, kc=kc, bank=bank: e.matmul(
                        cx.ps[bank][:, :], lhsT=ab[:, kc, s * 128:(s + 1) * 128],
                        rhs=Wb[:, kc, hf * 512:(hf + 1) * 512], start=(kc == 0), stop=(kc == 7))),
                        reads=[atok, pfx + "W"], writes=[f"ps{bank}"])
                halves.append((bank, cx.ps[bank][:, :]))
            n2 = dict(junk=pfx + "junk", ssq2=pfx + "ssq2", rstd2=pfx + "rstd2", tmp=pfx + "tmp", hout=htok)
            post_norm_residual(cx, halves, hb, g_b, pfx + "g", ssq2, rstd2, junk, tmp, hb, [htok], n2)
            S.op("pool", (lambda e, hb=hb, s=s: e.dma_start(out=h_dst[s * 128:(s + 1) * 128, :], in_=hb)),
                 reads=[htok], writes=[f"{dst_name}@{row0 + 128 * s}"], dma_tok=htok)

    for ti, t in enumerate(tiles):
        tile_body(ti, *t)


def qkv_phase(cx, hsrc, hname, tiles, w_qkv, g_pre, QT, KT, V, pfx="q1"):
    S, A = cx.S, cx.A
    S.barrier()
    A.reset()
    W3 = A.bf16(8 * 3 * D).rearrange("p (k f) -> p k f", f=3 * D)
    g_b = A.f32(D)
    load_weight_bf16(cx, W3, w_qkv, 8, pfx + "W")
    load_bcast_row(cx, g_b, g_pre, pfx + "g")
    TT = 512
    h3b = [A.f32(4 * D).rearrange("p (s d) -> p s d", d=D) for _ in range(2)]
    xn = A.bf16(4 * D).rearrange("p (s d) -> p s d", d=D)
    xnT = A.bf16(8 * TT).rearrange("p (k t) -> p k t", t=TT)
    qt = [A.bf16(8 * TT).rearrange("p (k t) -> p k t", t=TT) for _ in range(2)]
    sq = [A.bf16(TT) for _ in range(2)]
    vb = [A.bf16(D) for _ in range(2)]
    vf = [A.f32(D) for _ in range(2)]
    junk = A.bf16(D)
    ssq = A.f32(4)
    rstd = A.f32(4)
    m1 = A.f32(2)
    SEL = A.bf16(8 * 16).rearrange("p (k h) -> p k h", h=16)
    S.op("pool", lambda e: e.memset(SEL, 0.0), writes=[pfx + "SEL"])
    for kc in range(8):
        for hh in range(2):
            S.op("pool", (lambda e, kc=kc, hh=hh: e.memset(SEL[64 * hh:64 * hh + 64, kc, 2 * kc + hh:2 * kc + hh + 1], 1.0)),
                 reads=[pfx + "SEL"], writes=[pfx + "SEL"])
    S.op("pool", lambda e: e.memset(cx.qkmax, 0.0), writes=["qkmax"])
    QT_v = QT.rearrange("(k p) t -> p k t", p=128)
    KT_v = KT.rearrange("(k p) t -> p k t", p=128)
    cnt = [0, 0]

    def tile_body(ti, row0, nsub, k_out, v_out, nvalid):
        ncols = nsub * 128
        htok = pfx + f"h{ti % 2}"
        h3 = h3b[ti % 2]
        S.op("sp", lambda e: e.dma_start(out=h3[:, 0:nsub, :],
                                         in_=hsrc[row0:row0 + ncols, :].rearrange("(s p) d -> p s d", p=128)),
             reads=[f"{hname}@{row0 + 128 * s_}" for s_ in range(nsub)], writes=[htok], dma_tok=htok)
        names = dict(junk=pfx + "junk", ssq=pfx + "ssq", rstd=pfx + "rstd", xn=pfx + "xn", xnT=pfx + "xnT")
        rms_prenorm_T(cx, h3, nsub, g_b, pfx + "g", ssq, rstd, junk, xn, xnT, [htok], names, [0, 1])
        for which, dst_v, dname in ((0, QT_v, "QT"), (1, KT_v, "KT")):
            qb = qt[cnt[0] % 2]
            qtok = f"{pfx}qt{cnt[0] % 2}"
            cnt[0] += 1
            for oc in range(8):
                bank = 2 + (oc % 2)
                for kc in range(8):
                    S.op("pe", (lambda e, oc=oc, kc=kc, bank=bank, which=which: e.matmul(
                        cx.ps[bank][:, 0:ncols], lhsT=W3[:, kc, which * D + oc * 128:which * D + (oc + 1) * 128],
                        rhs=xnT[:, kc, 0:ncols], start=(kc == 0), stop=(kc == 7))),
                        reads=[pfx + "xnT", pfx + "W"], writes=[f"ps{bank}"])
                S.op("act", (lambda e, oc=oc, bank=bank, qb=qb: e.activation(out=qb[:, oc, 0:ncols],
                                                                             in_=cx.ps[bank][:, 0:ncols], func=AF.Copy)),
                     reads=[f"ps{bank}"], writes=[qtok])
                if K_STAGE <= 6:
                    continue
                sb = sq[oc % 2]
                S.op("act", (lambda e, oc=oc, bank=bank, sb=sb: e.activation(out=sb[:, 0:ncols],
                                                                             in_=cx.ps[bank][:, 0:ncols], func=AF.Square)),
                     reads=[f"ps{bank}"], writes=[f"{pfx}sq{oc % 2}"])
                S.op("pe", (lambda e, oc=oc, sb=sb: e.matmul(cx.ps[4][0:16, 0:ncols], lhsT=SEL[:, oc, :], rhs=sb[:, 0:ncols],
                                                             start=(oc == 0), stop=(oc == 7))),
                     reads=[f"{pfx}sq{oc % 2}", pfx + "SEL"], writes=["ps4"])
            if K_STAGE > 6.5:
              S.op("dve", (lambda e: e.reduce_max(out=m1[0:16, 0:1], in_=cx.ps[4][0:16, 0:ncols], axis=AX.X)),
                 reads=["ps4"], writes=[pfx + "m1"])
            if K_STAGE > 6.7:
              S.op("dve", (lambda e, which=which: e.tensor_tensor(out=cx.qkmax[0:16, which:which + 1],
                                                                in0=cx.qkmax[0:16, which:which + 1], in1=m1[0:16, 0:1],
                                                                op=ALU.max)),
                 reads=[pfx + "m1", "qkmax"], writes=["qkmax"])
            S.op("pool", (lambda e, qb=qb, dst_v=dst_v: e.dma_start(out=dst_v[:, :, row0:row0 + ncols], in_=qb[:, :, 0:ncols])),
                 reads=[qtok], writes=[f"{dname}@{row0}"], dma_tok=qtok)
        for which, out_ap in ((2, v_out), (1, k_out)):
            if which == 1 and k_out is None:
                continue
            if K_STAGE <= 7:
                out_ap = None
                if which == 1:
                    continue
            for s in range(nsub):
                i = cnt[1] % 2
                cnt[1] += 1
                vbb, vff = vb[i], vf[i]
                vtok, ftok = f"{pfx}vb{i}", f"{pfx}vf{i}"
                for hf in range(2):
                    bank = 5 + hf
                    for kc in range(8):
                        S.op("pe", (lambda e, s=s, hf=hf, kc=kc, bank=bank, which=which: e.matmul(
                            cx.ps[bank][:, :], lhsT=xnT[:, kc, s * 128:(s + 1) * 128],
                            rhs=W3[:, kc, which * D + hf * 512:which * D + (hf + 1) * 512],
                            start=(kc == 0), stop=(kc == 7))),
                            reads=[pfx + "xnT", pfx + "W"], writes=[f"ps{bank}"])
                    if which == 2:
                        S.op("act", (lambda e, hf=hf, bank=bank, vbb=vbb: e.activation(
                            out=vbb[:, hf * 512:(hf + 1) * 512], in_=cx.ps[bank][:, :], func=AF.Copy)),
                            reads=[f"ps{bank}"], writes=[vtok])
                    if out_ap is not None:
                        S.op("dve", (lambda e, hf=hf, bank=bank, vff=vff: e.tensor_copy(
                            out=vff[:, hf * 512:(hf + 1) * 512], in_=cx.ps[bank][:, :])),
                            reads=[f"ps{bank}"], writes=[ftok])
                if which == 2:
                    S.op("pool", (lambda e, s=s, vbb=vbb: e.dma_start(out=V[row0 + s * 128:row0 + (s + 1) * 128, :], in_=vbb)),
                         reads=[vtok], writes=[f"V@{row0 + s * 128}"], dma_tok=vtok)
                if out_ap is not None:
                    nv = min(128, nvalid - s * 128)
                    if nv > 0:
                        S.op("pool", (lambda e, s=s, vff=vff, nv=nv, out_ap=out_ap: e.dma_start(
                            out=out_ap[s * 128:s * 128 + nv, :], in_=vff[0:nv, :])),
                            reads=[ftok], dma_tok=ftok)

    for ti, t in enumerate(tiles):
        tile_body(ti, *t)


DILS = tuple(int(x) for x in os.environ.get("K_DILS", "1,4,16").split(","))


def attn_setup(cx, pfx="a1"):
    S, A = cx.S, cx.A
    mask = A.bf16(512).rearrange("p (h k q) -> p h k q", h=2, k=2)
    cneg = A.f32(8)
    cb = A.f32(16)
    c16 = A.f32(2)
    dg = A.bf16(16)
    on16 = A.bf16(128)
    S.op("pool", lambda e: e.memset(mask, 1.0), writes=[pfx + "mask"])
    for hh in range(2):
        S.op("pool", (lambda e, hh=hh: e.affine_select(out=mask[:, hh, 0, :], in_=mask[:, hh, 0, :], pattern=[[-1, 128]],
                                                       compare_op=ALU.is_ge, fill=0.0, base=0, channel_multiplier=1)),
             reads=[pfx + "mask"], writes=[pfx + "mask"])
        S.op("pool", (lambda e, hh=hh: e.affine_select(out=mask[:, hh, 1, :], in_=mask[:, hh, 1, :], pattern=[[1, 128]],
                                                       compare_op=ALU.is_ge, fill=0.0, base=0, channel_multiplier=-1)),
             reads=[pfx + "mask"], writes=[pfx + "mask"])
    S.op("dve", lambda e: e.tensor_tensor(out=c16[0:16, 0:1], in0=cx.qkmax[0:16, 0:1], in1=cx.qkmax[0:16, 1:2], op=ALU.mult),
         reads=["qkmax"], writes=[pfx + "c16"])
    S.op("act", lambda e: e.activation(out=c16[0:16, 0:1], in_=c16[0:16, 0:1], func=AF.Sqrt, scale=(1.02 / 8) ** 2),
         reads=[pfx + "c16"], writes=[pfx + "c16"])
    S.op("dve", lambda e: e.tensor_scalar(out=dg[0:16, :], in0=cx.ident[0:16, 0:16], scalar1=c16[0:16, 0:1], scalar2=None,
                                          op0=ALU.mult), reads=[pfx + "c16", "ident"], writes=[pfx + "dg"])
    S.op("pool", lambda e: e.memset(on16, 1.0), writes=[pfx + "on16"])
    S.op("pe", lambda e: e.matmul(cx.ps[7][:, 0:16], lhsT=on16[0:16, :], rhs=dg[0:16, :], start=True, stop=True),
         reads=[pfx + "on16", pfx + "dg"], writes=["ps7"])
    S.op("dve", lambda e: e.tensor_copy(out=cb, in_=cx.ps[7][:, 0:16]), reads=["ps7"], writes=[pfx + "cb"])
    cb3 = cb.rearrange("p (k h) -> p k h", h=2)
    S.op("dve", lambda e: e.tensor_tensor(out=cneg, in0=cb3[:, :, 0], in1=cb3[:, :, 1], op=ALU.max),
         reads=[pfx + "cb"], writes=[pfx + "cneg"])
    S.op("dve", lambda e: e.tensor_scalar(out=cneg, in0=cneg, scalar1=-1.0, scalar2=None, op0=ALU.mult),
         reads=[pfx + "cneg"], writes=[pfx + "cneg"])
    return dict(mask=mask, cneg=cneg, pfx=pfx)


def attn_phase(cx, T, QT, KT, V, attnT, pfx="a1", st_list=None, halo_end=0, hflag=None):
    S, A = cx.S, cx.A
    S.barrier()
    A.reset()
    C = attn_setup(cx, pfx)
    mask, cneg = C["mask"], C["cneg"]
    ST = 2048
    NST = T // ST
    if st_list is None:
        st_list = list(range(NST))
    fcol = A.f32(1)
    if hflag is not None:
        S.op("sp", lambda e: e.dma_start(out=fcol, in_=hflag.partition_broadcast(128)), writes=[pfx + "fcol"], dma_tok=pfx + "fcol")
    qtb = [A.bf16(ST + 16) for _ in range(2)]
    ktw = [A.bf16(2 * ST + 16) for _ in range(2)]
    acc = [A.f32(2 * (ST + 16)).rearrange("p (n t) -> p n t", n=2) for _ in range(2)]
    rec = A.f32(ST)
    ob = [A.bf16(ST) for _ in range(2)]
    NV = 10
    vbuf = [A.bf16(128) for _ in range(NV)]
    NP = 4
    pT = [A.bf16(512).rearrange("p (h k q) -> p h k q", h=2, k=2) for _ in range(NP)]
    cnt = dict(v=0, p=0, s=0, n=0, par=0, m=0)

    def unit(kc, par, T0, d, r, beta, nb_first, qtile, kwin, accb, vmap, first_d):
        vmap = dict(vmap)
        if K_STAGE <= 12:
            return
        kblocks = [b for b in (beta - 1, beta) if b >= 0]
        qoff = r + d * 128 * beta - T0
        qv = qtile[:, qoff:qoff + 128 * d].rearrange("p (m s) -> p m s", s=d)[:, :, 0]
        pi = cnt["p"] % NP
        cnt["p"] += 1
        pt = pT[pi]
        ptok = f"{pfx}pT{pi}"
        sb0 = 2 * (cnt["s"] % 2)
        cnt["s"] += 1
        for hh in range(2):
            sbank = sb0 + hh
            for b in kblocks:
                kb = b - (beta - 1)
                koff = r + d * 128 * b - (T0 - ST)
                kv = kwin[:, koff:koff + 128 * d].rearrange("p (m s) -> p m s", s=d)[:, :, 0]
                S.op("pe", (lambda e, hh=hh, kb=kb, kv=kv, qv=qv, sbank=sbank: e.matmul(
                    cx.ps[sbank][:, kb * 128:(kb + 1) * 128],
                    lhsT=kv[64 * hh:64 * hh + 64, :], rhs=qv[64 * hh:64 * hh + 64, :], start=True, stop=True)),
                    reads=[f"{pfx}qt{par}", f"{pfx}kt{par}"], writes=[f"ps{sbank}"])
        if K_STAGE <= 12.5:
            return
        for hh in range(2):
            sbank = sb0 + hh
            S.op("act", (lambda e, sbank=sbank, pt=pt, hh=hh: e.activation(
                out=pt[:, hh, :, :].rearrange("p k q -> p (k q)"), in_=cx.ps[sbank][:, 0:256], func=AF.Exp, scale=0.125,
                bias=cneg[:, kc:kc + 1])), reads=[f"ps{sbank}", pfx + "cneg"], writes=[ptok])
        if K_STAGE <= 13:
            return
        halo_kb0 = (hflag is not None) and (beta - 1 >= 0) and (r + d * (128 * (beta - 1) + 127) < halo_end)
        if halo_kb0:
            S.op("dve", (lambda e, pt=pt: e.scalar_tensor_tensor(out=pt[:, :, 0, :], in0=pt[:, :, 0, :], scalar=fcol[:, 0:1],
                                                                 in1=mask[:, :, 0, :], op0=ALU.mult, op1=ALU.mult)),
                 reads=[ptok, pfx + "mask", pfx + "fcol"], writes=[ptok])
            S.op("dve", (lambda e, pt=pt: e.tensor_tensor(out=pt[:, :, 1, :], in0=pt[:, :, 1, :], in1=mask[:, :, 1, :], op=ALU.mult)),
                 reads=[ptok, pfx + "mask"], writes=[ptok])
        else:
            S.op("dve", (lambda e, pt=pt: e.tensor_tensor(out=pt.rearrange("p h k q -> p (h k q)"), in0=pt.rearrange("p h k q -> p (h k q)"),
                                                          in1=mask.rearrange("p h k q -> p (h k q)"), op=ALU.mult)),
                 reads=[ptok, pfx + "mask"], writes=[ptok])
        def stage_b():
            nbank = 4 + cnt["n"] % 3
            cnt["n"] += 1
            for hh in range(2):
                for j, b in enumerate(kblocks):
                    kb = b - (beta - 1)
                    vi = vmap[b]
                    S.op("pe", (lambda e, hh=hh, kb=kb, vi=vi, j=j, nbank=nbank, pt=pt: e.matmul(
                        cx.ps[nbank][64 * hh:64 * hh + 64, 0:128], lhsT=vbuf[vi][:, 64 * hh:64 * hh + 64], rhs=pt[:, hh, kb, :],
                        start=(j == 0), stop=(j == len(kblocks) - 1), tile_position=(0, 64 * hh))),
                        reads=[ptok, f"{pfx}v{vi}"], writes=[f"ps{nbank}"])
                for j, b in enumerate(kblocks):
                    kb = b - (beta - 1)
                    S.op("pe", (lambda e, hh=hh, kb=kb, j=j, nbank=nbank, pt=pt: e.matmul(
                        cx.ps[nbank][64 * hh:64 * hh + 64, 128:256], lhsT=cx.ones_bf[:, 0:64], rhs=pt[:, hh, kb, :],
                        start=(j == 0), stop=(j == len(kblocks) - 1), tile_position=(0, 64 * hh))),
                        reads=[ptok, "ones"], writes=[f"ps{nbank}"])
            av = accb[:, :, qoff:qoff + 128 * d].rearrange("p n (m s) -> p n m s", s=d)[:, :, :, 0]
            nd = cx.ps[nbank][:, 0:256].rearrange("p (n q) -> p n q", n=2)
            if first_d:
                S.op("act", (lambda e, av=av, nd=nd: e.activation(out=av, in_=nd, func=AF.Copy)),
                     reads=[f"ps{nbank}"], writes=[f"{pfx}acc{par}"])
            else:
                S.op("dve", (lambda e, av=av, nd=nd: e.tensor_tensor(out=av, in0=nd, in1=av, op=ALU.add)),
                     reads=[f"ps{nbank}", f"{pfx}acc{par}"], writes=[f"{pfx}acc{par}"])


        return stage_b

    def pair_body(st, kc):
        T0 = st * ST
        par = cnt["par"] % 2
        cnt["par"] += 1
        qtile, kwin, accb, obb = qtb[par], ktw[par], acc[par], ob[par]
        S.op("sp", lambda e: e.dma_start(out=qtile[:, 0:ST], in_=QT[kc * 128:(kc + 1) * 128, T0:T0 + ST]),
             reads=[f"QT@{T0 + 512 * i}" for i in range(4)], writes=[f"{pfx}qt{par}"], dma_tok=f"{pfx}qt{par}")
        k0 = max(0, T0 - ST)
        S.op("sp", lambda e: e.dma_start(out=kwin[:, k0 - (T0 - ST):2 * ST], in_=KT[kc * 128:(kc + 1) * 128, k0:T0 + ST]),
             reads=[f"KT@{k0 + 512 * i}" for i in range((T0 + ST - k0) // 512)], writes=[f"{pfx}kt{par}"],
             dma_tok=f"{pfx}kt{par}")
        pend = []
        SKEW = 2
        for di, d in enumerate(DILS):
            nbq = ST // (128 * d)
            for r in range(d):
                beta0 = T0 // (128 * d)
                vmap = {}
                for b in range(beta0 - 1, beta0 + nbq):
                    if b < 0:
                        continue
                    vi = cnt["v"] % NV
                    cnt["v"] += 1
                    vmap[b] = vi
                    vsrc = V[0:T, :].rearrange("(m s) c -> m s c", s=d)[128 * b:128 * b + 128, r, kc * 128:(kc + 1) * 128]
                    S.op("sp", (lambda e, vi=vi, vsrc=vsrc: e.dma_start(out=vbuf[vi], in_=vsrc)),
                         reads=[f"V@{128 * i}" for i in range((r + d * 128 * b) // 128, (r + d * (128 * b + 127)) // 128 + 1)],
                         writes=[f"{pfx}v{vi}"], dma_tok=f"{pfx}v{vi}")
                    if b >= beta0:
                        pend.append(unit(kc, par, T0, d, r, b, beta0, qtile, kwin, accb, vmap, di == 0))
                        if len(pend) > SKEW:
                            pend.pop(0)()
        while pend:
            pend.pop(0)()
        S.op("dve", lambda e: e.reciprocal(out=rec, in_=accb[:, 1, 0:ST]), reads=[f"{pfx}acc{par}"], writes=[pfx + "rec"])
        S.op("dve", lambda e: e.tensor_tensor(out=obb, in0=accb[:, 0, 0:ST], in1=rec, op=ALU.mult),
             reads=[f"{pfx}acc{par}", pfx + "rec"], writes=[f"{pfx}ob{par}"])
        S.op("pool", lambda e: e.dma_start(out=attnT[kc * 128:(kc + 1) * 128, T0:T0 + ST], in_=obb),
             reads=[f"{pfx}ob{par}"], writes=[f"attnT@{T0}k{kc}"], dma_tok=f"{pfx}ob{par}")

    for st in st_list:
        for kc in range(8):
            pair_body(st, kc)


def attn_sample_phase(cx, T, QT, KT, V, ck, cv, attnT, pfx="as"):
    S, A = cx.S, cx.A
    S.barrier()
    A.reset()
    NBS = 4
    tk = lambda n: pfx + n
    QTs = A.bf16(8 * 16).rearrange("p (k t) -> p k t", t=16)
    KTs = A.bf16(8 * 16).rearrange("p (k t) -> p k t", t=16)
    Vn = A.bf16(D)
    ktok = [A.bf16(D) for _ in range(3)]
    KTt = [A.bf16(8 * 128).rearrange("p (k t) -> p k t", t=128) for _ in range(9)]
    Vt = [A.bf16(D) for _ in range(9)]
    sq = A.bf16(D)
    SEL = A.bf16(8 * 16).rearrange("p (k h) -> p k h", h=16)
    on16 = A.bf16(128)
    pTs = A.bf16(2 * 16).rearrange("p (h c) -> p h c", h=2)
    mT1 = A.bf16(2 * 4).rearrange("p (h c) -> p h c", h=2)
    mnew = A.bf16(2 * 4).rearrange("p (h c) -> p h c", h=2)
    attn_s = A.bf16(8 * 128).rearrange("p (k t) -> p k t", t=128)
    m1 = A.f32(2)
    qk = A.f32(2)
    c16 = A.f32(2)
    dg = A.bf16(16)
    cb = A.f32(16)
    cneg = A.f32(8)
    rs = A.f32(4)
    QT_v = QT.rearrange("(k p) t -> p k t", p=128)
    KT_v = KT.rearrange("(k p) t -> p k t", p=128)
    attnT_v = attnT.rearrange("(k p) t -> p k t", p=128)
    S.op("pool", lambda e: e.memset(SEL, 0.0), writes=[tk("SEL")])
    for kc in range(8):
        for hh in range(2):
            S.op("pool", (lambda e, kc=kc, hh=hh: e.memset(SEL[64 * hh:64 * hh + 64, kc, 2 * kc + hh:2 * kc + hh + 1], 1.0)),
                 reads=[tk("SEL")], writes=[tk("SEL")])
    S.op("pool", lambda e: e.memset(on16, 1.0), writes=[tk("on16")])
    S.op("pool", lambda e: e.memset(attn_s, 0.0), writes=[tk("attn_s")])
    S.op("pool", lambda e: e.memset(mT1, 1.0), writes=[tk("mT1")])
    S.op("pool", lambda e: e.memset(mnew, 1.0), writes=[tk("mnew")])
    for hh in range(2):
        S.op("pool", (lambda e, hh=hh: e.affine_select(out=mT1[:, hh, :], in_=mT1[:, hh, :], pattern=[[-1, 4]],
                                                       compare_op=ALU.is_ge, fill=0.0, base=0, channel_multiplier=1)),
             reads=[tk("mT1")], writes=[tk("mT1")])
        S.op("pool", (lambda e, hh=hh: e.affine_select(out=mnew[0:4, hh, :], in_=mnew[0:4, hh, :], pattern=[[1, 4]],
                                                       compare_op=ALU.is_ge, fill=0.0, base=0, channel_multiplier=-1)),
             reads=[tk("mnew")], writes=[tk("mnew")])
        S.op("dve", (lambda e, hh=hh: e.scalar_tensor_tensor(out=mnew[0:4, hh, :], in0=cx.ident[0:4, 0:4], scalar=2.0,
                                                             in1=mnew[0:4, hh, :], op0=ALU.mult, op1=ALU.add)),
             reads=[tk("mnew"), "ident"], writes=[tk("mnew")])
    S.op("sp", lambda e: e.dma_start(out=QTs, in_=QT_v[:, :, T:T + 16]), writes=[tk("QTs")], dma_tok=tk("QTs"))
    S.op("sp", lambda e: e.dma_start(out=KTs, in_=KT_v[:, :, T:T + 16]), writes=[tk("KTs")], dma_tok=tk("KTs"))

    def norms(src3, ncols, dst_col, first, stok):
        sq3 = sq[:, 0:8 * ncols].rearrange("p (k t) -> p k t", t=ncols)
        S.op("act", lambda e: e.activation(out=sq3, in_=src3, func=AF.Square), reads=[stok, tk("sq")], writes=[tk("sq")])
        for kc in range(8):
            S.op("pe", (lambda e, kc=kc: e.matmul(cx.ps[2][0:16, 0:ncols], lhsT=SEL[:, kc, :], rhs=sq3[:, kc, :],
                                                  start=(kc == 0), stop=(kc == 7))),
                 reads=[tk("sq"), tk("SEL")], writes=["ps2"])
        if first:
            S.op("dve", lambda e: e.reduce_max(out=qk[0:16, dst_col:dst_col + 1], in_=cx.ps[2][0:16, 0:ncols], axis=AX.X),
                 reads=["ps2", tk("qk")], writes=[tk("qk")])
        else:
            S.op("dve", lambda e: e.reduce_max(out=m1[0:16, 0:1], in_=cx.ps[2][0:16, 0:ncols], axis=AX.X),
                 reads=["ps2"], writes=[tk("m1")])
            S.op("dve", lambda e: e.tensor_tensor(out=qk[0:16, dst_col:dst_col + 1], in0=qk[0:16, dst_col:dst_col + 1],
                                                  in1=m1[0:16, 0:1], op=ALU.max), reads=[tk("m1"), tk("qk")], writes=[tk("qk")])

    def batch_body(b):
        cs = slice(4 * b, 4 * b + 4)
        ckb, cvb = ck[b], cv[b]
        srcs = [(ckb[1920:2048, :], cvb[1920:2048, :])]
        for t in range(4):
            srcs.append((ckb.rearrange("(m s) c -> m s c", s=4)[384:512, t, :], cvb.rearrange("(m s) c -> m s c", s=4)[384:512, t, :]))
        for t in range(4):
            srcs.append((ckb.rearrange("(m s) c -> m s c", s=16)[0:128, t, :], cvb.rearrange("(m s) c -> m s c", s=16)[0:128, t, :]))
        S.op("sp", lambda e: e.dma_start(out=Vn[0:4, :], in_=V[T + 4 * b:T + 4 * b + 4, :]), reads=[tk("Vn")], writes=[tk("Vn")],
             dma_tok=tk("Vn"))
        norms(QTs[:, :, cs], 4, 0, True, tk("QTs"))
        norms(KTs[:, :, cs], 4, 1, True, tk("KTs"))
        for i, (ks, vs) in enumerate(srcs):
            kt_ = ktok[i % 3]
            ktk = tk(f"ktok{i % 3}")
            S.op("pool", (lambda e, kt_=kt_, ks=ks: e.dma_start(out=kt_, in_=ks)), writes=[ktk], dma_tok=ktk)
            S.op("pool", (lambda e, i=i, vs=vs: e.dma_start(out=Vt[i], in_=vs)), writes=[tk(f"Vt{i}")], dma_tok=tk(f"Vt{i}"))
            bank = i % 2
            pt = cx.ps[bank][:, :].bitcast(BF16).rearrange("p (k c) -> p k c", c=128)
            for kc in range(8):
                S.op("pe", (lambda e, pt=pt, kt_=kt_, kc=kc: e.transpose(pt[:, kc, :], kt_[:, kc * 128:(kc + 1) * 128], cx.ident)),
                     reads=[ktk, "ident"], writes=[f"ps{bank}"])
            S.op("act", (lambda e, pt=pt, i=i: e.activation(out=KTt[i], in_=pt, func=AF.Copy)),
                 reads=[f"ps{bank}"], writes=[tk(f"KTt{i}")])
            norms(KTt[i], 128, 1, False, tk(f"KTt{i}"))
        S.op("dve", lambda e: e.tensor_tensor(out=c16[0:16, 0:1], in0=qk[0:16, 0:1], in1=qk[0:16, 1:2], op=ALU.mult),
             reads=[tk("qk")], writes=[tk("c16")])
        S.op("act", lambda e: e.activation(out=c16[0:16, 0:1], in_=c16[0:16, 0:1], func=AF.Sqrt, scale=(1.02 / 8) ** 2),
             reads=[tk("c16")], writes=[tk("c16")])
        S.op("dve", lambda e: e.tensor_scalar(out=dg[0:16, :], in0=cx.ident[0:16, 0:16], scalar1=c16[0:16, 0:1], scalar2=None,
                                              op0=ALU.mult), reads=[tk("c16"), "ident"], writes=[tk("dg")])
        S.op("pe", lambda e: e.matmul(cx.ps[3][:, 0:16], lhsT=on16[0:16, :], rhs=dg[0:16, :], start=True, stop=True),
             reads=[tk("on16"), tk("dg")], writes=["ps3"])
        S.op("dve", lambda e: e.tensor_copy(out=cb, in_=cx.ps[3][:, 0:16]), reads=["ps3"], writes=[tk("cb")])
        cb3 = cb.rearrange("p (k h) -> p k h", h=2)
        S.op("dve", lambda e: e.tensor_tensor(out=cneg, in0=cb3[:, :, 0], in1=cb3[:, :, 1], op=ALU.max),
             reads=[tk("cb")], writes=[tk("cneg")])
        S.op("dve", lambda e: e.tensor_scalar(out=cneg, in0=cneg, scalar1=-1.0, scalar2=None, op0=ALU.mult),
             reads=[tk("cneg")], writes=[tk("cneg")])
        ktoks = [tk(f"KTt{i}") for i in range(9)]
        vtoks = [tk(f"Vt{i}") for i in range(9)]

        def pair(kc):
            for hh in range(2):
                bank = 4 + hh
                hs = slice(64 * hh, 64 * hh + 64)
                S.op("pe", (lambda e, hs=hs, bank=bank: e.matmul(cx.ps[bank][:, 0:4], lhsT=KTt[0][hs, kc, :], rhs=QTs[hs, kc, cs],
                                                                 start=True, stop=True)),
                     reads=ktoks + [tk("QTs")], writes=[f"ps{bank}"])
                for t in range(4):
                    for base, ti in ((4, 1 + t), (8, 5 + t)):
                        S.op("pe", (lambda e, hs=hs, bank=bank, t=t, base=base, ti=ti: e.matmul(
                            cx.ps[bank][:, base + t:base + t + 1], lhsT=KTt[ti][hs, kc, :],
                            rhs=QTs[hs, kc, 4 * b + t:4 * b + t + 1], start=True, stop=True)),
                            reads=ktoks + [tk("QTs")], writes=[f"ps{bank}"])
                S.op("pe", (lambda e, hs=hs, bank=bank: e.matmul(cx.ps[bank][0:4, 12:16], lhsT=KTs[hs, kc, cs], rhs=QTs[hs, kc, cs],
                                                                 start=True, stop=True)),
                     reads=[tk("KTs"), tk("QTs")], writes=[f"ps{bank}"])
            for hh in range(2):
                bank = 4 + hh
                S.op("act", (lambda e, hh=hh, bank=bank: e.activation(out=pTs[:, hh, :], in_=cx.ps[bank][:, 0:16], func=AF.Exp,
                                                                      scale=0.125, bias=cneg[:, kc:kc + 1])),
                     reads=[f"ps{bank}", tk("cneg")], writes=[tk("pTs")])
            S.op("dve", lambda e: e.tensor_tensor(out=pTs[:, :, 0:4], in0=pTs[:, :, 0:4], in1=mT1, op=ALU.mult),
                 reads=[tk("pTs"), tk("mT1")], writes=[tk("pTs")])
            S.op("dve", lambda e: e.tensor_tensor(out=pTs[0:4, :, 12:16], in0=pTs[0:4, :, 12:16], in1=mnew[0:4, :, :], op=ALU.mult),
                 reads=[tk("pTs"), tk("mnew")], writes=[tk("pTs")])
            for hh in range(2):
                hs = slice(64 * hh, 64 * hh + 64)
                vc = slice(kc * 128 + 64 * hh, kc * 128 + 64 * hh + 64)
                for which in range(2):
                    oc0 = 4 * which
                    lw = (lambda ti, vc=vc: Vt[ti][:, vc]) if which == 0 else (lambda ti: cx.ones_bf[:, 0:64])
                    ln = Vn[0:4, vc] if which == 0 else cx.ones_bf[0:4, 0:64]
                    S.op("pe", (lambda e, hs=hs, hh=hh, oc0=oc0, lw=lw: e.matmul(
                        cx.ps[6][hs, oc0:oc0 + 4], lhsT=lw(0), rhs=pTs[:, hh, 0:4], start=True, stop=False,
                        tile_position=(0, 64 * hh), skip_group_check=True)),
                        reads=vtoks + [tk("pTs"), "ones"], writes=["ps6"])
                    for t in range(4):
                        for base, ti in ((4, 1 + t), (8, 5 + t)):
                            S.op("pe", (lambda e, hs=hs, hh=hh, oc0=oc0, lw=lw, t=t, base=base, ti=ti: e.matmul(
                                cx.ps[6][hs, oc0 + t:oc0 + t + 1], lhsT=lw(ti), rhs=pTs[:, hh, base + t:base + t + 1],
                                start=False, stop=False, tile_position=(0, 64 * hh), skip_group_check=True)),
                                reads=vtoks + [tk("pTs"), "ones"], writes=["ps6"])
                    S.op("pe", (lambda e, hs=hs, hh=hh, oc0=oc0, ln=ln: e.matmul(
                        cx.ps[6][hs, oc0:oc0 + 4], lhsT=ln, rhs=pTs[0:4, hh, 12:16], start=False, stop=True,
                        tile_position=(0, 64 * hh), skip_group_check=True)),
                        reads=[tk("Vn"), tk("pTs"), "ones"], writes=["ps6"])
            S.op("dve", lambda e: e.reciprocal(out=rs, in_=cx.ps[6][:, 4:8]), reads=["ps6"], writes=[tk("rs")])
            S.op("dve", lambda e: e.tensor_tensor(out=attn_s[:, kc, cs], in0=cx.ps[6][:, 0:4], in1=rs, op=ALU.mult),
                 reads=["ps6", tk("rs"), tk("attn_s")], writes=[tk("attn_s")])

        for kc in range(8):
            pair(kc)

    for b in range(NBS):
        batch_body(b)
    S.op("sp", lambda e: e.dma_start(out=attnT_v[:, :, T:T + 128], in_=attn_s), reads=[tk("attn_s")], dma_tok=tk("attn_s"))


T_PROMPT = int(os.environ.get("K_T", "8192"))
NCORES = 8
WEIGHT_SHAPES = {
    "norm_gains": [2, 4, 1024], "ab_w_in": [1, 1024, 1024], "ab_pool_w": [1, 4, 128, 128], "ab_pool_scale": [1, 512],
    "ab_lambda_re": [1, 32, 64], "ab_lambda_im": [1, 32, 64], "ab_log_dt": [1, 32], "ab_b_re": [1, 32, 64, 16],
    "ab_b_im": [1, 32, 64, 16], "ab_c_re": [1, 32, 16, 64], "ab_c_im": [1, 32, 16, 64], "ab_d": [1, 512],
    "ab_w_glu": [1, 512, 512], "ab_b_glu": [1, 512], "ab_w_out": [1, 1024, 1024], "c_w_qkv": [1, 1024, 3072],
    "c_w_o": [1, 1024, 1024], "ffn_w_gate": [2, 1024, FH], "ffn_w_up": [2, 1024, FH], "ffn_w_down": [2, FH, 1024],
}


def build_program(T=T_PROMPT, upto=99):
    nc = bass.Bass("TRN2", target_bir_lowering=False)
    TA = T + 128
    OWN0 = T // 2
    HALO0 = T // 4
    NOWN = T - OWN0
    din = lambda n, s: nc.dram_tensor(n, s, F32, kind="ExternalInput").ap()
    dout = lambda n, s: nc.dram_tensor(n, s, F32, kind="ExternalOutput").ap()
    x = din("x", [T, 1024])
    xs = din("xs", [128, 1024])
    invc_in = din("invc", [64])
    hflag = din("hflag", [1])
    sp_in = din("state_pool_s", [4, 15, 512])
    s5_in = din("state_s5_s", [4, 32, 64, 2])
    ck = din("cache_k_s", [4, 2048, 1024])
    cv = din("cache_v_s", [4, 2048, 1024])
    Wd = {n: din(n, s) for n, s in WEIGHT_SHAPES.items()}
    y_p = dout("y_p", [NOWN, 1024])
    y_s = dout("y_s", [128, 1024])
    pool_p = dout("pool_p", [15, 512])
    s5_p = dout("s5_p", [32, 64, 2])
    KW = 2048
    k_p = dout("k_p", [KW, 1024])
    v_p = dout("v_p", [KW, 1024])
    pool_s = dout("pool_s", [4, 15, 512])
    s5_s = dout("s5_s", [4, 32, 64, 2])
    k_s = dout("k_s", [16, 1024])
    v_s = dout("v_s", [16, 1024])
    scr = lambda n, s, dt: nc.dram_tensor(n, s, dt, kind="Internal").ap()
    H1 = scr("H1", [TA, 1024], F32)
    H2 = scr("H2", [TA, 1024], F32)
    H3 = scr("H3", [TA, 1024], F32)
    ycatT = scr("ycatT", [1024, TA], BF16)
    attnT = scr("attnT", [1024, TA], BF16)
    QT = scr("QT", [1024, TA], BF16)
    KT = scr("KT", [1024, TA], BF16)
    V = scr("V", [TA, 1024], BF16)
    cx = Ctx(nc)
    g = Wd["norm_gains"]
    Pm = dict(lam_re=Wd["ab_lambda_re"][0], lam_im=Wd["ab_lambda_im"][0], log_dt=Wd["ab_log_dt"][0],
              b_re=Wd["ab_b_re"][0], b_im=Wd["ab_b_im"][0], c_re=Wd["ab_c_re"][0], c_im=Wd["ab_c_im"][0],
              w_in=Wd["ab_w_in"][0], w_glu=Wd["ab_w_glu"][0], pool_w=Wd["ab_pool_w"][0], g0=g[0, 0, :],
              d=Wd["ab_d"][0], b_glu=Wd["ab_b_glu"][0], pool_scale=Wd["ab_pool_scale"][0])
    samp = dict(x_s=xs, state_pool=sp_in, state_s5=s5_in, pool_out=pool_s, s5_out=s5_s)
    mixer_ab_phase(cx, x, T, ycatT, Pm, dict(pool_prompt=pool_p, s5_prompt=s5_p), samp,
                   n_light=HALO0 // 512, fix_tile=OWN0 // 512, invc_dram=invc_in)
    if upto >= 2:
        tiles = [(t0, 4, x[t0:t0 + 512, :], H1[t0:t0 + 512, :], "ycatT", "x", "H1", t0) for t0 in range(HALO0, T, 512)]
        tiles.append((T, 1, xs, H1[T:TA, :], "ycatT", "xs", "H1", T))
        proj_res_phase(cx, ycatT, tiles, Wd["ab_w_out"][0], g[0, 1, :], "p0")
    if upto >= 3:
        tiles = [(H1[t0:t0 + 256, :], H2[t0:t0 + 256, :], 2, "H1", "H2", t0) for t0 in range(HALO0, T, 256)]
        tiles.append((H1[T:TA, :], H2[T:TA, :], 1, "H1", "H2", T))
        ffn_phase(cx, tiles, Wd["ffn_w_gate"][0], Wd["ffn_w_up"][0], Wd["ffn_w_down"][0], g[0, 2, :], g[0, 3, :], "f0")
    if upto >= 4:
        tiles = []
        for t0 in range(HALO0, T, 512):
            if t0 >= T - KW:
                o = t0 - (T - KW)
                tiles.append((t0, 4, k_p[o:o + 512, :], v_p[o:o + 512, :], 512))
            else:
                tiles.append((t0, 4, None, None, 0))
        tiles.append((T, 1, k_s, v_s, 16))
        qkv_phase(cx, H2, "H2", tiles, Wd["c_w_qkv"][0], g[1, 0, :], QT, KT, V)
    if upto >= 5:
        attn_phase(cx, T, QT, KT, V, attnT, st_list=list(range(OWN0 // 2048, T // 2048)), halo_end=OWN0, hflag=hflag)
        attn_sample_phase(cx, T, QT, KT, V, ck, cv, attnT)
    if upto >= 6:
        tiles = [(t0, 4, H2[t0:t0 + 512, :], H3[t0:t0 + 512, :], "attnT", "H2", "H3", t0) for t0 in range(OWN0, T, 512)]
        tiles.append((T, 1, H2[T:TA, :], H3[T:TA, :], "attnT", "H2", "H3", T))
        proj_res_phase(cx, attnT, tiles, Wd["c_w_o"][0], g[1, 1, :], "p1")
    if upto >= 7:
        tiles = [(H3[t0:t0 + 256, :], y_p[t0 - OWN0:t0 - OWN0 + 256, :], 2, "H3", "y_p", t0) for t0 in range(OWN0, T, 256)]
        tiles.append((H3[T:TA, :], y_s, 1, "H3", "y_s", T))
        ffn_phase(cx, tiles, Wd["ffn_w_gate"][1], Wd["ffn_w_up"][1], Wd["ffn_w_down"][1], g[1, 2, :], g[1, 3, :], "f1")
    cx.S.finish()
    cx.S.emit()
    return nc


def kernel(**inputs):
    T = T_PROMPT
    OWN0 = T // 2
    NOWN = T - OWN0
    f32 = lambda a: np.ascontiguousarray(np.asarray(a, dtype=np.float32))
    xp = f32(inputs["x_prompt"])
    xsm = f32(inputs["x_sample"])
    spool = f32(inputs["state_pool"])
    ss5 = f32(inputs["state_s5"])
    ckk = np.asarray(inputs["cache_k"], dtype=np.float32)
    cvv = np.asarray(inputs["cache_v"], dtype=np.float32)
    weights = {n: f32(inputs[n]) for n in WEIGHT_SHAPES}
    nb = xp.shape[0]
    SEQ = xp.shape[1]
    assert SEQ == T and nb * 2 == NCORES
    invc_first = np.zeros((4, 16), np.float32)
    invc_plain = np.zeros((4, 16), np.float32)
    for gi, w in enumerate(POOL_W):
        invc_first[gi] = 1.0 / np.minimum(np.arange(16) + 1, w)
        invc_plain[gi] = 1.0 / w
    in_maps = []
    for c in range(NCORES):
        b, half = c // 2, c % 2
        m = dict(weights)
        xl = np.zeros((T, 1024), np.float32)
        if half == 0:
            xl[OWN0:] = xp[b, 0:NOWN]
        else:
            xl[:] = xp[b]
        m["x"] = xl
        m["invc"] = (invc_first if half == 0 else invc_plain).reshape(64).copy()
        m["hflag"] = np.array([float(half)], np.float32)
        xs_pad = np.zeros((128, 1024), np.float32)
        xs_pad[:16] = xsm[4 * c:4 * c + 4].reshape(16, 1024)
        m["xs"] = xs_pad
        m["state_pool_s"] = np.ascontiguousarray(spool[0, 4 * c:4 * c + 4])
        m["state_s5_s"] = np.ascontiguousarray(ss5[0, 4 * c:4 * c + 4])
        m["cache_k_s"] = np.ascontiguousarray(ckk[0, 4 * c:4 * c + 4].reshape(4, 2048, 1024))
        m["cache_v_s"] = np.ascontiguousarray(cvv[0, 4 * c:4 * c + 4].reshape(4, 2048, 1024))
        in_maps.append(m)
    nc = build_program(T)
    res = run_bass_kernel_spmd(nc, in_maps, core_ids=list(range(NCORES)))
    R = res.results
    KW = 2048
    y_prompt = np.stack([np.concatenate([np.asarray(R[2 * b]["y_p"]), np.asarray(R[2 * b + 1]["y_p"])]) for b in range(nb)])
    y_sample = np.concatenate([np.asarray(R[c]["y_s"])[:16].reshape(4, 4, 1024) for c in range(NCORES)])
    last = lambda b: R[2 * b + 1]
    pool_prompt = np.stack([np.asarray(last(b)["pool_p"]) for b in range(nb)])[None]
    s5_prompt = np.stack([np.asarray(last(b)["s5_p"]) for b in range(nb)])[None]
    k_prompt = np.stack([np.asarray(last(b)["k_p"]).reshape(KW, 16, 64) for b in range(nb)])[None]
    v_prompt = np.stack([np.asarray(last(b)["v_p"]).reshape(KW, 16, 64) for b in range(nb)])[None]
    pool_sample = np.concatenate([np.asarray(R[c]["pool_s"]) for c in range(NCORES)])[None]
    s5_sample = np.concatenate([np.asarray(R[c]["s5_s"]) for c in range(NCORES)])[None]
    k_sample = np.concatenate([np.asarray(R[c]["k_s"]).reshape(4, 4, 16, 64) for c in range(NCORES)])[None]
    v_sample = np.concatenate([np.asarray(R[c]["v_s"]).reshape(4, 4, 16, 64) for c in range(NCORES)])[None]
    outs = (y_prompt, y_sample, pool_prompt, s5_prompt, k_prompt, v_prompt, pool_sample, s5_sample, k_sample, v_sample)
    return tuple(np.ascontiguousarray(o, dtype=np.float32) for o in outs)
```

```python
import os
import numpy as np
import concourse.bass as bass
import concourse.mybir as mybir
from concourse.bass_utils import run_bass_kernel_spmd

F32 = mybir.dt.float32
BF16 = mybir.dt.bfloat16
AF = mybir.ActivationFunctionType
ALU = mybir.AluOpType
AX = mybir.AxisListType

D = 1024
FH = 2816
NFC = FH // 128
EPS = 1e-6
SEM_CH = 20000
K_STAGE = float(os.environ.get('K_STAGE', '99'))


class _Op:
    __slots__ = ("eng", "fn", "dma_tok", "waits", "signal", "seq", "dbg")


class Sched:
    ENGS = ("pe", "act", "dve", "pool", "sp")

    def __init__(self, nc, sync_same=True):
        self.nc = nc
        self.streams = {e: [] for e in self.ENGS}
        self.lastw = {}
        self.readers = {}
        self.slot_cum = []
        self.tok_slot = {}
        self.phase_used = 0
        self.sync_same = sync_same
        self.pending = {}

    def op(self, eng, fn, reads=(), writes=(), dma_tok=None):
        o = _Op()
        if dma_tok is not None:
            if dma_tok not in self.tok_slot:
                if self.phase_used >= len(self.slot_cum):
                    self.slot_cum.append(0)
                self.tok_slot[dma_tok] = self.phase_used
                self.phase_used += 1
            dma_tok = self.tok_slot[dma_tok]
        writes = list(writes) + [r for r in reads if r.startswith("ps") and r not in writes]
        o.dbg = (tuple(reads), tuple(writes))
        o.eng, o.fn, o.dma_tok, o.waits, o.signal, o.seq = eng, fn, dma_tok, self.pending.pop(eng, []), False, 0
        deps = []
        seen = set()

        def add(d):
            if d is not None and id(d) not in seen:
                seen.add(id(d))
                deps.append(d)

        for r in reads:
            add(self.lastw.get(r))
        for w in writes:
            add(self.lastw.get(w))
            for rd in self.readers.get(w, ()):
                add(rd)
        for d in deps:
            if d.dma_tok is not None:
                o.waits.append(("dma", d.dma_tok, self.slot_cum[d.dma_tok]))
            else:
                if d.eng == eng and (eng == "pe" or not self.sync_same):
                    continue
                d.signal = True
                o.waits.append(("eng", d))
        if dma_tok is not None:
            self.slot_cum[dma_tok] += 1
        for r in reads:
            self.readers.setdefault(r, []).append(o)
        for w in writes:
            self.lastw[w] = o
            self.readers[w] = []
        self.streams[eng].append(o)
        return o

    def barrier(self):
        lasts = {e: (self.streams[e][-1] if self.streams[e] else None) for e in self.ENGS}
        toks = dict(enumerate(self.slot_cum))
        self.tok_slot = {}
        self.phase_used = 0
        for e in self.ENGS:
            pend = self.pending.setdefault(e, [])
            for e2, d in lasts.items():
                if d is None or e2 == e:
                    continue
                if d.dma_tok is None:
                    d.signal = True
                    pend.append(("eng", d))
            for t, c in toks.items():
                pend.append(("dma", t, c))
        self.lastw = {}
        self.readers = {}

    def dump(self, path):
        seqs = {}
        for eng, ops in self.streams.items():
            c = 0
            for o in ops:
                if o.dma_tok is None and o.signal:
                    c += 1
                    seqs[id(o)] = c
        with open(path, "w") as f:
            for eng, ops in self.streams.items():
                f.write(f"==== {eng}\n")
                for i, o in enumerate(ops):
                    ws = []
                    for w in o.waits:
                        if w[0] == "dma":
                            ws.append(f"D[{w[1]}]>={w[2]}")
                        else:
                            ws.append(f"{w[1].eng}>={seqs.get(id(w[1]))}")
                    f.write(f"{i}: sig={seqs.get(id(o))} dma={o.dma_tok} r/w={getattr(o, 'dbg', None)} waits={ws}\n")

    def finish(self):
        o = _Op()
        o.eng, o.fn, o.dma_tok, o.waits, o.signal, o.seq = "sp", (lambda en: en.nop()), None, [], False, 0
        for t, c in enumerate(self.slot_cum):
            o.waits.append(("dma", t, c))
        self.streams["sp"].append(o)

    def emit(self):
        nc = self.nc
        nsem = {}
        for eng, ops in self.streams.items():
            c = 0
            for o in ops:
                if o.dma_tok is None and o.signal:
                    c += 1
                    o.seq = c
            nsem[eng] = (c + SEM_CH - 1) // SEM_CH
        eng_sem = {eng: [nc.alloc_semaphore(f"s_{eng}{j}") for j in range(max(1, nsem[eng]))]
                   for eng in self.ENGS}
        dma_sem = {i: nc.alloc_semaphore(f"d_{i}") for i in range(len(self.slot_cum))}
        print("[sched] dma sems", len(self.slot_cum), "max count", max(self.slot_cum) if self.slot_cum else 0,
              "ops", {e: len(v) for e, v in self.streams.items()})
        for tok, c in enumerate(self.slot_cum):
            assert c * 16 < 60000, (tok, c)
        streams = self.streams

        def mk(eng):
            def body(e):
                waited = {}
                for o in streams[eng]:
                    for w in o.waits:
                        if w[0] == "dma":
                            key, val, sem = ("d", w[1]), 16 * w[2], dma_sem[w[1]]
                        else:
                            d = w[1]
                            j = (d.seq - 1) // SEM_CH
                            key, val, sem = ("e", d.eng, j), (d.seq - 1) % SEM_CH + 1, eng_sem[d.eng][j]
                        if waited.get(key, 0) >= val:
                            continue
                        e.wait_ge(sem, val)
                        waited[key] = val
                    ins = o.fn(e)
                    if o.dma_tok is not None:
                        ins.then_inc(dma_sem[o.dma_tok], 16)
                    elif o.signal:
                        j = (o.seq - 1) // SEM_CH
                        ins.then_inc(eng_sem[eng][j], 1)
            return body

        with nc.Block() as block:
            block.sync(mk("sp"))
            block.scalar(mk("act"))
            block.vector(mk("dve"))
            block.gpsimd(mk("pool"))
            block.tensor(mk("pe"))


class Arena:
    def __init__(self, arena_ap, nwords):
        self.a = arena_ap
        self.n = nwords
        self.off = 0
        self.mark = 0

    def f32(self, n):
        assert self.off + n <= self.n, ("arena overflow", self.off, n, self.n)
        ap = self.a[:, self.off:self.off + n]
        self.off += n
        return ap

    def bf16(self, n):
        w = (n + 1) // 2
        assert self.off + w <= self.n, ("arena overflow", self.off, w, self.n)
        ap = self.a[:, self.off:self.off + w].bitcast(BF16)
        self.off += w
        return ap

    def set_mark(self):
        self.mark = self.off

    def reset(self):
        self.off = self.mark


class Ctx:
    def __init__(self, nc):
        self.nc = nc
        self.S = Sched(nc, sync_same=(os.environ.get('K_SYNC_SAME', '1') == '1'))
        nwords = (nc.sbuf_bytes_remaining - 2048) // 4
        self.arena_t = nc.alloc_sbuf_tensor("arena", [128, nwords], F32)
        self.A = Arena(self.arena_t[:, :], nwords)
        self.ps = [nc.alloc_psum_tensor(f"psb{i}", [128, 512], F32) for i in range(8)]
        self.uid = 0
        A = self.A
        self.ident = A.bf16(128)
        self.eps_t = A.f32(1)
        self.ones_bf = A.bf16(128)
        self.qkmax = A.f32(2)
        S = self.S
        S.op("pool", lambda e: e.memset(self.ident, 0.0), writes=["ident"])
        S.op("pool", lambda e: e.affine_select(out=self.ident, in_=self.ident, pattern=[[-1, 128]],
                                               compare_op=ALU.not_equal, fill=1.0, base=0,
                                               channel_multiplier=1), reads=["ident"], writes=["ident"])
        S.op("pool", lambda e: e.memset(self.eps_t, EPS), writes=["eps"])
        S.op("pool", lambda e: e.memset(self.ones_bf, 1.0), writes=["ones"])
        A.set_mark()

    def tok(self, name):
        self.uid += 1
        return f"{name}#{self.uid}"


def load_weight_bf16(cx, dst3, w_dram, kchunks, tokname):
    S = cx.S
    for kc in range(kchunks):
        S.op("pool", (lambda e, kc=kc: e.dma_start(out=dst3[:, kc, :], in_=w_dram[kc * 128:(kc + 1) * 128, :])),
             writes=[tokname], dma_tok=tokname)


def load_bcast_row(cx, dst, row_dram, tokname, eng="sp"):
    cx.S.op(eng, lambda e: e.dma_start(out=dst, in_=row_dram.partition_broadcast(128)),
            writes=[tokname], dma_tok=tokname)


def rms_prenorm_T(cx, h3, nsub, g_b, g_tok, ssq, rstd, junk, xn3, xnT3, in_toks, names, psbanks):
    S = cx.S
    for s in range(nsub):
        S.op("act", (lambda e, s=s: e.activation(out=junk, in_=h3[:, s, :], func=AF.Square,
                                                  accum_out=ssq[:, s:s + 1])),
             reads=in_toks, writes=[names["junk"], names["ssq"]])
    S.op("act", lambda e: e.activation(out=rstd[:, 0:nsub], in_=ssq[:, 0:nsub], func=AF.Sqrt,
                                       scale=1.0 / D, bias=cx.eps_t), reads=[names["ssq"], "eps"],
         writes=[names["rstd"]])
    S.op("dve", lambda e: e.reciprocal(out=rstd[:, 0:nsub], in_=rstd[:, 0:nsub]),
         reads=[names["rstd"]], writes=[names["rstd"]])
    for s in range(nsub):
        S.op("dve", (lambda e, s=s: e.scalar_tensor_tensor(out=xn3[:, s, :], in0=h3[:, s, :],
                                                            scalar=rstd[:, s:s + 1], in1=g_b,
                                                            op0=ALU.mult, op1=ALU.mult)),
             reads=in_toks + [names["rstd"], g_tok], writes=[names["xn"] + str(s)])
    for s in range(nsub):
        bank = psbanks[s % len(psbanks)]
        pt = cx.ps[bank][:, :].bitcast(BF16).rearrange("p (k c) -> p k c", c=128)
        for kc in range(8):
            S.op("pe", (lambda e, s=s, kc=kc, pt=pt: e.transpose(pt[:, kc, :], xn3[:, s, kc * 128:(kc + 1) * 128],
                                                                  cx.ident)),
                 reads=[names["xn"] + str(s), "ident"], writes=[f"ps{bank}"])
        eng = "act"
        if eng == "act":
            S.op("act", (lambda e, s=s, pt=pt: e.activation(out=xnT3[:, :, s * 128:(s + 1) * 128], in_=pt,
                                                            func=AF.Copy)),
                 reads=[f"ps{bank}"], writes=[names["xnT"]])
        else:
            S.op("dve", (lambda e, s=s, pt=pt: e.tensor_copy(out=xnT3[:, :, s * 128:(s + 1) * 128], in_=pt)),
                 reads=[f"ps{bank}"], writes=[names["xnT"]])


def post_norm_residual(cx, ps_halves, h_in, g_b, g_tok, ssq2, rstd2, junk, tmp, h_out, in_toks, names):
    S = cx.S
    for i, (bank, ap) in enumerate(ps_halves):
        S.op("act", (lambda e, i=i, ap=ap: e.activation(out=junk[:, 0:512], in_=ap, func=AF.Square,
                                                        accum_out=ssq2[:, i:i + 1])),
             reads=[f"ps{bank}"], writes=[names["junk"], names["ssq2"]])
    S.op("dve", lambda e: e.tensor_tensor(out=ssq2[:, 2:3], in0=ssq2[:, 0:1], in1=ssq2[:, 1:2], op=ALU.add),
         reads=[names["ssq2"]], writes=[names["ssq2"]])
    S.op("act", lambda e: e.activation(out=rstd2, in_=ssq2[:, 2:3], func=AF.Sqrt, scale=1.0 / D, bias=cx.eps_t),
         reads=[names["ssq2"], "eps"], writes=[names["rstd2"]])
    S.op("dve", lambda e: e.reciprocal(out=rstd2, in_=rstd2), reads=[names["rstd2"]], writes=[names["rstd2"]])
    for i, (bank, ap) in enumerate(ps_halves):
        S.op("dve", (lambda e, i=i, ap=ap: e.scalar_tensor_tensor(out=tmp[:, i * 512:(i + 1) * 512], in0=ap,
                                                                  scalar=rstd2, in1=g_b[:, i * 512:(i + 1) * 512],
                                                                  op0=ALU.mult, op1=ALU.mult)),
             reads=[f"ps{bank}", names["rstd2"], g_tok], writes=[names["tmp"]])
    S.op("pool", lambda e: e.tensor_tensor(out=h_out, in0=tmp, in1=h_in, op=ALU.add),
         reads=[names["tmp"]] + in_toks, writes=[names["hout"]])


def ffn_phase(cx, tiles, w_gate, w_up, w_down, g_pre, g_post, pfx):
    S, A = cx.S, cx.A
    S.barrier()
    A.reset()
    wg = A.bf16(8 * FH).rearrange("p (k f) -> p k f", f=FH)
    wu = A.bf16(8 * FH).rearrange("p (k f) -> p k f", f=FH)
    wd = A.bf16(NFC * D).rearrange("p (k f) -> p k f", f=D)
    gpre_b = A.f32(D)
    gpost_b = A.f32(D)
    load_weight_bf16(cx, wg, w_gate, 8, pfx + "wg")
    load_weight_bf16(cx, wu, w_up, 8, pfx + "wu")
    load_weight_bf16(cx, wd, w_down, NFC, pfx + "wd")
    load_bcast_row(cx, gpre_b, g_pre, pfx + "gpre")
    load_bcast_row(cx, gpost_b, g_post, pfx + "gpost")
    NB = 2
    NS = max(t[2] for t in tiles)
    TT = NS * 128
    hbuf = [A.f32(NS * D).rearrange("p (s d) -> p s d", d=D) for _ in range(NB)]
    obuf = [A.f32(D) for _ in range(NB)]
    xn = A.bf16(NS * D).rearrange("p (s d) -> p s d", d=D)
    xnT = [A.bf16(8 * TT).rearrange("p (k t) -> p k t", t=TT) for _ in range(NB)]
    hid = A.bf16(NFC * TT).rearrange("p (k t) -> p k t", t=TT)
    sg = [A.f32(TT) for _ in range(2)]
    junk = A.bf16(D)
    tmp = A.f32(D)
    ssq = A.f32(4)
    rstd = A.f32(4)
    ssq2 = A.f32(4)
    rstd2 = A.f32(1)
    osc_box = [0]

    def st_load(ti, src, dst, nsub, src_name, dst_name, row0):
        b = ti % NB
        h3 = hbuf[b]
        htok = f"{pfx}h{b}"
        S.op("sp", (lambda e, h3=h3, src=src, nsub=nsub: e.dma_start(
            out=h3[:, 0:nsub, :], in_=src.rearrange("(s p) d -> p s d", p=128))),
            reads=[f"{src_name}@{row0 + 128 * s_}" for s_ in range(nsub)], writes=[htok], dma_tok=htok)

    def st_prenorm(ti, src, dst, nsub, src_name, dst_name, row0):
        b = ti % NB
        names = dict(junk=pfx + "junk", ssq=pfx + "ssq", rstd=pfx + "rstd", xn=pfx + "xn", xnT=f"{pfx}xnT{b}")
        rms_prenorm_T(cx, hbuf[b], nsub, gpre_b, pfx + "gpre", ssq, rstd, junk, xn, xnT[b], [f"{pfx}h{b}"], names, [0, 1])

    def st_gateup(ti, src, dst, nsub, src_name, dst_name, row0):
        b = ti % NB
        nt = nsub * 128
        xt = xnT[b]
        xtok = f"{pfx}xnT{b}"
        for fc in range(NFC):
            bg = 2 + (fc % 2)
            bu = 4 + (fc % 2)
            for kc in range(8):
                S.op("pe", (lambda e, fc=fc, kc=kc, bg=bg, xt=xt: e.matmul(
                    cx.ps[bg][:, 0:nt], lhsT=wg[:, kc, fc * 128:(fc + 1) * 128], rhs=xt[:, kc, 0:nt],
                    start=(kc == 0), stop=(kc == 7))),
                    reads=[xtok, pfx + "wg"], writes=[f"ps{bg}"])
            for kc in range(8):
                S.op("pe", (lambda e, fc=fc, kc=kc, bu=bu, xt=xt: e.matmul(
                    cx.ps[bu][:, 0:nt], lhsT=wu[:, kc, fc * 128:(fc + 1) * 128], rhs=xt[:, kc, 0:nt],
                    start=(kc == 0), stop=(kc == 7))),
                    reads=[xtok, pfx + "wu"], writes=[f"ps{bu}"])
            sgb = sg[fc % 2]
            S.op("act", (lambda e, bg=bg, sgb=sgb: e.activation(out=sgb[:, 0:nt], in_=cx.ps[bg][:, 0:nt],
                                                                func=AF.Silu)),
                 reads=[f"ps{bg}"], writes=[f"{pfx}sg{fc % 2}"])
            S.op("dve", (lambda e, fc=fc, bu=bu, sgb=sgb: e.tensor_tensor(
                out=hid[:, fc, 0:nt], in0=cx.ps[bu][:, 0:nt], in1=sgb[:, 0:nt], op=ALU.mult)),
                reads=[f"ps{bu}", f"{pfx}sg{fc % 2}"], writes=[pfx + "hid"])

    def st_down(ti, src, dst, nsub, src_name, dst_name, row0):
        b = ti % NB
        h3 = hbuf[b]
        htok = f"{pfx}h{b}"
        for s in range(nsub):
            halves = []
            for hf in range(2):
                bank = (6 + hf) if s % 2 == 0 else hf
                for fc in range(NFC):
                    S.op("pe", (lambda e, s=s, hf=hf, fc=fc, bank=bank: e.matmul(
                        cx.ps[bank][:, :], lhsT=hid[:, fc, s * 128:(s + 1) * 128],
                        rhs=wd[:, fc, hf * 512:(hf + 1) * 512], start=(fc == 0), stop=(fc == NFC - 1))),
                        reads=[pfx + "hid", pfx + "wd"], writes=[f"ps{bank}"])
                halves.append((bank, cx.ps[bank][:, :]))
            ob = obuf[osc_box[0] % NB]
            otok = f"{pfx}o{osc_box[0] % NB}"
            osc_box[0] += 1
            n2 = dict(junk=pfx + "junk", ssq2=pfx + "ssq2", rstd2=pfx + "rstd2", tmp=pfx + "tmp", hout=otok)
            post_norm_residual(cx, halves, h3[:, s, :], gpost_b, pfx + "gpost", ssq2, rstd2, junk, tmp,
                               ob, [htok], n2)
            S.op("pool", (lambda e, ob=ob, dst=dst, s=s: e.dma_start(out=dst[s * 128:(s + 1) * 128, :], in_=ob)),
                 reads=[otok], writes=[f"{dst_name}@{row0 + 128 * s}"], dma_tok=otok)

    st_load(0, *tiles[0])
    st_prenorm(0, *tiles[0])
    for ti, t in enumerate(tiles):
        if ti + 1 < len(tiles):
            st_load(ti + 1, *tiles[ti + 1])
        st_gateup(ti, *t)
        if ti + 1 < len(tiles):
            st_prenorm(ti + 1, *tiles[ti + 1])
        st_down(ti, *t)


class Buf:
    def __init__(self, ap, tok):
        self.ap = ap
        self.tok = tok


def newbuf(cx, n, name, dt=F32):
    ap = cx.A.f32(n) if dt == F32 else cx.A.bf16(n)
    return Buf(ap, cx.tok(name))


def ew(cx, op, out, a, b, o_ap=None, a_ap=None, b_ap=None, eng="dve"):
    o_ap = out.ap if o_ap is None else o_ap
    a_ap = a.ap if a_ap is None else a_ap
    b_ap = b.ap if b_ap is None else b_ap
    cx.S.op(eng, lambda e: e.tensor_tensor(out=o_ap, in0=a_ap, in1=b_ap, op=op),
            reads=[a.tok, b.tok, out.tok], writes=[out.tok])


def ews(cx, out, a, s1, op0, s2=None, op1=None, o_ap=None, a_ap=None, eng="dve"):
    o_ap = out.ap if o_ap is None else o_ap
    a_ap = a.ap if a_ap is None else a_ap
    if op1 is None:
        cx.S.op(eng, lambda e: e.tensor_scalar(out=o_ap, in0=a_ap, scalar1=s1, scalar2=None, op0=op0),
                reads=[a.tok, out.tok], writes=[out.tok])
    else:
        cx.S.op(eng, lambda e: e.tensor_scalar(out=o_ap, in0=a_ap, scalar1=s1, scalar2=s2, op0=op0, op1=op1),
                reads=[a.tok, out.tok], writes=[out.tok])


def act(cx, out, a, func, scale=1.0, bias=None, o_ap=None, a_ap=None, extra_reads=()):
    o_ap = out.ap if o_ap is None else o_ap
    a_ap = a.ap if a_ap is None else a_ap
    if bias is None:
        cx.S.op("act", lambda e: e.activation(out=o_ap, in_=a_ap, func=func, scale=scale),
                reads=[a.tok, out.tok] + list(extra_reads), writes=[out.tok])
    else:
        cx.S.op("act", lambda e: e.activation(out=o_ap, in_=a_ap, func=func, scale=scale, bias=bias),
                reads=[a.tok, out.tok] + list(extra_reads), writes=[out.tok])


def cmul(cx, outr, outi, ar, ai, br, bi, t1, t2, shape3=None, bcast=None,
         or_ap=None, oi_ap=None, ar_ap=None, ai_ap=None, br_ap=None, bi_ap=None):
    or_ap = outr.ap if or_ap is None else or_ap
    oi_ap = outi.ap if oi_ap is None else oi_ap
    ar_ap = ar.ap if ar_ap is None else ar_ap
    ai_ap = ai.ap if ai_ap is None else ai_ap
    br_ap = br.ap if br_ap is None else br_ap
    bi_ap = bi.ap if bi_ap is None else bi_ap
    t1_ap, t2_ap = t1.ap, t2.ap
    if shape3 is not None:
        t1_ap = t1.ap[:, 0:shape3[0] * shape3[1]].rearrange("p (a b) -> p a b", b=shape3[1])
        t2_ap = t2.ap[:, 0:shape3[0] * shape3[1]].rearrange("p (a b) -> p a b", b=shape3[1])
    else:
        n = or_ap.shape[-1] if len(or_ap.shape) == 2 else None
        if n is not None:
            t1_ap = t1.ap[:, 0:n]
            t2_ap = t2.ap[:, 0:n]
    ew(cx, ALU.mult, t1, ar, br, o_ap=t1_ap, a_ap=ar_ap, b_ap=br_ap)
    ew(cx, ALU.mult, t2, ai, bi, o_ap=t2_ap, a_ap=ai_ap, b_ap=bi_ap)
    ew(cx, ALU.subtract, outr, t1, t2, o_ap=or_ap, a_ap=t1_ap, b_ap=t2_ap)
    ew(cx, ALU.mult, t1, ar, bi, o_ap=t1_ap, a_ap=ar_ap, b_ap=bi_ap)
    ew(cx, ALU.mult, t2, ai, br, o_ap=t2_ap, a_ap=ai_ap, b_ap=br_ap)
    ew(cx, ALU.add, outi, t1, t2, o_ap=oi_ap, a_ap=t1_ap, b_ap=t2_ap)


def s5_precompute(cx, Pm, NBLK, NSAMP):
    S, A = cx.S, cx.A
    import math
    W = {}
    W["KB"] = newbuf(cx, 4 * 8 * 128, "KB", BF16)
    W["XW"] = newbuf(cx, 4 * 8 * 2 * 128, "XW", BF16)
    W["XWs"] = newbuf(cx, 4 * 4 * 2 * 128, "XWs", BF16)
    W["CW"] = newbuf(cx, 16 * 8 * 2 * 32, "CW", BF16)
    W["ETr"] = newbuf(cx, 16 * NBLK, "ETr")
    W["ETi"] = newbuf(cx, 16 * NBLK, "ETi")
    W["RHO"] = newbuf(cx, 16 * NBLK, "RHO")
    W["L8r"] = newbuf(cx, 16, "L8r")
    W["L8i"] = newbuf(cx, 16, "L8i")
    W["L4r"] = newbuf(cx, 16, "L4r")
    W["L4i"] = newbuf(cx, 16, "L4i")
    keep = A.off
    KB4 = W["KB"].ap.rearrange("p (k m c) -> p k m c", k=4, m=8)
    XW5 = W["XW"].ap.rearrange("p (k s r c) -> p k s r c", k=4, s=8, r=2)
    XWs5 = W["XWs"].ap.rearrange("p (k s r c) -> p k s r c", k=4, s=4, r=2)
    CW5 = W["CW"].ap.rearrange("p (q t r c) -> p q t r c", q=16, t=8, r=2)
    nb = lambda n, name, dt=F32: newbuf(cx, n, name, dt)
    LR, LI, LDT = nb(16, "LR"), nb(16, "LI"), nb(16, "LDT")
    DTe, Ar, PH, MAG = nb(16, "DTe"), nb(16, "Ar"), nb(16, "PH"), nb(16, "MAG")
    hp = nb(1, "halfpi")
    cc = [nb(16, "cc0"), nb(16, "cc1")]
    ss = [nb(16, "ss0"), nb(16, "ss1")]
    t1, t2 = nb(512, "t1"), nb(512, "t2")
    PWr, PWi = nb(9 * 16, "PWr"), nb(9 * 16, "PWi")
    gr, gi, den, ta, tb = nb(16, "gr"), nb(16, "gi"), nb(16, "den"), nb(16, "ta"), nb(16, "tb")
    BR, BI, CR, CI = nb(256, "BR"), nb(256, "BI"), nb(256, "CR"), nb(256, "CI")
    BBr, BBi = nb(256, "BBr"), nb(256, "BBi")
    Wr, Wi = nb(256, "Wr"), nb(256, "Wi")
    WZ0 = nb(16 * 2 * 32, "WZ0", BF16)
    BZ = nb(16 * 2 * 32, "BZ", BF16)
    ZX = [nb(16 * 2 * 32, f"ZX{i}", BF16) for i in range(2)]
    Ekr, Eki, Ek2r, Ek2i = nb(16, "Ekr"), nb(16, "Eki"), nb(16, "Ek2r"), nb(16, "Ek2i")
    R8 = nb(16, "R8")

    def v3(b, n=16):
        return b.ap.rearrange("p (q j) -> p q j", j=n)

    def bc(ap16):
        return ap16.rearrange("p (q o) -> p q o", o=1).to_broadcast([128, 16, 16])

    def dma(out_ap, in_ap, buf, slow=False):
        if slow:
            S.op("sp", lambda e: e.dma_start(out=out_ap, in_=in_ap, allow_slow_non_contiguous=True),
                 writes=[buf.tok], dma_tok=buf.tok)
        else:
            S.op("sp", lambda e: e.dma_start(out=out_ap, in_=in_ap), writes=[buf.tok], dma_tok=buf.tok)

    for gl in range(2):
        ps_ = slice(gl * 64, (gl + 1) * 64)
        dma(LR.ap[ps_, :], Pm["lam_re"].rearrange("(q gl) p -> gl p q", gl=2)[gl], LR, slow=True)
        dma(LI.ap[ps_, :], Pm["lam_im"].rearrange("(q gl) p -> gl p q", gl=2)[gl], LI, slow=True)
        dma(LDT.ap[ps_, :], Pm["log_dt"].rearrange("(q gl) -> gl q", gl=2)[gl].partition_broadcast(64), LDT, slow=True)
        dma(v3(BR)[ps_, :, :], Pm["b_re"].rearrange("(q gl) p j -> gl p q j", gl=2)[gl], BR)
        dma(v3(BI)[ps_, :, :], Pm["b_im"].rearrange("(q gl) p j -> gl p q j", gl=2)[gl], BI)
        for q in range(16):
            dma(v3(CR)[ps_, q, :], Pm["c_re"][2 * q + gl].rearrange("i p -> p i"), CR, slow=True)
            dma(v3(CI)[ps_, q, :], Pm["c_im"][2 * q + gl].rearrange("i p -> p i"), CI, slow=True)
    S.op("pool", lambda e: e.memset(hp.ap, math.pi / 2), writes=[hp.tok])
    act(cx, DTe, LDT, AF.Exp)
    ew(cx, ALU.mult, Ar, LR, DTe)
    ew(cx, ALU.mult, PH, LI, DTe)
    act(cx, MAG, Ar, AF.Exp)
    act(cx, ss[0], PH, AF.Sin, scale=1.0 / 64)
    act(cx, cc[0], PH, AF.Sin, scale=1.0 / 64, bias=hp.ap, extra_reads=[hp.tok])
    cur = 0
    for it in range(6):
        nx = 1 - cur
        ew(cx, ALU.mult, t1, cc[cur], cc[cur], o_ap=t1.ap[:, 0:16])
        ew(cx, ALU.mult, t2, ss[cur], ss[cur], o_ap=t2.ap[:, 0:16])
        S.op("dve", (lambda e, cur=cur, nx=nx: e.scalar_tensor_tensor(
            out=ss[nx].ap, in0=cc[cur].ap, scalar=2.0, in1=ss[cur].ap, op0=ALU.mult, op1=ALU.mult)),
            reads=[cc[cur].tok, ss[cur].tok, ss[nx].tok], writes=[ss[nx].tok])
        ew(cx, ALU.subtract, cc[nx], t1, t2, a_ap=t1.ap[:, 0:16], b_ap=t2.ap[:, 0:16])
        cur = nx
    PW3r = PWr.ap.rearrange("p (m q) -> p m q", q=16)
    PW3i = PWi.ap.rearrange("p (m q) -> p m q", q=16)
    S.op("pool", lambda e: e.memset(PW3r[:, 0, :], 1.0), writes=[PWr.tok])
    S.op("pool", lambda e: e.memset(PW3i[:, 0, :], 0.0), writes=[PWi.tok])
    ew(cx, ALU.mult, PWr, MAG, cc[cur], o_ap=PW3r[:, 1, :])
    ew(cx, ALU.mult, PWi, MAG, ss[cur], o_ap=PW3i[:, 1, :])
    for m in range(2, 9):
        cmul(cx, PWr, PWi, PWr, PWi, PWr, PWi, t1, t2,
             or_ap=PW3r[:, m, :], oi_ap=PW3i[:, m, :], ar_ap=PW3r[:, m - 1, :], ai_ap=PW3i[:, m - 1, :],
             br_ap=PW3r[:, 1, :], bi_ap=PW3i[:, 1, :])
    ews(cx, ta, PWr, -1.0, ALU.add, a_ap=PW3r[:, 1, :])
    ew(cx, ALU.mult, den, LR, LR)
    ew(cx, ALU.mult, tb, LI, LI)
    ew(cx, ALU.add, den, den, tb)
    S.op("dve", lambda e: e.reciprocal(out=den.ap, in_=den.ap), reads=[den.tok], writes=[den.tok])
    ew(cx, ALU.mult, gr, ta, LR)
    ew(cx, ALU.mult, tb, PWi, LI, a_ap=PW3i[:, 1, :])
    ew(cx, ALU.add, gr, gr, tb)
    ew(cx, ALU.mult, gr, gr, den)
    ew(cx, ALU.mult, gi, PWi, LR, a_ap=PW3i[:, 1, :])
    ew(cx, ALU.mult, tb, ta, LI)
    ew(cx, ALU.subtract, gi, gi, tb)
    ew(cx, ALU.mult, gi, gi, den)
    s3 = (16, 16)
    cmul(cx, BBr, BBi, BR, BI, gr, gi, t1, t2, shape3=s3, or_ap=v3(BBr), oi_ap=v3(BBi),
         ar_ap=v3(BR), ai_ap=v3(BI), br_ap=bc(gr.ap), bi_ap=bc(gi.ap))

    def expand(dst_ap3, dst_buf, src_buf, neg=False):
        for gl in range(2):
            ps_ = slice(gl * 64, (gl + 1) * 64)
            o_ap = dst_ap3[ps_, :, gl * 16:(gl + 1) * 16]
            i_ap = v3(src_buf)[ps_, :, :]
            S.op("act", (lambda e, o_ap=o_ap, i_ap=i_ap: e.activation(out=o_ap, in_=i_ap, func=AF.Copy,
                                                                    scale=(-1.0 if neg else 1.0))),
                 reads=[src_buf.tok, dst_buf.tok], writes=[dst_buf.tok])

    for b_ in (WZ0, BZ, ZX[0], ZX[1], W["CW"]):
        S.op("pool", (lambda e, b_=b_: e.memset(b_.ap, 0.0)), writes=[b_.tok])
    WZ04 = WZ0.ap.rearrange("p (q r c) -> p q r c", q=16, r=2)
    BZ4 = BZ.ap.rearrange("p (q r c) -> p q r c", q=16, r=2)
    expand(BZ4[:, :, 0, :], BZ, BBr)
    expand(BZ4[:, :, 1, :], BZ, BBi)
    expand(WZ04[:, :, 0, :], WZ0, CR)
    expand(WZ04[:, :, 1, :], WZ0, CI, neg=True)
    for m in range(1, 9):
        cmul(cx, Wr, Wi, CR, CI, PWr, PWi, t1, t2, shape3=s3, or_ap=v3(Wr), oi_ap=v3(Wi),
             ar_ap=v3(CR), ai_ap=v3(CI), br_ap=bc(PW3r[:, m, :]), bi_ap=bc(PW3i[:, m, :]))
        expand(CW5[:, :, m - 1, 0, :], W["CW"], Wr)
        expand(CW5[:, :, m - 1, 1, :], W["CW"], Wi, neg=True)
    for k in range(4):
        for m in range(8):
            bank = (k * 8 + m) % 2
            pk = cx.ps[bank][:, 0:128]
            S.op("dve", (lambda e, pk=pk: e.memset(pk, 0.0)), writes=[f"ps{bank}"])
            for ql in range(4):
                q = 4 * k + ql
                for ri in range(2):
                    rhs = WZ04[:, q, ri, :] if m == 0 else CW5[:, q, m - 1, ri, :]
                    rtok = WZ0.tok if m == 0 else W["CW"].tok
                    S.op("pe", (lambda e, pk=pk, ql=ql, q=q, ri=ri, rhs=rhs: e.matmul(
                        pk[32 * ql:32 * ql + 32, 32 * ql:32 * ql + 32], lhsT=BZ4[:, q, ri, :], rhs=rhs,
                        start=(ri == 0), stop=(ri == 1), tile_position=(0, 32 * ql), skip_group_check=True)),
                        reads=[BZ.tok, rtok], writes=[f"ps{bank}"])
            S.op("act", (lambda e, pk=pk, k=k, m=m: e.activation(out=KB4[:, k, m, :], in_=pk, func=AF.Copy)),
                 reads=[f"ps{bank}"], writes=[W["KB"].tok])

    def xweights(dst5, dst_buf, nsig, pw_of_sigma):
        for sg_ in range(nsig):
            m = pw_of_sigma(sg_)
            z = ZX[sg_ % 2]
            z4 = z.ap.rearrange("p (r q c) -> p r q c", q=16, r=2)
            cmul(cx, Wr, Wi, BBr, BBi, PWr, PWi, t1, t2, shape3=s3, or_ap=v3(Wr), oi_ap=v3(Wi),
                 ar_ap=v3(BBr), ai_ap=v3(BBi), br_ap=bc(PW3r[:, m, :]), bi_ap=bc(PW3i[:, m, :]))
            expand(z4[:, 0, :, :], z, Wr)
            expand(z4[:, 1, :, :], z, Wi)
            bank = 2 + (sg_ % 2)
            pt = cx.ps[bank][:, :].bitcast(BF16).rearrange("p (k r c) -> p k r c", k=4, r=2)
            for k in range(4):
                for ri in range(2):
                    S.op("pe", (lambda e, pt=pt, k=k, ri=ri, z4=z4: e.transpose(
                        pt[:, k, ri, :], z4[:, ri, 4 * k:4 * k + 4, :], cx.ident)),
                        reads=[z.tok, "ident"], writes=[f"ps{bank}"])
            S.op("act", (lambda e, pt=pt, sg_=sg_: e.activation(out=dst5[:, :, sg_, :, :], in_=pt, func=AF.Copy)),
                 reads=[f"ps{bank}"], writes=[dst_buf.tok])

    xweights(XW5, W["XW"], 8, lambda s_: 7 - s_)
    xweights(XWs5, W["XWs"], 4, lambda s_: 3 - s_)
    S.op("dve", lambda e: e.tensor_copy(out=W["L8r"].ap, in_=PW3r[:, 8, :]), reads=[PWr.tok], writes=[W["L8r"].tok])
    S.op("dve", lambda e: e.tensor_copy(out=W["L8i"].ap, in_=PW3i[:, 8, :]), reads=[PWi.tok], writes=[W["L8i"].tok])
    S.op("dve", lambda e: e.tensor_copy(out=W["L4r"].ap, in_=PW3r[:, 4, :]), reads=[PWr.tok], writes=[W["L4r"].tok])
    S.op("dve", lambda e: e.tensor_copy(out=W["L4i"].ap, in_=PW3i[:, 4, :]), reads=[PWi.tok], writes=[W["L4i"].tok])
    act(cx, R8, Ar, AF.Exp, scale=8.0)
    RH3 = W["RHO"].ap.rearrange("p (q c) -> p q c", c=NBLK)
    S.op("dve", lambda e: e.tensor_copy(out=RH3, in_=R8.ap.rearrange("p (q o) -> p q o", o=1).to_broadcast([128, 16, NBLK])),
         reads=[R8.tok], writes=[W["RHO"].tok])
    S.op("pool", lambda e: e.memset(RH3[:, :, 0:1], 0.0), reads=[W["RHO"].tok], writes=[W["RHO"].tok])
    S.op("dve", lambda e: e.reciprocal(out=den.ap, in_=R8.ap), reads=[R8.tok, den.tok], writes=[den.tok])
    ew(cx, ALU.mult, Ekr, W["L8r"], den)
    ew(cx, ALU.mult, Eki, W["L8i"], den)
    ET3r = W["ETr"].ap.rearrange("p (q c) -> p q c", c=NBLK)
    ET3i = W["ETi"].ap.rearrange("p (q c) -> p q c", c=NBLK)
    S.op("pool", lambda e: e.memset(ET3r[:, :, 0:1], 1.0), writes=[W["ETr"].tok])
    S.op("pool", lambda e: e.memset(ET3i[:, :, 0:1], 0.0), writes=[W["ETi"].tok])
    kk = 1
    ek = (Ekr, Eki)
    ek2 = (Ek2r, Ek2i)
    while kk < NBLK:
        bcr = ek[0].ap.rearrange("p (q o) -> p q o", o=1).to_broadcast([128, 16, kk])
        bci = ek[1].ap.rearrange("p (q o) -> p q o", o=1).to_broadcast([128, 16, kk])
        cmul(cx, W["ETr"], W["ETi"], W["ETr"], W["ETi"], ek[0], ek[1], t1, t2, shape3=(16, kk),
             or_ap=ET3r[:, :, kk:2 * kk], oi_ap=ET3i[:, :, kk:2 * kk],
             ar_ap=ET3r[:, :, 0:kk], ai_ap=ET3i[:, :, 0:kk], br_ap=bcr, bi_ap=bci)
        cmul(cx, ek2[0], ek2[1], ek[0], ek[1], ek[0], ek[1], t1, t2)
        ek, ek2 = ek2, ek
        kk *= 2
    W["_keep"] = keep
    W["_dbg"] = dict(PWr=PWr, PWi=PWi, BBr=BBr, BBi=BBi, gr=gr, gi=gi)
    return W


POOL_W = (2, 4, 8, 16)


def mixer_ab_phase(cx, x_dram, T, ycatT, Pm, outs, samp, n_light=0, fix_tile=0, invc_dram=None):
    S, A = cx.S, cx.A
    S.barrier()
    A.reset()
    TT = 512
    NBLK = TT // 8
    NT = T // TT
    pfx = "m0"
    w_in = A.bf16(8 * D).rearrange("p (k f) -> p k f", f=D)
    w_glu = A.bf16(4 * 512).rearrange("p (k f) -> p k f", f=512)
    pw = A.bf16(4 * 128).rearrange("p (g f) -> p g f", f=128)
    g0_b = A.f32(D)
    dcol = A.f32(4)
    bglu = A.f32(4)
    pscale = A.f32(4)
    invc = A.f32(4 * 16).rearrange("p (g t) -> p g t", t=16)
    load_weight_bf16(cx, w_in, Pm["w_in"], 8, pfx + "w_in")
    load_weight_bf16(cx, w_glu, Pm["w_glu"], 4, pfx + "w_glu")
    for g in range(4):
        S.op("pool", (lambda e, g=g: e.dma_start(out=pw[:, g, :], in_=Pm["pool_w"][g])),
             writes=[pfx + "pw"], dma_tok=pfx + "pw")
    load_bcast_row(cx, g0_b, Pm["g0"], pfx + "g0")
    S.op("sp", lambda e: e.dma_start(out=dcol, in_=Pm["d"].rearrange("(k p) -> p k", p=128),
                                     allow_slow_non_contiguous=True), writes=[pfx + "dcol"], dma_tok=pfx + "dcol")
    S.op("sp", lambda e: e.dma_start(out=bglu, in_=Pm["b_glu"].rearrange("(k p) -> p k", p=128),
                                     allow_slow_non_contiguous=True), writes=[pfx + "bglu"], dma_tok=pfx + "bglu")
    S.op("sp", lambda e: e.dma_start(out=pscale, in_=Pm["pool_scale"].rearrange("(k p) -> p k", p=128),
                                     allow_slow_non_contiguous=True), writes=[pfx + "pscale"], dma_tok=pfx + "pscale")
    if invc_dram is None:
        for g in range(4):
            w = POOL_W[g]
            S.op("pool", (lambda e, g=g, w=w: e.memset(invc[:, g, :], 1.0 / w)), writes=[pfx + "invc"])
            for t in range(w - 1):
                S.op("pool", (lambda e, g=g, t=t: e.memset(invc[:, g, t:t + 1], 1.0 / (t + 1))),
                     reads=[pfx + "invc"], writes=[pfx + "invc"])
    else:
        S.op("sp", lambda e: e.dma_start(out=invc.rearrange("p g t -> p (g t)"), in_=invc_dram.partition_broadcast(128)),
             writes=[pfx + "invc"], dma_tok=pfx + "invc")
    invcw = A.f32(4 * 16).rearrange("p (g t) -> p g t", t=16)
    for g in range(4):
        S.op("pool", (lambda e, g=g: e.tensor_scalar(out=invcw[:, g, :], in0=invc[:, g, :], scalar1=float(POOL_W[g]), scalar2=None,
                                                   op0=ALU.mult)), reads=[pfx + "invc"], writes=[pfx + "invc"])
    if K_STAGE <= -2:
        return None
    Wt = s5_precompute(cx, Pm, NBLK, 4)
    if K_STAGE <= -1:
        return None
    S.barrier()
    A.off = Wt["_keep"]
    KB4 = Wt["KB"].ap.rearrange("p (k m c) -> p k m c", k=4, m=8)
    XW5 = Wt["XW"].ap.rearrange("p (k s r c) -> p k s r c", k=4, s=8, r=2)
    XWs5 = Wt["XWs"].ap.rearrange("p (k s r c) -> p k s r c", k=4, s=4, r=2)
    CW5 = Wt["CW"].ap.rearrange("p (q t r c) -> p q t r c", q=16, t=8, r=2)
    ETr = Wt["ETr"].ap
    ETi = Wt["ETi"].ap
    RHO = Wt["RHO"].ap
    wtoks = [Wt[n].tok for n in ("KB", "XW", "XWs", "CW", "ETr", "ETi", "RHO", "L8r", "L8i", "L4r", "L4i")]
    hld = [A.f32(D) for _ in range(2)]
    xn = [A.bf16(D) for _ in range(2)]
    xnT = A.bf16(8 * TT).rearrange("p (k t) -> p k t", t=TT)
    uP = A.f32(4 * (16 + TT)).rearrange("p (g t) -> p g t", t=16 + TT)
    uSbb = [A.bf16(4 * TT).rearrange("p (k t) -> p k t", t=TT) for _ in range(2)]
    pa = A.f32(16 + TT)
    pb = A.f32(16 + TT)
    dT = A.bf16(4 * TT).rearrange("p (g t) -> p g t", t=TT)
    NE = 16 * NBLK
    Xr, Xi = A.f32(NE), A.f32(NE)
    Mr, Mi = A.f32(NE), A.f32(NE)
    Tt = A.f32(NE)
    junk = Tt[:, 0:D // 2].bitcast(BF16)
    Hr = A.f32(16 * (NBLK + 1))
    Hi = A.f32(16 * (NBLK + 1))
    Hb = A.bf16(2 * NE).rearrange("p (r q c) -> p r q c", r=2, q=16)
    cr_ = A.f32(16)
    ci_ = A.f32(16)
    ct_ = A.f32(16)
    ysb = [A.f32(4 * TT).rearrange("p (k t) -> p k t", t=TT) for _ in range(2)]
    zb = A.bf16(4 * TT).rearrange("p (k t) -> p k t", t=TT)
    sig = A.f32(TT)
    ycatb = [A.bf16(8 * TT).rearrange("p (k t) -> p k t", t=TT) for _ in range(2)]
    ssq = A.f32(4)
    rstd = A.f32(4)
    Hr3 = Hr.rearrange("p (q c) -> p q c", c=NBLK + 1)
    Hi3 = Hi.rearrange("p (q c) -> p q c", c=NBLK + 1)
    X3r = Xr.rearrange("p (q c) -> p q c", c=NBLK)
    X3i = Xi.rearrange("p (q c) -> p q c", c=NBLK)
    M3r = Mr.rearrange("p (q c) -> p q c", c=NBLK)
    M3i = Mi.rearrange("p (q c) -> p q c", c=NBLK)
    T3 = Tt.rearrange("p (q c) -> p q c", c=NBLK)
    ycatT_v = ycatT.rearrange("(k p) t -> p k t", p=128)
    tk = lambda n: pfx + n
    S.op("pool", lambda e: e.memset(Hr, 0.0), writes=[tk("H")])
    S.op("pool", lambda e: e.memset(Hi, 0.0), writes=[tk("H")])
    S.op("pool", lambda e: e.memset(uP, 0.0), writes=[tk("uP")])

    def tt_op(eng, op, o, a, b, reads, writes):
        S.op(eng, lambda e: e.tensor_tensor(out=o, in0=a, in1=b, op=op), reads=reads, writes=writes)

    def prenorm_tile(src_rows, nsub, ncols):
        for s in range(nsub):
            hb = hld[s % 2]
            htok = tk(f"hld{s % 2}")
            xb = xn[s % 2]
            xtok = tk(f"xn{s % 2}")
            src = src_rows(s)
            S.op("sp", (lambda e, hb=hb, src=src: e.dma_start(out=hb[0:src.shape[0], :], in_=src)),
                 writes=[htok], dma_tok=htok)
            S.op("act", (lambda e, hb=hb, s=s: e.activation(out=junk, in_=hb, func=AF.Square,
                                                            accum_out=ssq[:, s:s + 1])),
                 reads=[htok], writes=[tk("T"), tk("ssq")])
            S.op("act", (lambda e, s=s: e.activation(out=rstd[:, s:s + 1], in_=ssq[:, s:s + 1], func=AF.Sqrt,
                                                     scale=1.0 / D, bias=cx.eps_t)),
                 reads=[tk("ssq"), "eps"], writes=[tk("rstd")])
            S.op("dve", (lambda e, s=s: e.reciprocal(out=rstd[:, s:s + 1], in_=rstd[:, s:s + 1])),
                 reads=[tk("rstd")], writes=[tk("rstd")])
            S.op("dve", (lambda e, hb=hb, xb=xb, s=s: e.scalar_tensor_tensor(
                out=xb, in0=hb, scalar=rstd[:, s:s + 1], in1=g0_b, op0=ALU.mult, op1=ALU.mult)),
                reads=[htok, tk("rstd"), tk("g0")], writes=[xtok])
            bank = 4 + (s % 2)
            pt = cx.ps[bank][:, :].bitcast(BF16).rearrange("p (k c) -> p k c", c=128)
            for kc in range(8):
                S.op("pe", (lambda e, pt=pt, xb=xb, kc=kc: e.transpose(pt[:, kc, :], xb[:, kc * 128:(kc + 1) * 128],
                                                                       cx.ident)),
                     reads=[xtok, "ident"], writes=[f"ps{bank}"])
            S.op("act", (lambda e, pt=pt, s=s: e.activation(out=xnT[:, :, s * 128:(s + 1) * 128], in_=pt,
                                                            func=AF.Copy)),
                 reads=[f"ps{bank}"], writes=[tk("xnT")])

    def in_proj(ncols, uP_dst, uS_dst, uSb_dst, ytok=None, btok=None):
        for oc in range(8):
            bank = oc % 4
            for kc in range(8):
                S.op("pe", (lambda e, oc=oc, kc=kc, bank=bank: e.matmul(
                    cx.ps[bank][:, 0:ncols], lhsT=w_in[:, kc, oc * 128:(oc + 1) * 128], rhs=xnT[:, kc, 0:ncols],
                    start=(kc == 0), stop=(kc == 7))),
                    reads=[tk("xnT"), tk("w_in")], writes=[f"ps{bank}"])
            if K_STAGE <= 0.6:
                continue
            if oc < 4:
                dst = uP_dst(oc)
                S.op("act", (lambda e, bank=bank, dst=dst: e.activation(out=dst, in_=cx.ps[bank][:, 0:ncols],
                                                                        func=AF.Copy)),
                     reads=[f"ps{bank}", tk("uP")], writes=[tk("uP")])
            else:
                d1 = uS_dst(oc - 4)
                d2 = uSb_dst(oc - 4)
                S.op("act", (lambda e, bank=bank, d1=d1, oc=oc: e.activation(out=d1, in_=cx.ps[bank][:, 0:ncols],
                                                                             func=AF.Copy, scale=dcol[:, oc - 4:oc - 3])),
                     reads=[f"ps{bank}", tk("dcol"), ytok], writes=[ytok])
                S.op("act", (lambda e, bank=bank, d2=d2: e.activation(out=d2, in_=cx.ps[bank][:, 0:ncols], func=AF.Copy)),
                     reads=[f"ps{bank}", btok], writes=[btok])

    def glu_and_store(ncols, col0, ys, ystok, ycat, yctok):
        S.op("act", lambda e: e.activation(out=ys[:, :, 0:ncols], in_=ys[:, :, 0:ncols], func=AF.Gelu_apprx_tanh),
             reads=[ystok], writes=[ystok])
        S.op("act", lambda e: e.activation(out=zb[:, :, 0:ncols], in_=ys[:, :, 0:ncols], func=AF.Copy),
             reads=[ystok], writes=[tk("zb")])
        for oc in range(4):
            bank = oc % 4
            for kc in range(4):
                S.op("pe", (lambda e, oc=oc, kc=kc, bank=bank: e.matmul(
                    cx.ps[bank][:, 0:ncols], lhsT=w_glu[:, kc, oc * 128:(oc + 1) * 128], rhs=zb[:, kc, 0:ncols],
                    start=(kc == 0), stop=(kc == 3))),
                    reads=[tk("zb"), tk("w_glu")], writes=[f"ps{bank}"])
            S.op("act", (lambda e, oc=oc, bank=bank: e.activation(out=sig[:, 0:ncols], in_=cx.ps[bank][:, 0:ncols],
                                                                  func=AF.Sigmoid, bias=bglu[:, oc:oc + 1])),
                 reads=[f"ps{bank}", tk("bglu")], writes=[tk("sig")])
            S.op("dve", (lambda e, oc=oc: e.tensor_tensor(out=ycat[:, 4 + oc, 0:ncols], in0=ys[:, oc, 0:ncols],
                                                          in1=sig[:, 0:ncols], op=ALU.mult)),
                 reads=[ystok, tk("sig")], writes=[yctok])
        S.op("sp", lambda e: e.dma_start(out=ycatT_v[:, :, col0:col0 + ncols], in_=ycat[:, :, 0:ncols]),
             reads=[yctok], writes=[f"ycatT@{col0}"], dma_tok=yctok)

    def pool_matmul(ncols, ycat, yctok):
        for g in range(4):
            bank = g % 4
            S.op("pe", (lambda e, g=g, bank=bank: e.matmul(cx.ps[bank][:, 0:ncols], lhsT=pw[:, g, :],
                                                          rhs=dT[:, g, 0:ncols], start=True, stop=True)),
                 reads=[tk("dT"), tk("pw")], writes=[f"ps{bank}"])
            S.op("act", (lambda e, g=g, bank=bank: e.activation(out=ycat[:, g, 0:ncols], in_=cx.ps[bank][:, 0:ncols],
                                                                func=AF.Copy, scale=pscale[:, g:g + 1])),
                 reads=[f"ps{bank}", tk("pscale")], writes=[yctok])

    def stage1a(ti):
        t0 = ti * TT
        prenorm_tile(lambda s: x_dram[t0 + s * 128:t0 + (s + 1) * 128, :], 4, TT)

    def stage1b(ti):
        t0 = ti * TT
        par = ti % 2
        ys, uSb = ysb[par], uSbb[par]
        ycat = ycatb[par]
        in_proj(TT, lambda g: uP[:, g, 16:16 + TT], lambda k: ys[:, k, :], lambda k: uSb[:, k, :], ytok=tk(f"ys{par}"), btok=tk(f"uSb{par}"))
        light = ti < n_light
        for g in range(4):
            if light:
                break
            cur = uP[:, g, :]
            ctoks = [tk("uP")]
            sh = 1
            for step in range(g + 1):
                dst = pa if step % 2 == 0 else pb
                dtok = tk("pa") if step % 2 == 0 else tk("pb")
                lo = 2 * sh - 1
                S.op("pool", (lambda e, dst=dst, cur=cur, lo=lo, sh=sh: e.tensor_tensor(
                    out=dst[:, lo:16 + TT], in0=cur[:, lo:16 + TT], in1=cur[:, lo - sh:16 + TT - sh], op=ALU.add)),
                    reads=ctoks, writes=[dtok])
                cur, ctoks = dst, [dtok]
                sh *= 2
            w = POOL_W[g]
            S.op("pool", (lambda e, cur=cur, w=w: e.tensor_scalar(out=cur[:, 16:16 + TT], in0=cur[:, 16:16 + TT],
                                                                  scalar1=1.0 / w, scalar2=None, op0=ALU.mult)),
                 reads=ctoks, writes=ctoks)
            S.op("pool", (lambda e, g=g, cur=cur: e.tensor_tensor(out=dT[:, g, :], in0=cur[:, 16:16 + TT],
                                                                  in1=uP[:, g, 16:16 + TT], op=ALU.subtract)),
                 reads=ctoks + [tk("uP")], writes=[tk("dT")])
            if ti == fix_tile:
                S.op("pool", (lambda e, g=g, cur=cur, w=w: e.scalar_tensor_tensor(out=sig[:, 0:16], in0=cur[:, 16:32], scalar=float(w),
                                                                                  in1=invc[:, g, :], op0=ALU.mult, op1=ALU.mult))
                     if False else (lambda e, g=g, cur=cur: e.tensor_tensor(out=sig[:, 0:16], in0=cur[:, 16:32],
                                                                            in1=invcw[:, g, :], op=ALU.mult)),
                     reads=ctoks + [tk("invc")], writes=[tk("sig")])
                S.op("pool", (lambda e, g=g: e.tensor_tensor(out=dT[:, g, 0:16], in0=sig[:, 0:16],
                                                            in1=uP[:, g, 16:32], op=ALU.subtract)),
                     reads=[tk("sig"), tk("uP"), tk("dT")], writes=[tk("dT")])
        if ti == NT - 1:
            for g in range(4):
                S.op("sp", (lambda e, g=g: e.dma_start(
                    out=outs["pool_prompt"][:, g * 128:(g + 1) * 128].rearrange("t c -> c t"),
                    in_=uP[:, g, TT + 1:TT + 16], allow_slow_non_contiguous=True)),
                    reads=[tk("uP")], dma_tok=tk("pp_out"))
        S.op("pool", lambda e: e.tensor_copy(out=uP[:, :, 0:16], in_=uP[:, :, TT:TT + 16]),
             reads=[tk("uP")], writes=[tk("uP")])
        uSb4 = uSb.rearrange("p k (c s) -> p k c s", s=8)
        for q in range(16):
            k, ql = q // 4, q % 4
            for ri in range(2):
                bank = ri * 2 + (q // 8)
                outp = cx.ps[bank][:, (q % 8) * NBLK:(q % 8 + 1) * NBLK]
                for sg_ in range(8):
                    S.op("pe", (lambda e, k=k, ql=ql, ri=ri, sg_=sg_, outp=outp: e.matmul(
                        outp, lhsT=XW5[32 * ql:32 * ql + 32, k, sg_, ri, :], rhs=uSb4[32 * ql:32 * ql + 32, k, :, sg_],
                        start=(sg_ == 0), stop=(sg_ == 7), tile_position=(32 * ql, 0))),
                        reads=[tk(f"uSb{par}")] + wtoks, writes=[f"ps{bank}"])
        for ri, X_ in ((0, Xr), (1, Xi)):
            for hf in range(2):
                bank = ri * 2 + hf
                S.op("act", (lambda e, X_=X_, hf=hf, bank=bank: e.activation(
                    out=X_[:, hf * 512:(hf + 1) * 512], in_=cx.ps[bank][:, :], func=AF.Copy)),
                    reads=[f"ps{bank}"], writes=[tk("X")])

    def stage2(ti):
        t0 = ti * TT
        hpr, hpi = Hr3[:, :, NBLK], Hi3[:, :, NBLK]
        L8r, L8i = Wt["L8r"].ap, Wt["L8i"].ap
        tt_op("dve", ALU.mult, cr_, L8r, hpr, [tk("H")] + wtoks, [tk("c")])
        tt_op("dve", ALU.mult, ct_, L8i, hpi, [tk("H")] + wtoks, [tk("ct")])
        tt_op("dve", ALU.subtract, cr_, cr_, ct_, [tk("c"), tk("ct")], [tk("c")])
        tt_op("dve", ALU.mult, ci_, L8r, hpi, [tk("H")] + wtoks, [tk("ci")])
        tt_op("dve", ALU.mult, ct_, L8i, hpr, [tk("H")] + wtoks + [tk("c")], [tk("ct")])
        tt_op("dve", ALU.add, ci_, ci_, ct_, [tk("ci"), tk("ct")], [tk("ci")])
        tt_op("dve", ALU.add, X3r[:, :, 0], X3r[:, :, 0], cr_, [tk("X"), tk("c")], [tk("X")])
        tt_op("dve", ALU.add, X3i[:, :, 0], X3i[:, :, 0], ci_, [tk("X"), tk("ci")], [tk("X")])
        S.op("dve", lambda e: e.tensor_copy(out=Hr3[:, :, 0], in_=hpr), reads=[tk("H"), tk("Hb")], writes=[tk("H")])
        S.op("dve", lambda e: e.tensor_copy(out=Hi3[:, :, 0], in_=hpi), reads=[tk("H"), tk("Hb")], writes=[tk("H")])
        tt_op("dve", ALU.mult, Mr, Xr, ETr, [tk("X")] + wtoks, [tk("M")])
        tt_op("dve", ALU.mult, Tt, Xi, ETi, [tk("X")] + wtoks, [tk("T")])
        tt_op("dve", ALU.add, Mr, Mr, Tt, [tk("M"), tk("T")], [tk("M")])
        tt_op("dve", ALU.mult, Mi, Xi, ETr, [tk("X")] + wtoks, [tk("M")])
        tt_op("dve", ALU.mult, Tt, Xr, ETi, [tk("X"), tk("M")] + wtoks, [tk("T")])
        tt_op("dve", ALU.subtract, Mi, Mi, Tt, [tk("M"), tk("T")], [tk("M")])
        S.op("dve", lambda e: e.tensor_tensor_scan(out=Xr, data0=RHO, data1=Mr, initial=0.0, op0=ALU.mult, op1=ALU.add),
             reads=[tk("M"), tk("X")] + wtoks, writes=[tk("X")])
        S.op("dve", lambda e: e.tensor_tensor_scan(out=Xi, data0=RHO, data1=Mi, initial=0.0, op0=ALU.mult, op1=ALU.add),
             reads=[tk("M"), tk("X")] + wtoks, writes=[tk("X")])
        tt_op("dve", ALU.mult, Mr, Xr, ETr, [tk("X"), tk("M")] + wtoks, [tk("M")])
        tt_op("dve", ALU.mult, Tt, Xi, ETi, [tk("X"), tk("T")] + wtoks, [tk("T")])
        tt_op("dve", ALU.subtract, Hr3[:, :, 1:NBLK + 1], M3r, T3, [tk("M"), tk("T"), tk("H"), tk("c"), tk("ci")], [tk("H")])
        tt_op("dve", ALU.mult, Mi, Xi, ETr, [tk("X"), tk("M")] + wtoks, [tk("M")])
        tt_op("dve", ALU.mult, Tt, Xr, ETi, [tk("X"), tk("T"), tk("H")] + wtoks, [tk("T")])
        tt_op("dve", ALU.add, Hi3[:, :, 1:NBLK + 1], M3i, T3, [tk("M"), tk("T"), tk("H")], [tk("H")])
        S.op("act", lambda e: e.activation(out=Hb[:, 0, :, :], in_=Hr3[:, :, 0:NBLK], func=AF.Copy),
             reads=[tk("H")], writes=[tk("Hb")])
        S.op("act", lambda e: e.activation(out=Hb[:, 1, :, :], in_=Hi3[:, :, 0:NBLK], func=AF.Copy),
             reads=[tk("H")], writes=[tk("Hb")])
        if ti == NT - 1:
            for gl in range(2):
                for ri, H3 in ((0, Hr3), (1, Hi3)):
                    S.op("sp", (lambda e, gl=gl, ri=ri, H3=H3: e.dma_start(
                        out=outs["s5_prompt"].rearrange("(q gl) p r -> gl p q r", gl=2)[gl][:, :, ri],
                        in_=H3[gl * 64:(gl + 1) * 64, :, NBLK], allow_slow_non_contiguous=True)),
                        reads=[tk("H")], dma_tok=tk("s5_out"))

    def stage3(ti):
        t0 = ti * TT
        par = ti % 2
        ys, uSb = ysb[par], uSbb[par]
        ycat = ycatb[par]
        uSb4 = uSb.rearrange("p k (c s) -> p k c s", s=8)
        for k in range(4):
            bank = 4 + k
            yv = cx.ps[bank][:, :].rearrange("p (c s) -> p c s", s=8)
            first = [True]
            for tau in range(8):
                for sg_ in range(tau + 1):
                    st = first[0]
                    first[0] = False
                    S.op("pe", (lambda e, k=k, tau=tau, sg_=sg_, yv=yv, st=st: e.matmul(
                        yv[:, :, tau], lhsT=KB4[:, k, tau - sg_, :], rhs=uSb4[:, k, :, sg_],
                        start=st, stop=False, skip_group_check=True)),
                        reads=[tk(f"uSb{par}")] + wtoks, writes=[f"ps{bank}"])
                for ql in range(4):
                    q = 4 * k + ql
                    for ri in range(2):
                        S.op("pe", (lambda e, q=q, ql=ql, tau=tau, ri=ri, yv=yv: e.matmul(
                            yv[32 * ql:32 * ql + 32, :, tau], lhsT=CW5[:, q, tau, ri, :], rhs=Hb[:, ri, q, :],
                            start=False, stop=(ri == 1 and ql == 3), tile_position=(0, 32 * ql),
                            skip_group_check=True)),
                            reads=[tk("Hb")] + wtoks, writes=[f"ps{bank}"])
            S.op("dve", (lambda e, k=k, bank=bank: e.tensor_tensor(out=ys[:, k, :], in0=cx.ps[bank][:, :], in1=ys[:, k, :],
                                                               op=ALU.add)),
                 reads=[f"ps{bank}", tk(f"ys{par}")], writes=[tk(f"ys{par}")])
        glu_and_store(TT, t0, ys, tk(f"ys{par}"), ycat, tk(f"ycat{par}"))
    def sample_mixer():
        NBS = 4
        ys, uSb, ycat = ysb[0], uSbb[0], ycatb[0]
        S.op("pool", lambda e: e.memset(ycat[:, :, 0:128], 0.0), reads=[tk("ycat0")], writes=[tk("ycat0")])
        S.op("pool", lambda e: e.memset(ys[:, :, 0:128], 0.0), reads=[tk("ys0")], writes=[tk("ys0")])
        prenorm_tile(lambda s: samp["x_s"], 1, 128)
        uPs = pa[:, 0:4 * NBS * 20]
        uPs4 = uPs.rearrange("p (g b t) -> p g b t", g=4, b=NBS)
        pbs = pb[:, 0:NBS * 20].rearrange("p (b t) -> p b t", t=20)
        pcs = sig[:, 0:NBS * 20].rearrange("p (b t) -> p b t", t=20)
        S.op("pool", lambda e: e.memset(uPs, 0.0), reads=[tk("pa")], writes=[tk("pa")])
        for b in range(NBS):
            for g in range(4):
                S.op("sp", (lambda e, b=b, g=g: e.dma_start(
                    out=uPs4[:, g, b, 1:16], in_=samp["state_pool"][b, :, g * 128:(g + 1) * 128].rearrange("t c -> c t"),
                    allow_slow_non_contiguous=True)), reads=[tk("pa")], writes=[tk("pa")], dma_tok=tk("spl"))
        for oc in range(8):
            bank = oc % 4
            for kc in range(8):
                S.op("pe", (lambda e, oc=oc, kc=kc, bank=bank: e.matmul(
                    cx.ps[bank][:, 0:16], lhsT=w_in[:, kc, oc * 128:(oc + 1) * 128], rhs=xnT[:, kc, 0:16],
                    start=(kc == 0), stop=(kc == 7))), reads=[tk("xnT"), tk("w_in")], writes=[f"ps{bank}"])
            if oc < 4:
                S.op("act", (lambda e, oc=oc, bank=bank: e.activation(
                    out=uPs4[:, oc, :, 16:20], in_=cx.ps[bank][:, 0:16].rearrange("p (b t) -> p b t", t=4), func=AF.Copy)),
                    reads=[f"ps{bank}", tk("pa")], writes=[tk("pa")])
            else:
                S.op("act", (lambda e, oc=oc, bank=bank: e.activation(out=ys[:, oc - 4, 0:16], in_=cx.ps[bank][:, 0:16],
                                                                      func=AF.Copy, scale=dcol[:, oc - 4:oc - 3])),
                     reads=[f"ps{bank}", tk("ys0"), tk("dcol")], writes=[tk("ys0")])
                S.op("act", (lambda e, oc=oc, bank=bank: e.activation(out=uSb[:, oc - 4, 0:16], in_=cx.ps[bank][:, 0:16],
                                                                      func=AF.Copy)),
                     reads=[f"ps{bank}", tk("uSb0")], writes=[tk("uSb0")])
        for g in range(4):
            cur = uPs4[:, g, :, :]
            ctoks = [tk("pa")]
            sh = 1
            for step in range(g + 1):
                dst = pbs if step % 2 == 0 else pcs
                dtok = tk("pb") if step % 2 == 0 else tk("sig")
                lo = 2 * sh - 1
                S.op("pool", (lambda e, dst=dst, cur=cur, lo=lo, sh=sh: e.tensor_tensor(
                    out=dst[:, :, lo:20], in0=cur[:, :, lo:20], in1=cur[:, :, lo - sh:20 - sh], op=ALU.add)),
                    reads=ctoks + [dtok], writes=[dtok])
                cur, ctoks = dst, [dtok]
                sh *= 2
            w = POOL_W[g]
            S.op("dve", (lambda e, g=g, cur=cur, w=w: e.scalar_tensor_tensor(
                out=dT[:, g, 0:16].rearrange("p (b t) -> p b t", t=4), in0=cur[:, :, 16:20], scalar=1.0 / w,
                in1=uPs4[:, g, :, 16:20], op0=ALU.mult, op1=ALU.subtract)),
                reads=ctoks + [tk("pa"), tk("dT")], writes=[tk("dT")])
        pool_matmul(16, ycat, tk("ycat0"))
        for b in range(NBS):
            for g in range(4):
                S.op("sp", (lambda e, b=b, g=g: e.dma_start(
                    out=samp["pool_out"][b, :, g * 128:(g + 1) * 128].rearrange("t c -> c t"), in_=uPs4[:, g, b, 5:20],
                    allow_slow_non_contiguous=True)), reads=[tk("pa")], dma_tok=tk("pso"))
        H0r = Hr[:, 0:16 * NBS].rearrange("p (q b) -> p q b", b=NBS)
        H0i = Hi[:, 0:16 * NBS].rearrange("p (q b) -> p q b", b=NBS)
        Hbs = Hb.rearrange("p r q c -> p (r q c)")[:, 0:2 * 16 * NBS].rearrange("p (r q b) -> p r q b", r=2, q=16)
        for b in range(NBS):
            for gl in range(2):
                for ri, H0 in ((0, H0r), (1, H0i)):
                    S.op("sp", (lambda e, b=b, gl=gl, ri=ri, H0=H0: e.dma_start(
                        out=H0[gl * 64:(gl + 1) * 64, :, b],
                        in_=samp["state_s5"][b].rearrange("(q gl) p r -> gl p q r", gl=2)[gl][:, :, ri],
                        allow_slow_non_contiguous=True)), reads=[tk("H"), tk("Hb")], writes=[tk("H")], dma_tok=tk("s5l"))
        S.op("act", lambda e: e.activation(out=Hbs[:, 0, :, :], in_=H0r, func=AF.Copy), reads=[tk("H"), tk("Hb")], writes=[tk("Hb")])
        S.op("act", lambda e: e.activation(out=Hbs[:, 1, :, :], in_=H0i, func=AF.Copy), reads=[tk("H"), tk("Hb")], writes=[tk("Hb")])
        uSb4s = uSb[:, :, 0:16].rearrange("p k (b s) -> p k b s", s=4)
        for q in range(16):
            k, ql = q // 4, q % 4
            for ri in range(2):
                bank = 2 * ri
                for sg_ in range(4):
                    S.op("pe", (lambda e, k=k, ql=ql, q=q, ri=ri, sg_=sg_, bank=bank: e.matmul(
                        cx.ps[bank][:, q * NBS:(q + 1) * NBS], lhsT=XWs5[32 * ql:32 * ql + 32, k, sg_, ri, :],
                        rhs=uSb4s[32 * ql:32 * ql + 32, k, :, sg_], start=(sg_ == 0), stop=(sg_ == 3),
                        tile_position=(32 * ql, 0))), reads=[tk("uSb0")] + wtoks, writes=[f"ps{bank}"])
        Xs_r = Xr[:, 0:16 * NBS].rearrange("p (q b) -> p q b", b=NBS)
        Xs_i = Xi[:, 0:16 * NBS].rearrange("p (q b) -> p q b", b=NBS)
        S.op("act", lambda e: e.activation(out=Xr[:, 0:16 * NBS], in_=cx.ps[0][:, 0:16 * NBS], func=AF.Copy),
             reads=["ps0", tk("X")], writes=[tk("X")])
        S.op("act", lambda e: e.activation(out=Xi[:, 0:16 * NBS], in_=cx.ps[2][:, 0:16 * NBS], func=AF.Copy),
             reads=["ps2", tk("X")], writes=[tk("X")])
        bc4 = lambda ap16: ap16.rearrange("p (q o) -> p q o", o=1).to_broadcast([128, 16, NBS])
        L4r, L4i = bc4(Wt["L4r"].ap), bc4(Wt["L4i"].ap)
        M4r = Mr[:, 0:16 * NBS].rearrange("p (q b) -> p q b", b=NBS)
        M4i = Mi[:, 0:16 * NBS].rearrange("p (q b) -> p q b", b=NBS)
        T4 = Tt[:, 0:16 * NBS].rearrange("p (q b) -> p q b", b=NBS)
        tt_op("dve", ALU.mult, M4r, H0r, L4r, [tk("H"), tk("M")] + wtoks, [tk("M")])
        tt_op("dve", ALU.mult, T4, H0i, L4i, [tk("H"), tk("T")] + wtoks, [tk("T")])
        tt_op("dve", ALU.subtract, M4r, M4r, T4, [tk("M"), tk("T")], [tk("M")])
        tt_op("dve", ALU.add, Xs_r, Xs_r, M4r, [tk("M"), tk("X")], [tk("X")])
        tt_op("dve", ALU.mult, M4i, H0r, L4i, [tk("H"), tk("M")] + wtoks, [tk("M")])
        tt_op("dve", ALU.mult, T4, H0i, L4r, [tk("H"), tk("T"), tk("M")] + wtoks, [tk("T")])
        tt_op("dve", ALU.add, M4i, M4i, T4, [tk("M"), tk("T")], [tk("M")])
        tt_op("dve", ALU.add, Xs_i, Xs_i, M4i, [tk("M"), tk("X")], [tk("X")])
        for b in range(NBS):
            for gl in range(2):
                for ri, X_ in ((0, Xs_r), (1, Xs_i)):
                    S.op("sp", (lambda e, b=b, gl=gl, ri=ri, X_=X_: e.dma_start(
                        out=samp["s5_out"][b].rearrange("(q gl) p r -> gl p q r", gl=2)[gl][:, :, ri],
                        in_=X_[gl * 64:(gl + 1) * 64, :, b], allow_slow_non_contiguous=True)),
                        reads=[tk("X")], dma_tok=tk("s5so"))
        for k in range(4):
            bank = 4 + k
            yv = cx.ps[bank][:, 0:16].rearrange("p (b s) -> p b s", s=4)
            first = [True]
            for tau in range(4):
                for sg_ in range(tau + 1):
                    st = first[0]
                    first[0] = False
                    S.op("pe", (lambda e, k=k, tau=tau, sg_=sg_, yv=yv, st=st: e.matmul(
                        yv[:, :, tau], lhsT=KB4[:, k, tau - sg_, :], rhs=uSb4s[:, k, :, sg_],
                        start=st, stop=False, skip_group_check=True)),
                        reads=[tk("uSb0")] + wtoks, writes=[f"ps{bank}"])
                for ql in range(4):
                    q = 4 * k + ql
                    for ri in range(2):
                        S.op("pe", (lambda e, q=q, ql=ql, tau=tau, ri=ri, yv=yv: e.matmul(
                            yv[32 * ql:32 * ql + 32, :, tau], lhsT=CW5[:, q, tau, ri, :], rhs=Hbs[:, ri, q, :],
                            start=False, stop=(ri == 1 and ql == 3), tile_position=(0, 32 * ql),
                            skip_group_check=True)),
                            reads=[tk("Hb")] + wtoks, writes=[f"ps{bank}"])
            S.op("dve", (lambda e, k=k, bank=bank: e.tensor_tensor(out=ys[:, k, 0:16], in0=cx.ps[bank][:, 0:16],
                                                               in1=ys[:, k, 0:16], op=ALU.add)),
                 reads=[f"ps{bank}", tk("ys0")], writes=[tk("ys0")])
        glu_and_store(128, T, ys, tk("ys0"), ycat, tk("ycat0"))

    def stage1c(ti):
        if ti >= n_light:
            par = ti % 2
            pool_matmul(TT, ycatb[par], tk(f"ycat{par}"))

    stage1a(0)
    stage1b(0)
    stage1c(0)
    for ti in range(NT):
        if ti + 1 < NT:
            stage1a(ti + 1)
        stage2(ti)
        if ti + 1 < NT:
            stage1b(ti + 1)
        if ti >= n_light:
            stage3(ti)
        if ti + 1 < NT:
            stage1c(ti + 1)
    if samp is not None:
        sample_mixer()
    return None


def proj_res_phase(cx, aT, tiles, W_dram, g_post, pfx, row_perm=None):
    S, A = cx.S, cx.A
    S.barrier()
    A.reset()
    Wb = A.bf16(8 * D).rearrange("p (k f) -> p k f", f=D)
    g_b = A.f32(D)
    load_weight_bf16(cx, Wb, W_dram, 8, pfx + "W")
    load_bcast_row(cx, g_b, g_post, pfx + "g")
    NS = max(t[1] for t in tiles)
    TT = NS * 128
    NB = 2
    abuf = [A.bf16(8 * TT).rearrange("p (k t) -> p k t", t=TT) for _ in range(NB)]
    hbuf = [A.f32(D) for _ in range(3)]
    junk = A.bf16(D)
    tmp = A.f32(D)
    ssq2 = A.f32(4)
    rstd2 = A.f32(1)
    aT_v = aT.rearrange("(k p) t -> p k t", p=128)
    cnt = [0]

    def tile_body(ti, col0, nsub, h_src, h_dst, a_name, hsrc_name, dst_name, row0):
        ab = abuf[ti % NB]
        atok = f"{pfx}a{ti % NB}"
        nc_ = nsub * 128
        S.op("sp", lambda e: e.dma_start(out=ab[:, :, 0:nc_], in_=aT_v[:, :, col0:col0 + nc_]),
             reads=[f"{a_name}@{col0}"], writes=[atok], dma_tok=atok)
        for s in range(nsub):
            i = cnt[0] % 3
            cnt[0] += 1
            hb = hbuf[i]
            htok = f"{pfx}h{i}"
            S.op("sp", (lambda e, hb=hb, s=s: e.dma_start(out=hb, in_=h_src[s * 128:(s + 1) * 128, :])),
                 reads=[f"{hsrc_name}@{row0 + 128 * s}"], writes=[htok], dma_tok=htok)
            halves = []
            for hf in range(2):
                bank = 2 * (s % 2) + hf
                for kc in range(8):
                    S.op("pe", (lambda e, s=s, hf=hf, kc=kc, bank=bank: e.matmul(
                        cx.ps[bank][:, :], lhsT=ab[:, kc, s * 128:(s + 1) * 128],
                        rhs=Wb[:, kc, hf * 512:(hf + 1) * 512], start=(kc == 0), stop=(kc == 7))),
                        reads=[atok, pfx + "W"], writes=[f"ps{bank}"])
                halves.append((bank, cx.ps[bank][:, :]))
            n2 = dict(junk=pfx + "junk", ssq2=pfx + "ssq2", rstd2=pfx + "rstd2", tmp=pfx + "tmp", hout=htok)
            post_norm_residual(cx, halves, hb, g_b, pfx + "g", ssq2, rstd2, junk, tmp, hb, [htok], n2)
            S.op("pool", (lambda e, hb=hb, s=s: e.dma_start(out=h_dst[s * 128:(s + 1) * 128, :], in_=hb)),
                 reads=[htok], writes=[f"{dst_name}@{row0 + 128 * s}"], dma_tok=htok)

    for ti, t in enumerate(tiles):
        tile_body(ti, *t)


def qkv_phase(cx, hsrc, hname, tiles, w_qkv, g_pre, QT, KT, V, pfx="q1"):
    S, A = cx.S, cx.A
    S.barrier()
    A.reset()
    W3 = A.bf16(8 * 3 * D).rearrange("p (k f) -> p k f", f=3 * D)
    g_b = A.f32(D)
    load_weight_bf16(cx, W3, w_qkv, 8, pfx + "W")
    load_bcast_row(cx, g_b, g_pre, pfx + "g")
    TT = 512
    h3b = [A.f32(4 * D).rearrange("p (s d) -> p s d", d=D) for _ in range(2)]
    xn = A.bf16(4 * D).rearrange("p (s d) -> p s d", d=D)
    xnTb = [A.bf16(8 * TT).rearrange("p (k t) -> p k t", t=TT) for _ in range(2)]
    qt = [A.bf16(8 * TT).rearrange("p (k t) -> p k t", t=TT) for _ in range(2)]
    sq = [A.bf16(TT) for _ in range(2)]
    vb = [A.bf16(D) for _ in range(2)]
    vf = [A.f32(D) for _ in range(2)]
    junk = A.bf16(D)
    ssq = A.f32(4)
    rstd = A.f32(4)
    m1 = A.f32(2)
    SEL = A.bf16(8 * 16).rearrange("p (k h) -> p k h", h=16)
    S.op("pool", lambda e: e.memset(SEL, 0.0), writes=[pfx + "SEL"])
    for kc in range(8):
        for hh in range(2):
            S.op("pool", (lambda e, kc=kc, hh=hh: e.memset(SEL[64 * hh:64 * hh + 64, kc, 2 * kc + hh:2 * kc + hh + 1], 1.0)),
                 reads=[pfx + "SEL"], writes=[pfx + "SEL"])
    S.op("pool", lambda e: e.memset(cx.qkmax, 0.0), writes=["qkmax"])
    QT_v = QT.rearrange("(k p) t -> p k t", p=128)
    KT_v = KT.rearrange("(k p) t -> p k t", p=128)
    cnt = [0, 0]

    def st_load(ti, row0, nsub, k_out, v_out, nvalid):
        ncols = nsub * 128
        htok = pfx + f"h{ti % 2}"
        h3 = h3b[ti % 2]
        S.op("sp", lambda e: e.dma_start(out=h3[:, 0:nsub, :],
                                         in_=hsrc[row0:row0 + ncols, :].rearrange("(s p) d -> p s d", p=128)),
             reads=[f"{hname}@{row0 + 128 * s_}" for s_ in range(nsub)], writes=[htok], dma_tok=htok)

    def st_prenorm(ti, row0, nsub, k_out, v_out, nvalid):
        htok = pfx + f"h{ti % 2}"
        names = dict(junk=pfx + "junk", ssq=pfx + "ssq", rstd=pfx + "rstd", xn=pfx + "xn", xnT=pfx + f"xnT{ti % 2}")
        rms_prenorm_T(cx, h3b[ti % 2], nsub, g_b, pfx + "g", ssq, rstd, junk, xn, xnTb[ti % 2], [htok], names, [0, 1])

    def st_qk(ti, row0, nsub, k_out, v_out, nvalid):
        ncols = nsub * 128
        xnT = xnTb[ti % 2]
        xtok = pfx + f"xnT{ti % 2}"
        for which, dst_v, dname in ((0, QT_v, "QT"), (1, KT_v, "KT")):
            qb = qt[cnt[0] % 2]
            qtok = f"{pfx}qt{cnt[0] % 2}"
            cnt[0] += 1
            for oc in range(8):
                bank = 2 + (oc % 2)
                for kc in range(8):
                    S.op("pe", (lambda e, oc=oc, kc=kc, bank=bank, which=which: e.matmul(
                        cx.ps[bank][:, 0:ncols], lhsT=W3[:, kc, which * D + oc * 128:which * D + (oc + 1) * 128],
                        rhs=xnT[:, kc, 0:ncols], start=(kc == 0), stop=(kc == 7))),
                        reads=[xtok, pfx + "W"], writes=[f"ps{bank}"])
                S.op("act", (lambda e, oc=oc, bank=bank, qb=qb: e.activation(out=qb[:, oc, 0:ncols],
                                                                             in_=cx.ps[bank][:, 0:ncols], func=AF.Copy)),
                     reads=[f"ps{bank}"], writes=[qtok])
                if K_STAGE <= 6:
                    continue
                sb = sq[oc % 2]
                S.op("act", (lambda e, oc=oc, bank=bank, sb=sb: e.activation(out=sb[:, 0:ncols],
                                                                             in_=cx.ps[bank][:, 0:ncols], func=AF.Square)),
                     reads=[f"ps{bank}"], writes=[f"{pfx}sq{oc % 2}"])
                S.op("pe", (lambda e, oc=oc, sb=sb: e.matmul(cx.ps[4][0:16, 0:ncols], lhsT=SEL[:, oc, :], rhs=sb[:, 0:ncols],
                                                             start=(oc == 0), stop=(oc == 7))),
                     reads=[f"{pfx}sq{oc % 2}", pfx + "SEL"], writes=["ps4"])
            if K_STAGE > 6.5:
              S.op("dve", (lambda e: e.reduce_max(out=m1[0:16, 0:1], in_=cx.ps[4][0:16, 0:ncols], axis=AX.X)),
                 reads=["ps4"], writes=[pfx + "m1"])
            if K_STAGE > 6.7:
              S.op("dve", (lambda e, which=which: e.tensor_tensor(out=cx.qkmax[0:16, which:which + 1],
                                                                in0=cx.qkmax[0:16, which:which + 1], in1=m1[0:16, 0:1],
                                                                op=ALU.max)),
                 reads=[pfx + "m1", "qkmax"], writes=["qkmax"])
            S.op("pool", (lambda e, qb=qb, dst_v=dst_v: e.dma_start(out=dst_v[:, :, row0:row0 + ncols], in_=qb[:, :, 0:ncols])),
                 reads=[qtok], writes=[f"{dname}@{row0}"], dma_tok=qtok)
    def st_v(ti, row0, nsub, k_out, v_out, nvalid):
        ncols = nsub * 128
        xnT = xnTb[ti % 2]
        xtok = pfx + f"xnT{ti % 2}"
        for which, out_ap in ((2, v_out), (1, k_out)):
            if which == 1 and k_out is None:
                continue
            if K_STAGE <= 7:
                out_ap = None
                if which == 1:
                    continue
            for s in range(nsub):
                i = cnt[1] % 2
                cnt[1] += 1
                vbb, vff = vb[i], vf[i]
                vtok, ftok = f"{pfx}vb{i}", f"{pfx}vf{i}"
                for hf in range(2):
                    bank = 5 + hf
                    for kc in range(8):
                        S.op("pe", (lambda e, s=s, hf=hf, kc=kc, bank=bank, which=which: e.matmul(
                            cx.ps[bank][:, :], lhsT=xnT[:, kc, s * 128:(s + 1) * 128],
                            rhs=W3[:, kc, which * D + hf * 512:which * D + (hf + 1) * 512],
                            start=(kc == 0), stop=(kc == 7))),
                            reads=[xtok, pfx + "W"], writes=[f"ps{bank}"])
                    if which == 2:
                        S.op("act", (lambda e, hf=hf, bank=bank, vbb=vbb: e.activation(
                            out=vbb[:, hf * 512:(hf + 1) * 512], in_=cx.ps[bank][:, :], func=AF.Copy)),
                            reads=[f"ps{bank}"], writes=[vtok])
                    if out_ap is not None:
                        S.op("dve", (lambda e, hf=hf, bank=bank, vff=vff: e.tensor_copy(
                            out=vff[:, hf * 512:(hf + 1) * 512], in_=cx.ps[bank][:, :])),
                            reads=[f"ps{bank}"], writes=[ftok])
                if which == 2:
                    S.op("pool", (lambda e, s=s, vbb=vbb: e.dma_start(out=V[row0 + s * 128:row0 + (s + 1) * 128, :], in_=vbb)),
                         reads=[vtok], writes=[f"V@{row0 + s * 128}"], dma_tok=vtok)
                if out_ap is not None:
                    nv = min(128, nvalid - s * 128)
                    if nv > 0:
                        S.op("pool", (lambda e, s=s, vff=vff, nv=nv, out_ap=out_ap: e.dma_start(
                            out=out_ap[s * 128:s * 128 + nv, :], in_=vff[0:nv, :])),
                            reads=[ftok], dma_tok=ftok)

    st_load(0, *tiles[0])
    st_prenorm(0, *tiles[0])
    for ti, t in enumerate(tiles):
        if ti + 1 < len(tiles):
            st_load(ti + 1, *tiles[ti + 1])
        st_qk(ti, *t)
        if ti + 1 < len(tiles):
            st_prenorm(ti + 1, *tiles[ti + 1])
        st_v(ti, *t)


DILS = tuple(int(x) for x in os.environ.get("K_DILS", "1,4,16").split(","))


def attn_setup(cx, pfx="a1"):
    S, A = cx.S, cx.A
    mask = A.bf16(512).rearrange("p (h k q) -> p h k q", h=2, k=2)
    cneg = A.f32(8)
    cb = A.f32(16)
    c16 = A.f32(2)
    dg = A.bf16(16)
    on16 = A.bf16(128)
    S.op("pool", lambda e: e.memset(mask, 1.0), writes=[pfx + "mask"])
    for hh in range(2):
        S.op("pool", (lambda e, hh=hh: e.affine_select(out=mask[:, hh, 0, :], in_=mask[:, hh, 0, :], pattern=[[-1, 128]],
                                                       compare_op=ALU.is_ge, fill=0.0, base=0, channel_multiplier=1)),
             reads=[pfx + "mask"], writes=[pfx + "mask"])
        S.op("pool", (lambda e, hh=hh: e.affine_select(out=mask[:, hh, 1, :], in_=mask[:, hh, 1, :], pattern=[[1, 128]],
                                                       compare_op=ALU.is_ge, fill=0.0, base=0, channel_multiplier=-1)),
             reads=[pfx + "mask"], writes=[pfx + "mask"])
    S.op("dve", lambda e: e.tensor_tensor(out=c16[0:16, 0:1], in0=cx.qkmax[0:16, 0:1], in1=cx.qkmax[0:16, 1:2], op=ALU.mult),
         reads=["qkmax"], writes=[pfx + "c16"])
    S.op("act", lambda e: e.activation(out=c16[0:16, 0:1], in_=c16[0:16, 0:1], func=AF.Sqrt, scale=(1.02 / 8) ** 2),
         reads=[pfx + "c16"], writes=[pfx + "c16"])
    S.op("dve", lambda e: e.tensor_scalar(out=dg[0:16, :], in0=cx.ident[0:16, 0:16], scalar1=c16[0:16, 0:1], scalar2=None,
                                          op0=ALU.mult), reads=[pfx + "c16", "ident"], writes=[pfx + "dg"])
    S.op("pool", lambda e: e.memset(on16, 1.0), writes=[pfx + "on16"])
    S.op("pe", lambda e: e.matmul(cx.ps[7][:, 0:16], lhsT=on16[0:16, :], rhs=dg[0:16, :], start=True, stop=True),
         reads=[pfx + "on16", pfx + "dg"], writes=["ps7"])
    S.op("dve", lambda e: e.tensor_copy(out=cb, in_=cx.ps[7][:, 0:16]), reads=["ps7"], writes=[pfx + "cb"])
    cb3 = cb.rearrange("p (k h) -> p k h", h=2)
    S.op("dve", lambda e: e.tensor_tensor(out=cneg, in0=cb3[:, :, 0], in1=cb3[:, :, 1], op=ALU.max),
         reads=[pfx + "cb"], writes=[pfx + "cneg"])
    S.op("dve", lambda e: e.tensor_scalar(out=cneg, in0=cneg, scalar1=-1.0, scalar2=None, op0=ALU.mult),
         reads=[pfx + "cneg"], writes=[pfx + "cneg"])
    return dict(mask=mask, cneg=cneg, pfx=pfx)


def attn_phase(cx, T, QT, KT, V, attnT, pfx="a1", st_list=None, halo_end=0, hflag=None):
    S, A = cx.S, cx.A
    S.barrier()
    A.reset()
    C = attn_setup(cx, pfx)
    mask, cneg = C["mask"], C["cneg"]
    ST = 2048
    NST = T // ST
    if st_list is None:
        st_list = list(range(NST))
    fcol = A.f32(1)
    if hflag is not None:
        S.op("sp", lambda e: e.dma_start(out=fcol, in_=hflag.partition_broadcast(128)), writes=[pfx + "fcol"], dma_tok=pfx + "fcol")
    qtb = [A.bf16(ST + 16) for _ in range(2)]
    ktw = [A.bf16(2 * ST + 16) for _ in range(2)]
    acc = [A.f32(2 * (ST + 16)).rearrange("p (n t) -> p n t", n=2) for _ in range(2)]
    rec = A.f32(ST)
    ob = [A.bf16(ST) for _ in range(2)]
    NV = 10
    vbuf = [A.bf16(128) for _ in range(NV)]
    NP = 4
    pT = [A.bf16(512).rearrange("p (h k q) -> p h k q", h=2, k=2) for _ in range(NP)]
    cnt = dict(v=0, p=0, s=0, n=0, par=0, m=0)

    def unit(kc, par, T0, d, r, beta, nb_first, qtile, kwin, accb, vmap, first_d):
        vmap = dict(vmap)
        if K_STAGE <= 12:
            return
        kblocks = [b for b in (beta - 1, beta) if b >= 0]
        qoff = r + d * 128 * beta - T0
        qv = qtile[:, qoff:qoff + 128 * d].rearrange("p (m s) -> p m s", s=d)[:, :, 0]
        pi = cnt["p"] % NP
        cnt["p"] += 1
        pt = pT[pi]
        ptok = f"{pfx}pT{pi}"
        sb0 = 2 * (cnt["s"] % 2)
        cnt["s"] += 1
        for hh in range(2):
            sbank = sb0 + hh
            for b in kblocks:
                kb = b - (beta - 1)
                koff = r + d * 128 * b - (T0 - ST)
                kv = kwin[:, koff:koff + 128 * d].rearrange("p (m s) -> p m s", s=d)[:, :, 0]
                S.op("pe", (lambda e, hh=hh, kb=kb, kv=kv, qv=qv, sbank=sbank: e.matmul(
                    cx.ps[sbank][:, kb * 128:(kb + 1) * 128],
                    lhsT=kv[64 * hh:64 * hh + 64, :], rhs=qv[64 * hh:64 * hh + 64, :], start=True, stop=True)),
                    reads=[f"{pfx}qt{par}", f"{pfx}kt{par}"], writes=[f"ps{sbank}"])
        if K_STAGE <= 12.5:
            return
        for hh in range(2):
            sbank = sb0 + hh
            S.op("act", (lambda e, sbank=sbank, pt=pt, hh=hh: e.activation(
                out=pt[:, hh, :, :].rearrange("p k q -> p (k q)"), in_=cx.ps[sbank][:, 0:256], func=AF.Exp, scale=0.125,
                bias=cneg[:, kc:kc + 1])), reads=[f"ps{sbank}", pfx + "cneg"], writes=[ptok])
        if K_STAGE <= 13:
            return
        halo_kb0 = (hflag is not None) and (beta - 1 >= 0) and (r + d * (128 * (beta - 1) + 127) < halo_end)
        if halo_kb0:
            S.op("dve", (lambda e, pt=pt: e.scalar_tensor_tensor(out=pt[:, :, 0, :], in0=pt[:, :, 0, :], scalar=fcol[:, 0:1],
                                                                 in1=mask[:, :, 0, :], op0=ALU.mult, op1=ALU.mult)),
                 reads=[ptok, pfx + "mask", pfx + "fcol"], writes=[ptok])
            S.op("dve", (lambda e, pt=pt: e.tensor_tensor(out=pt[:, :, 1, :], in0=pt[:, :, 1, :], in1=mask[:, :, 1, :], op=ALU.mult)),
                 reads=[ptok, pfx + "mask"], writes=[ptok])
        else:
            S.op("dve", (lambda e, pt=pt: e.tensor_tensor(out=pt.rearrange("p h k q -> p (h k q)"), in0=pt.rearrange("p h k q -> p (h k q)"),
                                                          in1=mask.rearrange("p h k q -> p (h k q)"), op=ALU.mult)),
                 reads=[ptok, pfx + "mask"], writes=[ptok])
        def stage_b():
            nbank = 4 + cnt["n"] % 3
            cnt["n"] += 1
            for hh in range(2):
                for j, b in enumerate(kblocks):
                    kb = b - (beta - 1)
                    vi = vmap[b]
                    S.op("pe", (lambda e, hh=hh, kb=kb, vi=vi, j=j, nbank=nbank, pt=pt: e.matmul(
                        cx.ps[nbank][64 * hh:64 * hh + 64, 0:128], lhsT=vbuf[vi][:, 64 * hh:64 * hh + 64], rhs=pt[:, hh, kb, :],
                        start=(j == 0), stop=(j == len(kblocks) - 1), tile_position=(0, 64 * hh))),
                        reads=[ptok, f"{pfx}v{vi}"], writes=[f"ps{nbank}"])
                for j, b in enumerate(kblocks):
                    kb = b - (beta - 1)
                    S.op("pe", (lambda e, hh=hh, kb=kb, j=j, nbank=nbank, pt=pt: e.matmul(
                        cx.ps[nbank][64 * hh:64 * hh + 64, 128:256], lhsT=cx.ones_bf[:, 0:64], rhs=pt[:, hh, kb, :],
                        start=(j == 0), stop=(j == len(kblocks) - 1), tile_position=(0, 64 * hh))),
                        reads=[ptok, "ones"], writes=[f"ps{nbank}"])
            av = accb[:, :, qoff:qoff + 128 * d].rearrange("p n (m s) -> p n m s", s=d)[:, :, :, 0]
            nd = cx.ps[nbank][:, 0:256].rearrange("p (n q) -> p n q", n=2)
            if first_d:
                S.op("act", (lambda e, av=av, nd=nd: e.activation(out=av, in_=nd, func=AF.Copy)),
                     reads=[f"ps{nbank}"], writes=[f"{pfx}acc{par}"])
            else:
                S.op("dve", (lambda e, av=av, nd=nd: e.tensor_tensor(out=av, in0=nd, in1=av, op=ALU.add)),
                     reads=[f"ps{nbank}", f"{pfx}acc{par}"], writes=[f"{pfx}acc{par}"])


        return stage_b

    def pair_body(st, kc):
        T0 = st * ST
        par = cnt["par"] % 2
        cnt["par"] += 1
        qtile, kwin, accb, obb = qtb[par], ktw[par], acc[par], ob[par]
        S.op("sp", lambda e: e.dma_start(out=qtile[:, 0:ST], in_=QT[kc * 128:(kc + 1) * 128, T0:T0 + ST]),
             reads=[f"QT@{T0 + 512 * i}" for i in range(4)], writes=[f"{pfx}qt{par}"], dma_tok=f"{pfx}qt{par}")
        k0 = max(0, T0 - ST)
        S.op("sp", lambda e: e.dma_start(out=kwin[:, k0 - (T0 - ST):2 * ST], in_=KT[kc * 128:(kc + 1) * 128, k0:T0 + ST]),
             reads=[f"KT@{k0 + 512 * i}" for i in range((T0 + ST - k0) // 512)], writes=[f"{pfx}kt{par}"],
             dma_tok=f"{pfx}kt{par}")
        pend = []
        SKEW = int(os.environ.get('K_SKEW', '2'))
        for di, d in enumerate(DILS):
            nbq = ST // (128 * d)
            for r in range(d):
                beta0 = T0 // (128 * d)
                vmap = {}
                for b in range(beta0 - 1, beta0 + nbq):
                    if b < 0:
                        continue
                    vi = cnt["v"] % NV
                    cnt["v"] += 1
                    vmap[b] = vi
                    vsrc = V[0:T, :].rearrange("(m s) c -> m s c", s=d)[128 * b:128 * b + 128, r, kc * 128:(kc + 1) * 128]
                    S.op("sp", (lambda e, vi=vi, vsrc=vsrc: e.dma_start(out=vbuf[vi], in_=vsrc)),
                         reads=[f"V@{128 * i}" for i in range((r + d * 128 * b) // 128, (r + d * (128 * b + 127)) // 128 + 1)],
                         writes=[f"{pfx}v{vi}"], dma_tok=f"{pfx}v{vi}")
                    if b >= beta0:
                        pend.append(unit(kc, par, T0, d, r, b, beta0, qtile, kwin, accb, vmap, di == 0))
                        if len(pend) > SKEW:
                            pend.pop(0)()
        while pend:
            pend.pop(0)()
        S.op("dve", lambda e: e.reciprocal(out=rec, in_=accb[:, 1, 0:ST]), reads=[f"{pfx}acc{par}"], writes=[pfx + "rec"])
        S.op("dve", lambda e: e.tensor_tensor(out=obb, in0=accb[:, 0, 0:ST], in1=rec, op=ALU.mult),
             reads=[f"{pfx}acc{par}", pfx + "rec"], writes=[f"{pfx}ob{par}"])
        S.op("pool", lambda e: e.dma_start(out=attnT[kc * 128:(kc + 1) * 128, T0:T0 + ST], in_=obb),
             reads=[f"{pfx}ob{par}"], writes=[f"attnT@{T0}k{kc}"], dma_tok=f"{pfx}ob{par}")

    for st in st_list:
        for kc in range(8):
            pair_body(st, kc)


def attn_sample_phase(cx, T, QT, KT, V, ck, cv, attnT, pfx="as"):
    S, A = cx.S, cx.A
    S.barrier()
    A.reset()
    NBS = 4
    tk = lambda n: pfx + n
    QTs = A.bf16(8 * 16).rearrange("p (k t) -> p k t", t=16)
    KTs = A.bf16(8 * 16).rearrange("p (k t) -> p k t", t=16)
    Vn = A.bf16(D)
    ktok = [A.bf16(D) for _ in range(3)]
    KTt = [A.bf16(8 * 128).rearrange("p (k t) -> p k t", t=128) for _ in range(9)]
    Vt = [A.bf16(D) for _ in range(9)]
    sq = A.bf16(D)
    SEL = A.bf16(8 * 16).rearrange("p (k h) -> p k h", h=16)
    on16 = A.bf16(128)
    pTs = A.bf16(2 * 16).rearrange("p (h c) -> p h c", h=2)
    mT1 = A.bf16(2 * 4).rearrange("p (h c) -> p h c", h=2)
    mnew = A.bf16(2 * 4).rearrange("p (h c) -> p h c", h=2)
    attn_s = A.bf16(8 * 128).rearrange("p (k t) -> p k t", t=128)
    m1 = A.f32(2)
    qk = A.f32(2)
    c16 = A.f32(2)
    dg = A.bf16(16)
    cb = A.f32(16)
    cneg = A.f32(8)
    rs = A.f32(4)
    QT_v = QT.rearrange("(k p) t -> p k t", p=128)
    KT_v = KT.rearrange("(k p) t -> p k t", p=128)
    attnT_v = attnT.rearrange("(k p) t -> p k t", p=128)
    S.op("pool", lambda e: e.memset(SEL, 0.0), writes=[tk("SEL")])
    for kc in range(8):
        for hh in range(2):
            S.op("pool", (lambda e, kc=kc, hh=hh: e.memset(SEL[64 * hh:64 * hh + 64, kc, 2 * kc + hh:2 * kc + hh + 1], 1.0)),
                 reads=[tk("SEL")], writes=[tk("SEL")])
    S.op("pool", lambda e: e.memset(on16, 1.0), writes=[tk("on16")])
    S.op("pool", lambda e: e.memset(attn_s, 0.0), writes=[tk("attn_s")])
    S.op("pool", lambda e: e.memset(mT1, 1.0), writes=[tk("mT1")])
    S.op("pool", lambda e: e.memset(mnew, 1.0), writes=[tk("mnew")])
    for hh in range(2):
        S.op("pool", (lambda e, hh=hh: e.affine_select(out=mT1[:, hh, :], in_=mT1[:, hh, :], pattern=[[-1, 4]],
                                                       compare_op=ALU.is_ge, fill=0.0, base=0, channel_multiplier=1)),
             reads=[tk("mT1")], writes=[tk("mT1")])
        S.op("pool", (lambda e, hh=hh: e.affine_select(out=mnew[0:4, hh, :], in_=mnew[0:4, hh, :], pattern=[[1, 4]],
                                                       compare_op=ALU.is_ge, fill=0.0, base=0, channel_multiplier=-1)),
             reads=[tk("mnew")], writes=[tk("mnew")])
        S.op("dve", (lambda e, hh=hh: e.scalar_tensor_tensor(out=mnew[0:4, hh, :], in0=cx.ident[0:4, 0:4], scalar=2.0,
                                                             in1=mnew[0:4, hh, :], op0=ALU.mult, op1=ALU.add)),
             reads=[tk("mnew"), "ident"], writes=[tk("mnew")])
    S.op("sp", lambda e: e.dma_start(out=QTs, in_=QT_v[:, :, T:T + 16]), writes=[tk("QTs")], dma_tok=tk("QTs"))
    S.op("sp", lambda e: e.dma_start(out=KTs, in_=KT_v[:, :, T:T + 16]), writes=[tk("KTs")], dma_tok=tk("KTs"))

    def norms(src3, ncols, dst_col, first, stok):
        sq3 = sq[:, 0:8 * ncols].rearrange("p (k t) -> p k t", t=ncols)
        S.op("act", lambda e: e.activation(out=sq3, in_=src3, func=AF.Square), reads=[stok, tk("sq")], writes=[tk("sq")])
        for kc in range(8):
            S.op("pe", (lambda e, kc=kc: e.matmul(cx.ps[2][0:16, 0:ncols], lhsT=SEL[:, kc, :], rhs=sq3[:, kc, :],
                                                  start=(kc == 0), stop=(kc == 7))),
                 reads=[tk("sq"), tk("SEL")], writes=["ps2"])
        if first:
            S.op("dve", lambda e: e.reduce_max(out=qk[0:16, dst_col:dst_col + 1], in_=cx.ps[2][0:16, 0:ncols], axis=AX.X),
                 reads=["ps2", tk("qk")], writes=[tk("qk")])
        else:
            S.op("dve", lambda e: e.reduce_max(out=m1[0:16, 0:1], in_=cx.ps[2][0:16, 0:ncols], axis=AX.X),
                 reads=["ps2"], writes=[tk("m1")])
            S.op("dve", lambda e: e.tensor_tensor(out=qk[0:16, dst_col:dst_col + 1], in0=qk[0:16, dst_col:dst_col + 1],
                                                  in1=m1[0:16, 0:1], op=ALU.max), reads=[tk("m1"), tk("qk")], writes=[tk("qk")])

    def batch_body(b):
        cs = slice(4 * b, 4 * b + 4)
        ckb, cvb = ck[b], cv[b]
        srcs = [(ckb[1920:2048, :], cvb[1920:2048, :])]
        for t in range(4):
            srcs.append((ckb.rearrange("(m s) c -> m s c", s=4)[384:512, t, :], cvb.rearrange("(m s) c -> m s c", s=4)[384:512, t, :]))
        for t in range(4):
            srcs.append((ckb.rearrange("(m s) c -> m s c", s=16)[0:128, t, :], cvb.rearrange("(m s) c -> m s c", s=16)[0:128, t, :]))
        S.op("sp", lambda e: e.dma_start(out=Vn[0:4, :], in_=V[T + 4 * b:T + 4 * b + 4, :]), reads=[tk("Vn")], writes=[tk("Vn")],
             dma_tok=tk("Vn"))
        norms(QTs[:, :, cs], 4, 0, True, tk("QTs"))
        norms(KTs[:, :, cs], 4, 1, True, tk("KTs"))
        for i, (ks, vs) in enumerate(srcs):
            kt_ = ktok[i % 3]
            ktk = tk(f"ktok{i % 3}")
            S.op("pool", (lambda e, kt_=kt_, ks=ks: e.dma_start(out=kt_, in_=ks)), writes=[ktk], dma_tok=ktk)
            S.op("pool", (lambda e, i=i, vs=vs: e.dma_start(out=Vt[i], in_=vs)), writes=[tk(f"Vt{i}")], dma_tok=tk(f"Vt{i}"))
            bank = i % 2
            pt = cx.ps[bank][:, :].bitcast(BF16).rearrange("p (k c) -> p k c", c=128)
            for kc in range(8):
                S.op("pe", (lambda e, pt=pt, kt_=kt_, kc=kc: e.transpose(pt[:, kc, :], kt_[:, kc * 128:(kc + 1) * 128], cx.ident)),
                     reads=[ktk, "ident"], writes=[f"ps{bank}"])
            S.op("act", (lambda e, pt=pt, i=i: e.activation(out=KTt[i], in_=pt, func=AF.Copy)),
                 reads=[f"ps{bank}"], writes=[tk(f"KTt{i}")])
            norms(KTt[i], 128, 1, False, tk(f"KTt{i}"))
        S.op("dve", lambda e: e.tensor_tensor(out=c16[0:16, 0:1], in0=qk[0:16, 0:1], in1=qk[0:16, 1:2], op=ALU.mult),
             reads=[tk("qk")], writes=[tk("c16")])
        S.op("act", lambda e: e.activation(out=c16[0:16, 0:1], in_=c16[0:16, 0:1], func=AF.Sqrt, scale=(1.02 / 8) ** 2),
             reads=[tk("c16")], writes=[tk("c16")])
        S.op("dve", lambda e: e.tensor_scalar(out=dg[0:16, :], in0=cx.ident[0:16, 0:16], scalar1=c16[0:16, 0:1], scalar2=None,
                                              op0=ALU.mult), reads=[tk("c16"), "ident"], writes=[tk("dg")])
        S.op("pe", lambda e: e.matmul(cx.ps[3][:, 0:16], lhsT=on16[0:16, :], rhs=dg[0:16, :], start=True, stop=True),
             reads=[tk("on16"), tk("dg")], writes=["ps3"])
        S.op("dve", lambda e: e.tensor_copy(out=cb, in_=cx.ps[3][:, 0:16]), reads=["ps3"], writes=[tk("cb")])
        cb3 = cb.rearrange("p (k h) -> p k h", h=2)
        S.op("dve", lambda e: e.tensor_tensor(out=cneg, in0=cb3[:, :, 0], in1=cb3[:, :, 1], op=ALU.max),
             reads=[tk("cb")], writes=[tk("cneg")])
        S.op("dve", lambda e: e.tensor_scalar(out=cneg, in0=cneg, scalar1=-1.0, scalar2=None, op0=ALU.mult),
             reads=[tk("cneg")], writes=[tk("cneg")])
        ktoks = [tk(f"KTt{i}") for i in range(9)]
        vtoks = [tk(f"Vt{i}") for i in range(9)]

        def pair(kc):
            for hh in range(2):
                bank = 4 + hh
                hs = slice(64 * hh, 64 * hh + 64)
                S.op("pe", (lambda e, hs=hs, bank=bank: e.matmul(cx.ps[bank][:, 0:4], lhsT=KTt[0][hs, kc, :], rhs=QTs[hs, kc, cs],
                                                                 start=True, stop=True)),
                     reads=ktoks + [tk("QTs")], writes=[f"ps{bank}"])
                for t in range(4):
                    for base, ti in ((4, 1 + t), (8, 5 + t)):
                        S.op("pe", (lambda e, hs=hs, bank=bank, t=t, base=base, ti=ti: e.matmul(
                            cx.ps[bank][:, base + t:base + t + 1], lhsT=KTt[ti][hs, kc, :],
                            rhs=QTs[hs, kc, 4 * b + t:4 * b + t + 1], start=True, stop=True)),
                            reads=ktoks + [tk("QTs")], writes=[f"ps{bank}"])
                S.op("pe", (lambda e, hs=hs, bank=bank: e.matmul(cx.ps[bank][0:4, 12:16], lhsT=KTs[hs, kc, cs], rhs=QTs[hs, kc, cs],
                                                                 start=True, stop=True)),
                     reads=[tk("KTs"), tk("QTs")], writes=[f"ps{bank}"])
            for hh in range(2):
                bank = 4 + hh
                S.op("act", (lambda e, hh=hh, bank=bank: e.activation(out=pTs[:, hh, :], in_=cx.ps[bank][:, 0:16], func=AF.Exp,
                                                                      scale=0.125, bias=cneg[:, kc:kc + 1])),
                     reads=[f"ps{bank}", tk("cneg")], writes=[tk("pTs")])
            S.op("dve", lambda e: e.tensor_tensor(out=pTs[:, :, 0:4], in0=pTs[:, :, 0:4], in1=mT1, op=ALU.mult),
                 reads=[tk("pTs"), tk("mT1")], writes=[tk("pTs")])
            S.op("dve", lambda e: e.tensor_tensor(out=pTs[0:4, :, 12:16], in0=pTs[0:4, :, 12:16], in1=mnew[0:4, :, :], op=ALU.mult),
                 reads=[tk("pTs"), tk("mnew")], writes=[tk("pTs")])
            for hh in range(2):
                hs = slice(64 * hh, 64 * hh + 64)
                vc = slice(kc * 128 + 64 * hh, kc * 128 + 64 * hh + 64)
                for which in range(2):
                    oc0 = 4 * which
                    lw = (lambda ti, vc=vc: Vt[ti][:, vc]) if which == 0 else (lambda ti: cx.ones_bf[:, 0:64])
                    ln = Vn[0:4, vc] if which == 0 else cx.ones_bf[0:4, 0:64]
                    S.op("pe", (lambda e, hs=hs, hh=hh, oc0=oc0, lw=lw: e.matmul(
                        cx.ps[6][hs, oc0:oc0 + 4], lhsT=lw(0), rhs=pTs[:, hh, 0:4], start=True, stop=False,
                        tile_position=(0, 64 * hh), skip_group_check=True)),
                        reads=vtoks + [tk("pTs"), "ones"], writes=["ps6"])
                    for t in range(4):
                        for base, ti in ((4, 1 + t), (8, 5 + t)):
                            S.op("pe", (lambda e, hs=hs, hh=hh, oc0=oc0, lw=lw, t=t, base=base, ti=ti: e.matmul(
                                cx.ps[6][hs, oc0 + t:oc0 + t + 1], lhsT=lw(ti), rhs=pTs[:, hh, base + t:base + t + 1],
                                start=False, stop=False, tile_position=(0, 64 * hh), skip_group_check=True)),
                                reads=vtoks + [tk("pTs"), "ones"], writes=["ps6"])
                    S.op("pe", (lambda e, hs=hs, hh=hh, oc0=oc0, ln=ln: e.matmul(
                        cx.ps[6][hs, oc0:oc0 + 4], lhsT=ln, rhs=pTs[0:4, hh, 12:16], start=False, stop=True,
                        tile_position=(0, 64 * hh), skip_group_check=True)),
                        reads=[tk("Vn"), tk("pTs"), "ones"], writes=["ps6"])
            S.op("dve", lambda e: e.reciprocal(out=rs, in_=cx.ps[6][:, 4:8]), reads=["ps6"], writes=[tk("rs")])
            S.op("dve", lambda e: e.tensor_tensor(out=attn_s[:, kc, cs], in0=cx.ps[6][:, 0:4], in1=rs, op=ALU.mult),
                 reads=["ps6", tk("rs"), tk("attn_s")], writes=[tk("attn_s")])

        for kc in range(8):
            pair(kc)

    for b in range(NBS):
        batch_body(b)
    S.op("sp", lambda e: e.dma_start(out=attnT_v[:, :, T:T + 128], in_=attn_s), reads=[tk("attn_s")], dma_tok=tk("attn_s"))


T_PROMPT = int(os.environ.get("K_T", "8192"))
NCORES = 8
WEIGHT_SHAPES = {
    "norm_gains": [2, 4, 1024], "ab_w_in": [1, 1024, 1024], "ab_pool_w": [1, 4, 128, 128], "ab_pool_scale": [1, 512],
    "ab_lambda_re": [1, 32, 64], "ab_lambda_im": [1, 32, 64], "ab_log_dt": [1, 32], "ab_b_re": [1, 32, 64, 16],
    "ab_b_im": [1, 32, 64, 16], "ab_c_re": [1, 32, 16, 64], "ab_c_im": [1, 32, 16, 64], "ab_d": [1, 512],
    "ab_w_glu": [1, 512, 512], "ab_b_glu": [1, 512], "ab_w_out": [1, 1024, 1024], "c_w_qkv": [1, 1024, 3072],
    "c_w_o": [1, 1024, 1024], "ffn_w_gate": [2, 1024, FH], "ffn_w_up": [2, 1024, FH], "ffn_w_down": [2, FH, 1024],
}


def build_program(T=T_PROMPT, upto=99):
    nc = bass.Bass("TRN2", target_bir_lowering=False)
    TA = T + 128
    OWN0 = T // 2
    HALO0 = T // 4
    NOWN = T - OWN0
    din = lambda n, s: nc.dram_tensor(n, s, F32, kind="ExternalInput").ap()
    dout = lambda n, s: nc.dram_tensor(n, s, F32, kind="ExternalOutput").ap()
    x = din("x", [T, 1024])
    xs = din("xs", [128, 1024])
    invc_in = din("invc", [64])
    hflag = din("hflag", [1])
    sp_in = din("state_pool_s", [4, 15, 512])
    s5_in = din("state_s5_s", [4, 32, 64, 2])
    ck = din("cache_k_s", [4, 2048, 1024])
    cv = din("cache_v_s", [4, 2048, 1024])
    Wd = {n: din(n, s) for n, s in WEIGHT_SHAPES.items()}
    y_p = dout("y_p", [NOWN, 1024])
    y_s = dout("y_s", [128, 1024])
    pool_p = dout("pool_p", [15, 512])
    s5_p = dout("s5_p", [32, 64, 2])
    KW = 2048
    k_p = dout("k_p", [KW, 1024])
    v_p = dout("v_p", [KW, 1024])
    pool_s = dout("pool_s", [4, 15, 512])
    s5_s = dout("s5_s", [4, 32, 64, 2])
    k_s = dout("k_s", [16, 1024])
    v_s = dout("v_s", [16, 1024])
    scr = lambda n, s, dt: nc.dram_tensor(n, s, dt, kind="Internal").ap()
    H1 = scr("H1", [TA, 1024], F32)
    H2 = scr("H2", [TA, 1024], F32)
    H3 = scr("H3", [TA, 1024], F32)
    ycatT = scr("ycatT", [1024, TA], BF16)
    attnT = scr("attnT", [1024, TA], BF16)
    QT = scr("QT", [1024, TA], BF16)
    KT = scr("KT", [1024, TA], BF16)
    V = scr("V", [TA, 1024], BF16)
    cx = Ctx(nc)
    g = Wd["norm_gains"]
    Pm = dict(lam_re=Wd["ab_lambda_re"][0], lam_im=Wd["ab_lambda_im"][0], log_dt=Wd["ab_log_dt"][0],
              b_re=Wd["ab_b_re"][0], b_im=Wd["ab_b_im"][0], c_re=Wd["ab_c_re"][0], c_im=Wd["ab_c_im"][0],
              w_in=Wd["ab_w_in"][0], w_glu=Wd["ab_w_glu"][0], pool_w=Wd["ab_pool_w"][0], g0=g[0, 0, :],
              d=Wd["ab_d"][0], b_glu=Wd["ab_b_glu"][0], pool_scale=Wd["ab_pool_scale"][0])
    samp = dict(x_s=xs, state_pool=sp_in, state_s5=s5_in, pool_out=pool_s, s5_out=s5_s)
    mixer_ab_phase(cx, x, T, ycatT, Pm, dict(pool_prompt=pool_p, s5_prompt=s5_p), samp,
                   n_light=HALO0 // 512, fix_tile=OWN0 // 512, invc_dram=invc_in)
    if upto >= 2:
        tiles = [(t0, 4, x[t0:t0 + 512, :], H1[t0:t0 + 512, :], "ycatT", "x", "H1", t0) for t0 in range(HALO0, T, 512)]
        tiles.append((T, 1, xs, H1[T:TA, :], "ycatT", "xs", "H1", T))
        proj_res_phase(cx, ycatT, tiles, Wd["ab_w_out"][0], g[0, 1, :], "p0")
    if upto >= 3:
        tiles = [(H1[t0:t0 + 256, :], H2[t0:t0 + 256, :], 2, "H1", "H2", t0) for t0 in range(HALO0, T, 256)]
        tiles.append((H1[T:TA, :], H2[T:TA, :], 1, "H1", "H2", T))
        ffn_phase(cx, tiles, Wd["ffn_w_gate"][0], Wd["ffn_w_up"][0], Wd["ffn_w_down"][0], g[0, 2, :], g[0, 3, :], "f0")
    if upto >= 4:
        tiles = []
        for t0 in range(HALO0, T, 512):
            if t0 >= T - KW:
                o = t0 - (T - KW)
                tiles.append((t0, 4, k_p[o:o + 512, :], v_p[o:o + 512, :], 512))
            else:
                tiles.append((t0, 4, None, None, 0))
        tiles.append((T, 1, k_s, v_s, 16))
        qkv_phase(cx, H2, "H2", tiles, Wd["c_w_qkv"][0], g[1, 0, :], QT, KT, V)
    if upto >= 5:
        attn_phase(cx, T, QT, KT, V, attnT, st_list=list(range(OWN0 // 2048, T // 2048)), halo_end=OWN0, hflag=hflag)
        attn_sample_phase(cx, T, QT, KT, V, ck, cv, attnT)
    if upto >= 6:
        tiles = [(t0, 4, H2[t0:t0 + 512, :], H3[t0:t0 + 512, :], "attnT", "H2", "H3", t0) for t0 in range(OWN0, T, 512)]
        tiles.append((T, 1, H2[T:TA, :], H3[T:TA, :], "attnT", "H2", "H3", T))
        proj_res_phase(cx, attnT, tiles, Wd["c_w_o"][0], g[1, 1, :], "p1")
    if upto >= 7:
        tiles = [(H3[t0:t0 + 256, :], y_p[t0 - OWN0:t0 - OWN0 + 256, :], 2, "H3", "y_p", t0) for t0 in range(OWN0, T, 256)]
        tiles.append((H3[T:TA, :], y_s, 1, "H3", "y_s", T))
        ffn_phase(cx, tiles, Wd["ffn_w_gate"][1], Wd["ffn_w_up"][1], Wd["ffn_w_down"][1], g[1, 2, :], g[1, 3, :], "f1")
    cx.S.finish()
    cx.S.emit()
    return nc


def kernel(**inputs):
    T = T_PROMPT
    OWN0 = T // 2
    NOWN = T - OWN0
    f32 = lambda a: np.ascontiguousarray(np.asarray(a, dtype=np.float32))
    xp = f32(inputs["x_prompt"])
    xsm = f32(inputs["x_sample"])
    spool = f32(inputs["state_pool"])
    ss5 = f32(inputs["state_s5"])
    ckk = np.asarray(inputs["cache_k"], dtype=np.float32)
    cvv = np.asarray(inputs["cache_v"], dtype=np.float32)
    weights = {n: f32(inputs[n]) for n in WEIGHT_SHAPES}
    nb = xp.shape[0]
    SEQ = xp.shape[1]
    assert SEQ == T and nb * 2 == NCORES
    invc_first = np.zeros((4, 16), np.float32)
    invc_plain = np.zeros((4, 16), np.float32)
    for gi, w in enumerate(POOL_W):
        invc_first[gi] = 1.0 / np.minimum(np.arange(16) + 1, w)
        invc_plain[gi] = 1.0 / w
    in_maps = []
    for c in range(NCORES):
        b, half = c // 2, c % 2
        m = dict(weights)
        xl = np.zeros((T, 1024), np.float32)
        if half == 0:
            xl[OWN0:] = xp[b, 0:NOWN]
        else:
            xl[:] = xp[b]
        m["x"] = xl
        m["invc"] = (invc_first if half == 0 else invc_plain).reshape(64).copy()
        m["hflag"] = np.array([float(half)], np.float32)
        xs_pad = np.zeros((128, 1024), np.float32)
        xs_pad[:16] = xsm[4 * c:4 * c + 4].reshape(16, 1024)
        m["xs"] = xs_pad
        m["state_pool_s"] = np.ascontiguousarray(spool[0, 4 * c:4 * c + 4])
        m["state_s5_s"] = np.ascontiguousarray(ss5[0, 4 * c:4 * c + 4])
        m["cache_k_s"] = np.ascontiguousarray(ckk[0, 4 * c:4 * c + 4].reshape(4, 2048, 1024))
        m["cache_v_s"] = np.ascontiguousarray(cvv[0, 4 * c:4 * c + 4].reshape(4, 2048, 1024))
        in_maps.append(m)
    nc = build_program(T)
    res = run_bass_kernel_spmd(nc, in_maps, core_ids=list(range(NCORES)))
    R = res.results
    KW = 2048
    y_prompt = np.stack([np.concatenate([np.asarray(R[2 * b]["y_p"]), np.asarray(R[2 * b + 1]["y_p"])]) for b in range(nb)])
    y_sample = np.concatenate([np.asarray(R[c]["y_s"])[:16].reshape(4, 4, 1024) for c in range(NCORES)])
    last = lambda b: R[2 * b + 1]
    pool_prompt = np.stack([np.asarray(last(b)["pool_p"]) for b in range(nb)])[None]
    s5_prompt = np.stack([np.asarray(last(b)["s5_p"]) for b in range(nb)])[None]
    k_prompt = np.stack([np.asarray(last(b)["k_p"]).reshape(KW, 16, 64) for b in range(nb)])[None]
    v_prompt = np.stack([np.asarray(last(b)["v_p"]).reshape(KW, 16, 64) for b in range(nb)])[None]
    pool_sample = np.concatenate([np.asarray(R[c]["pool_s"]) for c in range(NCORES)])[None]
    s5_sample = np.concatenate([np.asarray(R[c]["s5_s"]) for c in range(NCORES)])[None]
    k_sample = np.concatenate([np.asarray(R[c]["k_s"]).reshape(4, 4, 16, 64) for c in range(NCORES)])[None]
    v_sample = np.concatenate([np.asarray(R[c]["v_s"]).reshape(4, 4, 16, 64) for c in range(NCORES)])[None]
    outs = (y_prompt, y_sample, pool_prompt, s5_prompt, k_prompt, v_prompt, pool_sample, s5_sample, k_sample, v_sample)
    return tuple(np.ascontiguousarray(o, dtype=np.float32) for o in outs)
```
